# Optimizing a Trainium2 kernel written in Bass

```python
import math
import jax, jax.numpy as jnp
from jax import lax
import numpy as np

D_MODEL = 1024
BATCH = 4
SEQ = 4096
DEPTH = 1

CTX_LEN = 256
GRID_W = 64
D_ATTN = 512
N_HEADS_A = 8
HEAD_DIM = D_ATTN // N_HEADS_A
D_CONV = D_MODEL - D_ATTN
CONV_WIDTH = 31
NA_ROWS = 8
NA_COLS = 16
Q_BLOCK_COLS = 16
K_BLOCK_COLS = Q_BLOCK_COLS + NA_COLS
D_FF = 2816
FFN_CONV_WIDTH = 3
EPS = 1e-6
ATTN_SCALE = HEAD_DIM ** -0.5
SPLITS = [D_ATTN, 2 * D_ATTN, 3 * D_ATTN, 3 * D_ATTN + D_CONV]

kernel_name = 'hybrid_natten_conformer_dit_layer'


def rms_norm(x, g):
    xf = x.astype(jnp.float32)
    y = xf * lax.rsqrt(jnp.mean(xf * xf, axis=-1, keepdims=True) + EPS)
    return (y * g.astype(jnp.float32)).astype(x.dtype)


def layer_norm(x, g, b):
    xf = x.astype(jnp.float32)
    mu = jnp.mean(xf, axis=-1, keepdims=True)
    xc = xf - mu
    var = jnp.mean(xc * xc, axis=-1, keepdims=True)
    y = xc * lax.rsqrt(var + EPS) * g.astype(jnp.float32) + b.astype(jnp.float32)
    return y.astype(x.dtype)


def modulate(h, shift, scale):
    return h * (1 + scale) + shift


def depthwise_conv(x, w, b):
    y = lax.conv_general_dilated(x, w[:, None, :], window_strides=(1,), padding='SAME',
                                 dimension_numbers=('NWC', 'WIO', 'NWC'),
                                 feature_group_count=x.shape[-1])
    return y + b


def heads(t):
    return t.reshape(t.shape[0], t.shape[1], N_HEADS_A, HEAD_DIM)


def neighbourhood_tables(rows):
    wr = min(NA_ROWS, rows)
    n_blk = GRID_W // Q_BLOCK_COLS
    r = np.arange(rows)
    key_rows = np.clip(r - wr // 2, 0, rows - wr)[:, None] + np.arange(wr)[None, :]
    blk = np.arange(n_blk)
    key_cols = (np.clip(blk * Q_BLOCK_COLS - NA_COLS // 2, 0, GRID_W - K_BLOCK_COLS)[:, None]
                + np.arange(K_BLOCK_COLS)[None, :])
    q_cols = blk[:, None] * Q_BLOCK_COLS + np.arange(Q_BLOCK_COLS)[None, :]
    col_start = np.clip(q_cols - NA_COLS // 2, 0, GRID_W - NA_COLS)
    kc = key_cols[:, None, :]
    valid = (kc >= col_start[..., None]) & (kc < col_start[..., None] + NA_COLS)
    key_tok = key_rows[:, None, :, None] * GRID_W + key_cols[None, :, None, :]
    row_off = key_rows - r[:, None] + (NA_ROWS - 1)
    col_off = np.clip(kc - q_cols[..., None] + (NA_COLS - 1), 0, 2 * NA_COLS - 2)
    shape5 = (rows, n_blk, Q_BLOCK_COLS, wr, K_BLOCK_COLS)
    n_keys = wr * K_BLOCK_COLS
    shape4 = (rows, n_blk, Q_BLOCK_COLS, n_keys)
    valid = np.broadcast_to(valid[None, :, :, None, :], shape5).reshape(shape4)
    row_idx = np.broadcast_to(row_off[:, None, None, :, None], shape5).reshape(shape4)
    col_idx = np.broadcast_to(col_off[None, :, :, None, :], shape5).reshape(shape4)
    return (jnp.asarray(key_tok.reshape(rows, n_blk, n_keys), jnp.int32),
            jnp.asarray(row_idx, jnp.int32), jnp.asarray(col_idx, jnp.int32), jnp.asarray(valid))


def neighbourhood_bias(rpb, row_idx, col_idx, valid):
    b = rpb[:, row_idx, col_idx].astype(jnp.float32)
    b = jnp.where(valid[None], b, -jnp.inf)
    return jnp.transpose(b, (1, 2, 0, 3, 4))


def neighbourhood_attention(q, k, v, k_ctx, v_ctx, key_tok, bias):
    B, S = q.shape[0], q.shape[1]
    rows, n_blk, n_keys = key_tok.shape
    qb = q.reshape(B, rows, n_blk, Q_BLOCK_COLS, N_HEADS_A, HEAD_DIM)
    kg = k[:, key_tok]
    vg = v[:, key_tok]
    s_loc = jnp.einsum('brnqhd,brnkhd->brnhqk', qb, kg, preferred_element_type=jnp.float32) * ATTN_SCALE + bias
    s_ctx = jnp.einsum('brnqhd,bchd->brnhqc', qb, k_ctx, preferred_element_type=jnp.float32) * ATTN_SCALE
    p = jax.nn.softmax(jnp.concatenate([s_loc, s_ctx], axis=-1), axis=-1).astype(v.dtype)
    out = (jnp.einsum('brnhqk,brnkhd->brnqhd', p[..., :n_keys], vg)
           + jnp.einsum('brnhqc,bchd->brnqhd', p[..., n_keys:], v_ctx))
    return out.reshape(B, S, D_ATTN)


def context_attention(q, k, v):
    s = jnp.einsum('bqhd,bkhd->bhqk', q, k, preferred_element_type=jnp.float32) * ATTN_SCALE
    p = jax.nn.softmax(s, axis=-1).astype(v.dtype)
    out = jnp.einsum('bhqk,bkhd->bqhd', p, v)
    return out.reshape(q.shape[0], q.shape[1], D_ATTN)


def conformer_conv(a, g, conv_w, conv_b, ln_g, ln_b):
    u = a * jax.nn.sigmoid(g)
    u = depthwise_conv(u, conv_w, conv_b)
    u = layer_norm(u, ln_g, ln_b)
    return jax.nn.silu(u)


def conv_ffn(h, w_up, ffn_w, ffn_b, w_down):
    u = depthwise_conv(h @ w_up, ffn_w, ffn_b)
    gate, val = jnp.split(u, 2, axis=-1)
    return (jax.nn.silu(gate) * val) @ w_down


def setup_inputs(seed: int = 0) -> dict:
    key = jax.random.key(seed)
    ks = jax.random.split(key, 20)
    D = D_MODEL
    n_in = 3 * D_ATTN + 2 * D_CONV

    def nrm(k, shape, scale):
        return jax.random.normal(k, shape, jnp.float32) * scale

    return {
        'x': nrm(ks[0], (BATCH, SEQ, D), 1.0),
        'c': nrm(ks[1], (BATCH, D), 1.0),
        'ctx': nrm(ks[2], (BATCH, CTX_LEN, D), 1.0),
        'c_ctx': nrm(ks[3], (D,), 1.0),
        'w_mod': nrm(ks[4], (DEPTH, D, 6 * D), D ** -0.5),
        'b_mod': nrm(ks[5], (DEPTH, 6 * D), 0.02),
        'g_norm1': 1.0 + nrm(ks[6], (DEPTH, D), 0.05),
        'w_in': nrm(ks[7], (DEPTH, D, n_in), D ** -0.5),
        'rpb': nrm(ks[8], (DEPTH, N_HEADS_A, 2 * NA_ROWS - 1, 2 * NA_COLS - 1), 0.5),
        'conv_w': nrm(ks[9], (DEPTH, CONV_WIDTH, D_CONV), CONV_WIDTH ** -0.5),
        'conv_b': nrm(ks[10], (DEPTH, D_CONV), 0.02),
        'ln_g': 1.0 + nrm(ks[11], (DEPTH, D_CONV), 0.05),
        'ln_b': nrm(ks[12], (DEPTH, D_CONV), 0.02),
        'w_out': nrm(ks[13], (DEPTH, D, D), D ** -0.5),
        'g_norm2': 1.0 + nrm(ks[14], (DEPTH, D), 0.05),
        'w_up': nrm(ks[15], (DEPTH, D, 2 * D_FF), D ** -0.5),
        'ffn_conv_w': nrm(ks[16], (DEPTH, FFN_CONV_WIDTH, 2 * D_FF), FFN_CONV_WIDTH ** -0.5),
        'ffn_conv_b': nrm(ks[17], (DEPTH, 2 * D_FF), 0.02),
        'w_down': nrm(ks[18], (DEPTH, D_FF, D), D_FF ** -0.5),
        'g_final': 1.0 + nrm(ks[19], (D,), 0.05),
    }


def reference(x, c, ctx, c_ctx, w_mod, b_mod, g_norm1, w_in, rpb, conv_w, conv_b, ln_g, ln_b,
              w_out, g_norm2, w_up, ffn_conv_w, ffn_conv_b, w_down, g_final):
    S = x.shape[1]
    rows = S // GRID_W
    key_tok, row_idx, col_idx, valid = neighbourhood_tables(rows)
    c_act = jax.nn.silu(c)
    cctx_act = jax.nn.silu(c_ctx)
    for l in range(DEPTH):
        last = l == DEPTH - 1
        mod = (c_act @ w_mod[l] + b_mod[l])[:, None, :]
        sh1, sc1, gt1, sh2, sc2, gt2 = jnp.split(mod, 6, axis=-1)
        mod_c = cctx_act @ w_mod[l] + b_mod[l]
        csh1, csc1, cgt1, csh2, csc2, cgt2 = jnp.split(mod_c, 6, axis=-1)
        bias = neighbourhood_bias(rpb[l], row_idx, col_idx, valid)

        h = modulate(rms_norm(x, g_norm1[l]), sh1, sc1)
        hc = modulate(rms_norm(ctx, g_norm1[l]), csh1, csc1)
        q, k, v, a, g = jnp.split(h @ w_in[l], SPLITS, axis=-1)
        if last:
            k_c, v_c = jnp.split(hc @ w_in[l][:, D_ATTN:3 * D_ATTN], 2, axis=-1)
        else:
            q_c, k_c, v_c, a_c, g_c = jnp.split(hc @ w_in[l], SPLITS, axis=-1)
        k_c, v_c = heads(k_c), heads(v_c)
        y_na = neighbourhood_attention(heads(q), heads(k), heads(v), k_c, v_c, key_tok, bias)
        y_cv = conformer_conv(a, g, conv_w[l], conv_b[l], ln_g[l], ln_b[l])
        x = x + gt1 * (jnp.concatenate([y_na, y_cv], axis=-1) @ w_out[l])

        if not last:
            yc_na = context_attention(heads(q_c), k_c, v_c)
            yc_cv = conformer_conv(a_c, g_c, conv_w[l], conv_b[l], ln_g[l], ln_b[l])
            ctx = ctx + cgt1 * (jnp.concatenate([yc_na, yc_cv], axis=-1) @ w_out[l])
            hc2 = modulate(rms_norm(ctx, g_norm2[l]), csh2, csc2)
            ctx = ctx + cgt2 * conv_ffn(hc2, w_up[l], ffn_conv_w[l], ffn_conv_b[l], w_down[l])

        h2 = modulate(rms_norm(x, g_norm2[l]), sh2, sc2)
        x = x + gt2 * conv_ffn(h2, w_up[l], ffn_conv_w[l], ffn_conv_b[l], w_down[l])
    return rms_norm(x, g_final)
```

```python
import numpy as np
from contextlib import ExitStack
import concourse.bass as bass
import concourse.mybir as mybir
from concourse.bass_utils import run_bass_kernel_spmd

F32 = mybir.dt.float32
BF16 = mybir.dt.bfloat16
AF = mybir.ActivationFunctionType
ALU = mybir.AluOpType

D = 1024
NT = 19
NOWN = 16
NQ = 17
TOK = NT * 128
NQT = NQ * 128
DFF = 2816
NFC = 22
EPS = 1e-6
FF_GROUPS = [4, 4, 4, 4, 3, 3]
FF_BLOCKS = [(0, 410), (410, 410), (820, 410), (1230, 410), (1640, 408)]
NTAB = 13
DBGW = 4096


class Buf:
    __slots__ = ("w", "r")

    def __init__(self):
        self.w = None
        self.r = {}


class Eng:
    def __init__(self, e, sem, pe=False):
        self.e = e
        self.sem = sem
        self.n = 0
        self.known = {}
        self.pe = pe


class DSem:
    def __init__(self, sem):
        self.sem = sem
        self.n = 0


class KB:
    def wait(self, eng, sem, val):
        if val <= 0 or eng.known.get(id(sem), 0) >= val:
            return
        eng.e.wait_ge(sem, val)
        eng.known[id(sem)] = val

    def deps(self, eng, reads, writes):
        need = {}

        def add(tok):
            if tok is None:
                return
            s, v = tok
            k = id(s)
            if k not in need or need[k][1] < v:
                need[k] = (s, v)

        for b in reads:
            add(b.w)
        for b in writes:
            add(b.w)
            for tok in b.r.values():
                add(tok)
        for s, v in need.values():
            if eng.pe and s is eng.sem:
                continue
            self.wait(eng, s, v)

    def op(self, eng, fn, reads=(), writes=(), mark=True):
        self.deps(eng, reads, writes)
        ins = fn()
        if mark:
            eng.n += 1
            ins.then_inc(eng.sem, 1)
            tok = (eng.sem, eng.n)
        else:
            tok = (eng.sem, eng.n + 1)
        for b in reads:
            b.r[id(eng.sem)] = tok
        for b in writes:
            b.w = tok
            b.r = {}
        return ins

    def dma(self, q, ds, out, in_, reads=(), writes=(), accum_op=None):
        self.deps(q, reads, writes)
        if accum_op is None:
            ins = q.e.dma_start(out=out, in_=in_)
        else:
            ins = q.e.dma_start(out=out, in_=in_, accum_op=accum_op)
        ds.n += 16
        ins.then_inc(ds.sem, 16)
        tok = (ds.sem, ds.n)
        for b in reads:
            b.r[id(ds.sem)] = tok
        for b in writes:
            b.w = tok
            b.r = {}
        return ins

    def barrier(self, engs, dsems):
        for a in engs:
            for b in engs:
                if a is not b:
                    self.wait(a, b.sem, b.n)
            for d in dsems:
                self.wait(a, d.sem, d.n)


def build(stop_after=99, dbg=False):
    nc = bass.Bass("TRN2", target_bir_lowering=False)

    def din(name, shape):
        return nc.dram_tensor(name, list(shape), F32, kind="ExternalInput").ap()

    x_d = din("x", [TOK, D])
    c_d = din("cT", [128, 8, 2])
    ctx_d = din("ctx", [256, D])
    wmod_d = din("w_mod", [D, 6 * D])
    bmodT_d = din("bmodT", [128, 48])
    bmodR_d = din("bmodR", [1, 6 * D])
    g1T_d = din("g1T", [128, 8])
    g2T_d = din("g2T", [128, 8])
    win_d = din("w_in", [D, 2560])
    btab_d = din("btab", [128, 8, NTAB, 128])
    cw_d = din("cwT", [128, 4, 31])
    cb_d = din("cbT", [128, 4])
    lng_d = din("lngT", [128, 4])
    lnb_d = din("lnbT", [128, 4])
    wout_d = din("w_out", [D, D])
    wup_d = din("w_up", [D, 2 * DFF])
    fcw_d = din("fcwT", [128, 44, 3])
    fcb_d = din("fcbT", [128, 44])
    wdn_d = din("w_down", [DFF, D])
    gfin_d = din("g_final", [D])
    ident_d = din("ident", [128, 128])
    out_d = nc.dram_tensor("out", [NOWN * 128, D], F32, kind="ExternalOutput").ap()
    dbg_d = nc.dram_tensor("dbg", [128, DBGW], F32, kind="ExternalOutput").ap() if dbg else None

    kb = KB()
    with ExitStack() as top:
        def sem(name):
            return top.enter_context(nc.semaphore(name))

        PE = Eng(nc.tensor, sem("s_pe"), pe=True)
        ACT = Eng(nc.scalar, sem("s_act"))
        DVE = Eng(nc.vector, sem("s_dve"))
        POOL = Eng(nc.gpsimd, sem("s_pool"))
        SP = Eng(nc.sync, sem("s_sp"))
        ENGS = [PE, ACT, DVE, POOL, SP]
        DSEMS = []

        def dsem(name):
            d = DSem(sem(name))
            DSEMS.append(d)
            return d

        d_c = dsem("d_c")
        d_g = dsem("d_g")
        d_gtr = dsem("d_gtr")
        d_bst = dsem("d_bst")
        d_wm = [dsem(f"d_wm{i}") for i in range(2)]
        d_win = [dsem(f"d_win{i}") for i in range(4)]
        d_wout = dsem("d_wout")
        d_xt = [dsem(f"d_xt{i}") for i in range(2)]
        d_xr = [dsem(f"d_xr{i}") for i in range(3)]
        d_xh = dsem("d_xh")
        d_wg = [dsem(f"d_wg{i}") for i in range(2)]
        d_wv = [dsem(f"d_wv{i}") for i in range(2)]
        d_wd = [dsem(f"d_wd{i}") for i in range(2)]
        d_ot = [dsem(f"d_ot{i}") for i in range(3)]
        d_acc = [dsem(f"d_acc{i}") for i in range(8)]

        class Arena:
            def __init__(self, lo, hi):
                self.lo, self.hi, self.cur = lo, hi, lo

            def reset(self):
                self.cur = self.lo

        uniq = [0]

        def sb(ar, name, shape, dt):
            size = int(np.prod(shape[1:])) * (2 if dt == BF16 else 4)
            off = (ar.cur + 31) // 32 * 32
            assert off + size <= ar.hi, (name, off, size, ar.hi)
            ar.cur = off + size
            uniq[0] += 1
            return nc.alloc_sbuf_tensor_at(f"sb{uniq[0]}_{name}", list(shape), dt, offset=off), Buf()

        SB_LO, SB_HI = 16512, 229344
        C_END = SB_LO + 10752
        Y_END = C_END + 34816
        K_END = Y_END + 78400
        XM_END = Y_END + 65536
        H2_END = XM_END + 32800
        top_ar = Arena(SB_LO, C_END)
        Y_ar = Arena(C_END, Y_END)
        K_ar = Arena(Y_END, K_END)
        B_ar = Arena(K_END, SB_HI)
        XM_ar = Arena(Y_END, XM_END)
        H2_ar = Arena(XM_END, H2_END)
        B2_ar = Arena(H2_END, SB_HI)

        PSBIG = top.enter_context(nc.psum_tensor("psbig", [128, 8 * 512], F32))
        PS = [PSBIG[:, i * 512:(i + 1) * 512] for i in range(8)]
        PSB = [Buf() for _ in range(8)]

        dbg_col = [0]

        def dump(ap, parts, width, reads):
            if not dbg:
                return
            c0 = dbg_col[0]
            assert c0 + width <= DBGW
            kb.dma(POOL, d_g, dbg_d[0:parts, c0:c0 + width], ap, reads=reads)
            dbg_col[0] += width
            return c0

        ident, identB = sb(top_ar, "ident", [128, 128], BF16)
        ones, onesB = sb(top_ar, "ones", [128, 128], BF16)
        A1, A1B = sb(top_ar, "A1", [128, 8, 2], F32)
        SH1, SH1B = sb(top_ar, "SH1", [128, 8, 2], F32)
        A2, A2B = sb(top_ar, "A2", [128, 8], F32)
        SH2, SH2B = sb(top_ar, "SH2", [128, 8], F32)
        gt1b, gt1bB = sb(top_ar, "gt1b", [128, D], F32)
        gt2b, gt2bB = sb(top_ar, "gt2b", [128, D], F32)
        fcw, fcwB = sb(top_ar, "fcw", [128, 44, 3], F32)
        fcb, fcbB = sb(top_ar, "fcb", [128, 44], F32)
        cwT, cwTB = sb(top_ar, "cwT", [128, 4, 31], F32)
        cbT, cbTB = sb(top_ar, "cbT", [128, 4], F32)
        lngT, lngTB = sb(top_ar, "lngT", [128, 4], F32)
        lnbT, lnbTB = sb(top_ar, "lnbT", [128, 4], F32)
        epsT, epsTB = sb(top_ar, "epsT", [128, 1], F32)

        kb.dma(POOL, d_c, ident[:], ident_d, writes=[identB])
        kb.op(DVE, lambda: nc.vector.memset(ones[:], 1.0), writes=[onesB])
        kb.op(DVE, lambda: nc.vector.memset(epsT[:], EPS), writes=[epsTB])
        kb.dma(SP, d_c, fcw[:], fcw_d, writes=[fcwB])
        kb.dma(SP, d_c, fcb[:], fcb_d, writes=[fcbB])
        kb.dma(SP, d_c, cwT[:], cw_d, writes=[cwTB])
        kb.dma(SP, d_c, cbT[:], cb_d, writes=[cbTB])
        kb.dma(SP, d_c, lngT[:], lng_d, writes=[lngTB])
        kb.dma(SP, d_c, lnbT[:], lnb_d, writes=[lnbTB])

        def rstd_from_ssq(eng_unused, ssq_ap, ssqB, out_ap, outB, n, parts=128):
            kb.op(ACT, lambda: nc.scalar.activation(out_ap, ssq_ap, AF.Sqrt, bias=epsT[0:parts, 0:1], scale=1.0 / n),
                  reads=[ssqB, epsTB], writes=[outB])
            kb.op(DVE, lambda: nc.vector.reciprocal(out_ap, out_ap), reads=[outB], writes=[outB])

        kT, kTB = sb(K_ar, "kT", [128, 4, TOK], BF16)
        qT, qTB = sb(K_ar, "qT", [128, 4, NQT], BF16)
        vaug, vaugB = sb(K_ar, "vaug", [128, NT, 8 * 65], BF16)
        UW = 15 + NQT + 1
        UT_ADDR = (K_ar.cur + 31) // 32 * 32
        uT, uTB = sb(K_ar, "uT", [128, 4, UW], BF16)
        kcT, kcTB = sb(K_ar, "kcT", [128, 4, 256], BF16)
        vcaug, vcaugB = sb(K_ar, "vcaug", [128, 2, 8 * 65], BF16)
        kT_t = [Buf() for _ in range(NT)]
        qT_t = [Buf() for _ in range(NQ)]
        v_t = [Buf() for _ in range(NT)]
        uT_t = [Buf() for _ in range(NQ + 1)]

        kb.op(DVE, lambda: nc.vector.memset(vaug[:], 1.0), writes=v_t)
        kb.op(DVE, lambda: nc.vector.memset(vcaug[:], 1.0), writes=[vcaugB])
        kb.op(DVE, lambda: nc.vector.memset(uT[:, :, 0:15], 0.0), writes=[uT_t[0]])

        if True:
            ETOP = 26624 + 3584 + 64
            Bhi_ar = Arena(SB_HI - ETOP, SB_HI)
            B_ar.hi = SB_HI - ETOP
            Etab, EtabB = sb(Bhi_ar, "Etab", [128, 8, NTAB, 128], BF16)
            BST_OFF = (Bhi_ar.cur + 31) // 32 * 32
            bst, bstB = sb(Bhi_ar, "bst", [128, 7, 128], F32)
            etab_items = []
            etab_specs = [(h_, tt0, ntt) for h_ in range(8) for (tt0, ntt) in ((0, 7), (7, 6))]

            def _etab_dma(k):
                h_, tt0, ntt = etab_specs[k]
                kb.dma(POOL, d_bst, bst[:, 0:ntt, :], btab_d[:, h_, tt0:tt0 + ntt, :], writes=[bstB])

            def _etab_exp(k):
                h_, tt0, ntt = etab_specs[k]
                kb.op(ACT, lambda: nc.scalar.activation(Etab[:, h_, tt0:tt0 + ntt, :], bst[:, 0:ntt, :], AF.Exp),
                      reads=[bstB], writes=[EtabB])

            for k_ in range(len(etab_specs) + 1):
                def _item(k_=k_):
                    if k_ >= 1:
                        _etab_exp(k_ - 1)
                    if k_ < len(etab_specs):
                        _etab_dma(k_)
                etab_items.append(_item)
            win, winB = sb(B_ar, "win", [128, 8, 2560], BF16)
            wm = [sb(Y_ar, f"wm{i}", [128, 8, 256], BF16) for i in range(2)]
            cc, ccB = sb(B_ar, "cc", [128, 8, 2], F32)
            ccb, ccbB = sb(B_ar, "ccb", [128, 8, 2], BF16)
            bmT, bmTB = sb(B_ar, "bmT", [128, 48], F32)
            g1T, g1TB = sb(B_ar, "g1T", [128, 8], F32)
            g2T, g2TB = sb(B_ar, "g2T", [128, 8], F32)
            modT, modTB = sb(B_ar, "modT", [128, 48, 2], F32)
            gtr, gtrB = sb(B_ar, "gtr", [1, 256], F32)
            gtrb, gtrbB = sb(B_ar, "gtrb", [1, 256], BF16)
            gtlo, gtloB = sb(B_ar, "gtlo", [1, 256], F32)
            gtlob, gtlobB = sb(B_ar, "gtlob", [1, 256], BF16)
            xt = [sb(Y_ar, f"xt{i}", [128, D], F32) for i in range(2)]
            xn = [sb(B_ar, f"xn{i}", [128, D], BF16) for i in range(2)]
            sqj, sqjB = sb(B_ar, "sqj", [128, D], BF16)
            ssq = [sb(B_ar, f"ssq{i}", [128, 1], F32) for i in range(2)]
            rst = [sb(B_ar, f"rst{i}", [128, 1], F32) for i in range(2)]
            hT = [sb(Y_ar, f"hT{i}", [128, 8, 512], BF16) for i in range(2)]
            sg = [sb(B_ar, f"sg{i}", [128, 512], F32) for i in range(2)]

            kb.dma(SP, d_c, cc[:], c_d, writes=[ccB])
            kb.dma(SP, d_c, bmT[:], bmodT_d, writes=[bmTB])
            kb.dma(SP, d_c, g1T[:], g1T_d, writes=[g1TB])
            kb.dma(SP, d_c, g2T[:], g2T_d, writes=[g2TB])
            for bb in (identB, fcwB, fcbB, cwTB, cbTB, lngTB, lnbTB, ccB, bmTB, g1TB, g2TB):
                bb.w = (d_c.sem, d_c.n)
            wmod_v = wmod_d.rearrange("(kc p) n -> p kc n", p=128)
            win_v = win_d.rearrange("(kc p) n -> p kc n", p=128)

            def load_wm(blk):
                t, tb = wm[blk % 2]
                kb.dma(POOL, d_wm[blk % 2], t[:], wmod_v[:, :, blk * 256:(blk + 1) * 256], writes=[tb])

            load_wm(0)
            load_wm(1)
            kb.op(ACT, lambda: nc.scalar.activation(ccb[:], cc[:], AF.Silu), reads=[ccB], writes=[ccbB])

            mT_ps, mT_B = PS[0], PSB[0]
            row_ps = [(PS[1], PSB[1]), (PS[2], PSB[2])]
            bcast_q = []

            def mod_block(blk):
                while bcast_q:
                    bcast_q.pop(0)()
                t, tb = wm[blk % 2]
                for sub in range(2):
                    ch = blk * 2 + sub
                    for kc in range(8):
                        kb.op(PE, lambda: nc.tensor.matmul(
                            mT_ps[:, ch * 2:ch * 2 + 2], t[:, kc, sub * 128:(sub + 1) * 128], ccb[:, kc, :],
                            start=(kc == 0), stop=(kc == 7)),
                            reads=[tb, ccbB], writes=[mT_B], mark=(kc == 7))
                if blk in (8, 9, 10, 11, 20, 21, 22, 23):
                    rp, rpB = row_ps[blk % 2]
                    for kc in range(8):
                        kb.op(PE, lambda: nc.tensor.matmul(
                            rp[0:2, 0:256], ccb[:, kc, :], t[:, kc, :], start=(kc == 0), stop=(kc == 7)),
                            reads=[tb, ccbB], writes=[rpB], mark=(kc == 7))
                    kb.dma(SP, d_gtr, gtr[:], bmodR_d[0:1, blk * 256:(blk + 1) * 256], writes=[gtrB])
                    kb.op(DVE, lambda: nc.vector.tensor_tensor(gtr[0:1, :], rp[0:1, 0:256], gtr[0:1, :], ALU.add),
                          reads=[rpB, gtrB], writes=[gtrB])
                    kb.op(DVE, lambda: nc.vector.tensor_copy(gtrb[:], gtr[:]), reads=[gtrB], writes=[gtrbB])
                    kb.op(DVE, lambda: nc.vector.tensor_tensor(gtlo[:], gtr[:], gtrb[:], ALU.subtract),
                          reads=[gtrB, gtrbB], writes=[gtloB])
                    kb.op(DVE, lambda: nc.vector.tensor_copy(gtlob[:], gtlo[:]), reads=[gtloB], writes=[gtlobB])

                    def bcast_part(blk=blk):
                        bp, bpB = PS[3 + (blk % 2)], PSB[3 + (blk % 2)]
                        kb.op(PE, lambda: nc.tensor.matmul(bp[:, 0:256], ones[0:1, :], gtrb[0:1, :], start=True, stop=False),
                              reads=[onesB, gtrbB], writes=[bpB], mark=False)
                        kb.op(PE, lambda: nc.tensor.matmul(bp[:, 0:256], ones[0:1, :], gtlob[0:1, :], start=False, stop=True),
                              reads=[onesB, gtlobB], writes=[bpB])
                        dst, dstB = (gt1b, gt1bB) if blk < 12 else (gt2b, gt2bB)
                        o = (blk % 4) * 256
                        kb.op(ACT, lambda: nc.scalar.copy(dst[:, o:o + 256], bp[:, 0:256]),
                              reads=[bpB], writes=[dstB])
                    bcast_q.append(bcast_part)
                if blk + 2 < 24:
                    load_wm(blk + 2)
            for blk in range(8):
                mod_block(blk)
            win_k, win_q, win_ag, win_vb = Buf(), Buf(), Buf(), Buf()
            for (c_lo, c_hi, bb, dd) in ((512, 1024, win_k, d_win[0]), (0, 512, win_q, d_win[1]),
                                         (1536, 2560, win_ag, d_win[2]), (1024, 1536, win_vb, d_win[3])):
                kb.dma(POOL, dd, win[:, :, c_lo:c_hi], win_v[:, :, c_lo:c_hi], writes=[bb])
            kb.op(DVE, lambda: nc.vector.tensor_tensor(
                modT[:, 0:16, :], mT_ps[:, 0:32].rearrange("p (c t) -> p c t", t=2),
                bmT[:, 0:16].unsqueeze(2).broadcast_to([128, 16, 2]), ALU.add),
                reads=[mT_B, bmTB], writes=[modTB])
            kb.op(DVE, lambda: nc.vector.scalar_tensor_tensor(
                A1[:], modT[:, 8:16, :], 1.0, g1T[:].unsqueeze(2).broadcast_to([128, 8, 2]), ALU.add, ALU.mult),
                reads=[modTB, g1TB], writes=[A1B])
            kb.op(DVE, lambda: nc.vector.tensor_copy(SH1[:], modT[:, 0:8, :]), reads=[modTB], writes=[SH1B])

            def mod_finish():
                kb.op(DVE, lambda: nc.vector.tensor_tensor(
                    modT[:, 16:48, :], mT_ps[:, 32:96].rearrange("p (c t) -> p c t", t=2),
                    bmT[:, 16:48].unsqueeze(2).broadcast_to([128, 32, 2]), ALU.add),
                    reads=[mT_B, bmTB], writes=[modTB])
                kb.op(DVE, lambda: nc.vector.scalar_tensor_tensor(
                    A2[:], modT[:, 32:40, 0], 1.0, g2T[:], ALU.add, ALU.mult),
                    reads=[modTB, g2TB], writes=[A2B])
                kb.op(DVE, lambda: nc.vector.tensor_copy(SH2[:], modT[:, 24:32, 0]), reads=[modTB], writes=[SH2B])
                if dbg:
                    dump(modT[:].rearrange("p c t -> p (c t)"), 128, 96, [modTB])
                    dump(gt1b[:, 0:64], 128, 64, [gt1bB])
                    dump(gt2b[:, 0:64], 128, 64, [gt2bB])

            tp_banks = [6, 7]
            state = {"i": 0}

            def front(src_ap, hT_t, hT_b, col0, var):
                i = state["i"]
                state["i"] += 1
                x_t, x_b = xt[i % 2]
                xn_t, xn_b = xn[i % 2]
                ss_t, ss_b = ssq[i % 2]
                rs_t, rs_b = rst[i % 2]
                pb = tp_banks[i % 2]
                tps = PS[pb].bitcast(BF16)

                def p1():
                    kb.dma(SP, d_xt[i % 2], x_t[:], src_ap, writes=[x_b])
                    kb.op(ACT, lambda: nc.scalar.activation(sqj[:], x_t[:], AF.Square, accum_out=ss_t[:]),
                          reads=[x_b], writes=[sqjB, ss_b])
                    rstd_from_ssq(None, ss_t[:], ss_b, rs_t[:], rs_b, D)
                    kb.op(DVE, lambda: nc.vector.tensor_scalar(xn_t[:], x_t[:], rs_t[:, 0:1], None, ALU.mult),
                          reads=[x_b, rs_b], writes=[xn_b])

                def p2():
                    for kc in range(8):
                        kb.op(PE, lambda: nc.tensor.transpose(tps[:, kc * 128:(kc + 1) * 128],
                                                              xn_t[:, kc * 128:(kc + 1) * 128], ident[:]),
                              reads=[xn_b, identB], writes=[PSB[pb]], mark=(kc == 7))
                    hv = hT_t[:, :, col0:col0 + 128]
                    kb.op(DVE, lambda: nc.vector.tensor_tensor(
                        hv, tps[:, 0:1024].rearrange("p (c t) -> p c t", t=128),
                        A1[:, :, var:var + 1].broadcast_to([128, 8, 128]), ALU.mult),
                        reads=[PSB[pb], A1B], writes=[hT_b])
                    kb.op(DVE, lambda: nc.vector.tensor_tensor(
                        hv, hv, SH1[:, :, var:var + 1].broadcast_to([128, 8, 128]), ALU.add),
                        reads=[SH1B], writes=[hT_b])
                return p1, p2

            groups = [(0, 2), (2, 4), (6, 4), (10, 4), (14, 3), (17, 2)]
            bank_rr = [0]

            def nb():
                b = 1 + bank_rr[0] % 5
                bank_rr[0] += 1
                return b

            def group_front(gi):
                t0, ntl = groups[gi]
                h_t, h_b = hT[(gi + 1) % 2]
                parts = [front(x_d[(t0 + j) * 128:(t0 + j + 1) * 128, :], h_t, h_b, j * 128, 0)
                         for j in range(ntl)]
                items = [parts[0][0]]
                for j in range(ntl):
                    nxt_p1 = parts[j + 1][0] if j + 1 < ntl else (lambda: None)
                    items.append((lambda a, b: (lambda: (a(), b())))(parts[j][1], nxt_p1))
                return items

            for it in group_front(0):
                it()
            mod_next = [8]
            defer = []

            def tick():
                if defer:
                    defer.pop(0)()

            def mod_some():
                if mod_next[0] < 24:
                    mod_block(mod_next[0])
                    mod_next[0] += 1

            for gi, (t0, ntl) in enumerate(groups if stop_after >= 1 else []):
                h_t, h_b = hT[(gi + 1) % 2]
                defer = []
                if gi + 1 < len(groups):
                    gf = group_front(gi + 1)
                    for it in gf:
                        defer.append(it)
                        defer.append(mod_some)
                else:
                    hc_t, hc_b = hT[(len(groups) - 1) % 2]
                    cparts = [front(ctx_d[t * 128:(t + 1) * 128, :], hc_t, hc_b, t * 128, 1) for t in range(2)]
                    citems = [cparts[0][0], (lambda: (cparts[0][1](), cparts[1][0]())), cparts[1][1]]
                    for it in citems:
                        defer.append(it)
                        defer.append(mod_some)
                    defer += [mod_some] * 5
                if gi >= 1:
                    nd = len(defer)
                    for q_ in range(4):
                        if etab_items:
                            defer.insert(min(len(defer), 2 + q_ * (nd // 4 + 1) + q_), etab_items.pop(0))
                tick()
                N = ntl * 128
                nq = max(0, min(N, (NQ - t0) * 128))
                c0 = t0 * 128
                for ch in range(4):
                    pb = nb()
                    for kc in range(8):
                        kb.op(PE, lambda: nc.tensor.matmul(
                            PS[pb][:, 0:N], win[:, kc, 512 + ch * 128:512 + (ch + 1) * 128], h_t[:, kc, 0:N],
                            start=(kc == 0), stop=(kc == 7)),
                            reads=[win_k, h_b], writes=[PSB[pb]], mark=(kc == 7))
                    kb.op(DVE, lambda: nc.vector.tensor_copy(kT[:, ch, c0:c0 + N], PS[pb][:, 0:N]),
                          reads=[PSB[pb]], writes=kT_t[t0:t0 + ntl])
                    tick()
                for ch in range(4 if nq > 0 else 0):
                    pb = nb()
                    for kc in range(8):
                        kb.op(PE, lambda: nc.tensor.matmul(
                            PS[pb][:, 0:nq], win[:, kc, ch * 128:(ch + 1) * 128], h_t[:, kc, 0:nq],
                            start=(kc == 0), stop=(kc == 7)),
                            reads=[win_q, h_b], writes=[PSB[pb]], mark=(kc == 7))
                    kb.op(ACT, lambda: nc.scalar.mul(qT[:, ch, c0:c0 + nq], PS[pb][:, 0:nq], 0.125),
                          reads=[PSB[pb]], writes=qT_t[t0:t0 + nq // 128])
                    tick()
                for ch in range(4 if nq > 0 else 0):
                    pa_, pg_ = nb(), nb()
                    for kc in range(8):
                        kb.op(PE, lambda: nc.tensor.matmul(
                            PS[pg_][:, 0:nq], win[:, kc, 2048 + ch * 128:2048 + (ch + 1) * 128], h_t[:, kc, 0:nq],
                            start=(kc == 0), stop=(kc == 7)),
                            reads=[win_ag, h_b], writes=[PSB[pg_]], mark=(kc == 7))
                    for kc in range(8):
                        kb.op(PE, lambda: nc.tensor.matmul(
                            PS[pa_][:, 0:nq], win[:, kc, 1536 + ch * 128:1536 + (ch + 1) * 128], h_t[:, kc, 0:nq],
                            start=(kc == 0), stop=(kc == 7)),
                            reads=[win_ag, h_b], writes=[PSB[pa_]], mark=(kc == 7))
                    s_t, s_b = sg[ch % 2]
                    kb.op(ACT, lambda: nc.scalar.activation(s_t[:, 0:nq], PS[pg_][:, 0:nq], AF.Sigmoid),
                          reads=[PSB[pg_]], writes=[s_b])
                    kb.op(DVE, lambda: nc.vector.tensor_tensor(
                        uT[:, ch, 15 + c0:15 + c0 + nq], PS[pa_][:, 0:nq], s_t[:, 0:nq], ALU.mult),
                        reads=[PSB[pa_], s_b], writes=uT_t[1 + t0:1 + t0 + nq // 128])
                for j in range(ntl):
                    pb = nb()
                    for kc in range(8):
                        kb.op(PE, lambda: nc.tensor.matmul(
                            PS[pb][:, :], h_t[:, kc, j * 128:(j + 1) * 128], win[:, kc, 1024:1536],
                            start=(kc == 0), stop=(kc == 7)),
                            reads=[win_vb, h_b], writes=[PSB[pb]], mark=(kc == 7))
                    kb.op(ACT, lambda: nc.scalar.copy(
                        vaug[:, t0 + j, :].rearrange("p (h e) -> p h e", e=65)[:, :, 0:64],
                        PS[pb][:, :].rearrange("p (h e) -> p h e", e=64)),
                        reads=[PSB[pb]], writes=[v_t[t0 + j]])
                    tick()
                while defer:
                    tick()

            for ch in range(4):
                pb = 1 + ch
                for kc in range(8):
                    kb.op(PE, lambda: nc.tensor.matmul(
                        PS[pb][:, 0:256], win[:, kc, 512 + ch * 128:512 + (ch + 1) * 128], hc_t[:, kc, 0:256],
                        start=(kc == 0), stop=(kc == 7)),
                        reads=[win_k, hc_b], writes=[PSB[pb]], mark=(kc == 7))
                kb.op(DVE, lambda: nc.vector.tensor_copy(kcT[:, ch, :], PS[pb][:, 0:256]),
                      reads=[PSB[pb]], writes=[kcTB])
            for t in range(2):
                pb = 1 + t
                for kc in range(8):
                    kb.op(PE, lambda: nc.tensor.matmul(
                        PS[pb][:, :], hc_t[:, kc, t * 128:(t + 1) * 128], win[:, kc, 1024:1536],
                        start=(kc == 0), stop=(kc == 7)),
                        reads=[win_vb, hc_b], writes=[PSB[pb]], mark=(kc == 7))
                kb.op(DVE, lambda: nc.vector.tensor_copy(
                    vcaug[:, t, :].rearrange("p (h e) -> p h e", e=65)[:, :, 0:64],
                    PS[pb][:, :].rearrange("p (h e) -> p h e", e=64)),
                    reads=[PSB[pb]], writes=[vcaugB])

            if dbg:
                dump(kcT[:, 0, 0:64], 128, 64, [kcTB])
                dump(vcaug[:, 0, 0:130], 128, 130, [vcaugB])

            while mod_next[0] < 24:
                mod_block(mod_next[0])
                mod_next[0] += 1
            while bcast_q:
                bcast_q.pop(0)()
            mod_finish()
            if dbg and stop_after >= 1:
                dump(kT[:, 1, 0:256], 128, 256, kT_t[0:2])
                dump(qT[:, 2, 128:384], 128, 256, qT_t[1:3])
                dump(uT[:, 3, 0:256], 128, 256, uT_t[0:3])
                dump(vaug[:, 5, 0:260], 128, 260, [v_t[5]])
                dump(kT[:, 3, TOK - 128:TOK], 128, 128, [kT_t[NT - 1]])

            kb.barrier(ENGS, DSEMS)

        if stop_after >= 2:
            B_ar.reset()
            Y_ar.reset()
            yT, _ = sb(Y_ar, "yT", [128, 8, NQT], BF16)
            yT_t = [Buf() for _ in range(NQ)]
            while etab_items:
                etab_items.pop(0)()
            pT = [sb(B_ar, f"pT{i}", [128, 7, 128], BF16) for i in range(4)]
            on = [sb(B_ar, f"on{i}", [128, 512], BF16) for i in range(2)]
            rec, recB = sb(B_ar, "rec", [128, 8], F32)
            CN = 256
            cvS, cvbS, sqbS = [], [], []
            for i_ in range(2):
                cvS.append((sb(B_ar, f"cv{i_}", [128, 4, CN], F32)[0], [Buf() for _ in range(4)]))
                cvbS.append((sb(B_ar, f"cvb{i_}", [128, 4, CN], BF16)[0], [Buf() for _ in range(4)]))
                sqbS.append((sb(B_ar, f"sqb{i_}", [128, 4, CN], BF16)[0], [Buf() for _ in range(4)]))
            rsd, rsdB = sb(B_ar, "rsd", [128, CN], F32)
            T_ar = Arena(BST_OFF, BST_OFF + 3584)
            mean, meanB = sb(T_ar, "mean", [128, CN], F32)
            tm = [sb(T_ar, f"tm{i}", [128, CN], F32) for i in range(2)]
            for bb_ in (meanB, tm[0][1], tm[1][1]):
                bb_.r = {id(e_.sem): (e_.sem, e_.n) for e_ in ENGS}
                for d_ in DSEMS:
                    bb_.r[id(d_.sem)] = (d_.sem, d_.n)
            DG_OFF = (B_ar.cur + 31) // 32 * 32
            assert DG_OFF >= H2_END, (DG_OFF, H2_END)
            dg, _ = sb(B_ar, "dg", [128, 4, 31, 128], BF16)
            dgB = [Buf() for _ in range(4)]
            for c in range(4):
                kb.op(DVE, lambda: nc.vector.tensor_tensor(
                    dg[:, c, :, :], ident[:].unsqueeze(1).broadcast_to([128, 31, 128]),
                    cwT[:, c, :].unsqueeze(2).broadcast_to([128, 31, 128]), ALU.mult),
                    reads=[identB, cwTB], writes=[dgB[c]])

            ST_PAIRS = [(0, 1), (2, 3), (6, 7)]

            def tile_info(qt):
                kts = list(range(max(0, qt - 2), max(qt + 2, 3) + 1))
                ti0 = 0 if qt == 0 else (4 if qt == 1 else 8)
                return kts, len(kts), ti0

            def qk(qt, h, n):
                kts, nl, ti0 = tile_info(qt)
                hc, hp = h // 2, (h % 2) * 64
                bA, bB = ST_PAIRS[n % 3]
                p_t, p_b = pT[n % 4]
                q_ap = (qT if h % 2 == 0 else qB)[:, hc, qt * 128:(qt + 1) * 128]
                for j in range(nl + 2):
                    bank = bA if j < 4 else bB
                    col = (j % 4) * 128
                    if j < nl:
                        l_ap = kT[:, hc, kts[j] * 128:(kts[j] + 1) * 128]
                        lb = kT_t[kts[j]]
                    else:
                        l_ap = kcT[:, hc, (j - nl) * 128:(j - nl + 1) * 128]
                        lb = kcTB
                    kb.op(PE, lambda: nc.tensor.matmul(PS[bank][:, col:col + 128], l_ap, q_ap,
                                                       start=True, stop=True),
                          reads=[lb, qT_t[qt], qB_t[qt]], writes=[PSB[bank]], mark=(j == 3 or j == nl + 1))
                nt_ = nl + 2
                assert bB == bA + 1
                kb.op(ACT, lambda: nc.scalar.activation(
                    p_t[:, 0:nt_, :].rearrange("p a b -> p (a b)"),
                    PSBIG[:, bA * 512:bA * 512 + nt_ * 128], AF.Exp),
                    reads=[PSB[bA], PSB[bB]], writes=[p_b])
                kb.op(DVE, lambda: nc.vector.tensor_tensor(
                    p_t[:, 0:nl, :], p_t[:, 0:nl, :], Etab[:, h, ti0:ti0 + nl, :], ALU.mult),
                    reads=[p_b, EtabB], writes=[p_b])

            def pv(qt, h, n):
                kts, nl, ti0 = tile_info(qt)
                p_t, p_b = pT[n % 4]
                ob, oc = 4 + h // 4, (h % 4) * 65
                for j in range(nl + 2):
                    if j < nl:
                        r_ap, rb = vaug[:, kts[j], h * 65:(h + 1) * 65], v_t[kts[j]]
                    else:
                        r_ap, rb = vcaug[:, j - nl, h * 65:(h + 1) * 65], vcaugB
                    kb.op(PE, lambda: nc.tensor.matmul(PS[ob][:, oc:oc + 65], p_t[:, j, :], r_ap,
                                                       start=(j == 0), stop=(j == nl + 1)),
                          reads=[p_b, rb], writes=[PSB[ob]], mark=(j == nl + 1))

            def normalize(qt, halves=(0, 1)):
                o_t, o_b = on[qt % 2]
                for half in halves:
                    ob = 4 + half
                    ov = PS[ob][:, 0:260].rearrange("p (h e) -> p h e", e=65)
                    kb.op(DVE, lambda: nc.vector.reciprocal(
                        rec[:, half * 4:(half + 1) * 4].unsqueeze(2), ov[:, :, 64:65]),
                        reads=[PSB[ob]], writes=[recB])
                    kb.op(DVE, lambda: nc.vector.tensor_tensor(
                        o_t[:, half * 256:(half + 1) * 256].rearrange("p (h e) -> p h e", e=64), ov[:, :, 0:64],
                        rec[:, half * 4:(half + 1) * 4].unsqueeze(2).broadcast_to([128, 4, 64]), ALU.mult),
                        reads=[PSB[ob], recB], writes=[o_b])

            def finish(qt):
                o_t, o_b = on[qt % 2]
                tps = PS[5].bitcast(BF16)
                for c in range(4):
                    kb.op(PE, lambda: nc.tensor.transpose(tps[:, c * 128:(c + 1) * 128],
                                                          o_t[:, c * 128:(c + 1) * 128], ident[:]),
                          reads=[o_b, identB], writes=[PSB[5]], mark=(c == 3))
                kb.op(DVE, lambda: nc.vector.tensor_copy(
                    yT[:, 0:4, qt * 128:(qt + 1) * 128], tps[:, 0:512].rearrange("p (c t) -> p c t", t=128)),
                    reads=[PSB[5]], writes=[yT_t[qt]])

            def attention_pass(mid_hook):
                units = [(qt, h) for qt in range(NQ) for h in range(8)]
                qprep(0)
                qprep(1)
                qk(units[0][0], units[0][1], 0)
                qk(units[1][0], units[1][1], 1)
                def pv_unit(m):
                    qt_, h_ = units[m]
                    pv(qt_, h_, m)
                    if h_ == 3:
                        normalize(qt_, (0,))
                    if h_ == 7:
                        normalize(qt_, (1,))
                    if h_ == 2 and qt_ > 0:
                        finish(qt_ - 1)

                for n, (qt, h) in enumerate(units):
                    if n + 2 < len(units):
                        if h == 0 and qt + 2 < NQ:
                            qprep(qt + 2)
                        qk(units[n + 2][0], units[n + 2][1], n + 2)
                    if n >= 1:
                        pv_unit(n - 1)
                    if 40 <= n < 72:
                        mid_hook(n - 40)
                pv_unit(len(units) - 1)
                finish(NQ - 1)

            CB = [0, 1, 4, 5]

            def conv_block(tok0, n, si):
                tl0 = max(tok0 - 15, 0) // 128
                tl1 = min((tok0 + n + 14) // 128, NQ - 1)
                ub = [uT_t[0]] + uT_t[1 + tl0:2 + tl1]
                cv, cvB = cvS[si]
                cvb, cvbB = cvbS[si]
                sqb, sqbB = sqbS[si]

                def part_a(c):
                    for k in range(31):
                        kb.op(PE, lambda: nc.tensor.matmul(PS[CB[c]][:, 0:n], dg[:, c, k, :],
                                                           uT[:, c, tok0 + k:tok0 + k + n],
                                                           start=(k == 0), stop=(k == 30)),
                              reads=[dgB[c]] + ub, writes=[PSB[CB[c]]], mark=(k == 30))
                    kb.op(ACT, lambda: nc.scalar.activation(cv[:, c, 0:n], PS[CB[c]][:, 0:n], AF.Identity,
                                                            bias=cbT[:, c:c + 1]),
                          reads=[PSB[CB[c]], cbTB], writes=[cvB[c]])
                    kb.op(ACT, lambda: nc.scalar.activation(sqb[:, c, 0:n], PS[CB[c]][:, 0:n], AF.Square,
                                                            bias=cbT[:, c:c + 1]),
                          reads=[PSB[CB[c]], cbTB], writes=[sqbB[c]])
                    kb.op(DVE, lambda: nc.vector.tensor_copy(cvb[:, c, 0:n], cv[:, c, 0:n]),
                          reads=[cvB[c]], writes=[cvbB[c]])

                def stats():
                    for c in range(4):
                        kb.op(PE, lambda: nc.tensor.matmul(PS[2][:, 0:n], ones[:], cvb[:, c, 0:n],
                                                           start=(c == 0), stop=(c == 3)),
                              reads=[onesB, cvbB[c]], writes=[PSB[2]], mark=(c == 3))
                    for c in range(4):
                        kb.op(PE, lambda: nc.tensor.matmul(PS[3][:, 0:n], ones[:], sqb[:, c, 0:n],
                                                           start=(c == 0), stop=(c == 3)),
                              reads=[onesB, sqbB[c]], writes=[PSB[3]], mark=(c == 3))

                def part_b():
                    kb.op(DVE, lambda: nc.vector.tensor_scalar(mean[:, 0:n], PS[2][:, 0:n], 1.0 / 512, None, ALU.mult),
                          reads=[PSB[2]], writes=[meanB])
                    kb.op(DVE, lambda: nc.vector.tensor_tensor(rsd[:, 0:n], mean[:, 0:n], mean[:, 0:n], ALU.mult),
                          reads=[meanB], writes=[rsdB])
                    kb.op(DVE, lambda: nc.vector.scalar_tensor_tensor(
                        rsd[:, 0:n], PS[3][:, 0:n], 1.0 / 512, rsd[:, 0:n], ALU.mult, ALU.subtract),
                        reads=[PSB[3], rsdB], writes=[rsdB])
                    kb.op(ACT, lambda: nc.scalar.activation(rsd[:, 0:n], rsd[:, 0:n], AF.Sqrt, bias=epsT[:, 0:1]),
                          reads=[rsdB, epsTB], writes=[rsdB])
                    kb.op(DVE, lambda: nc.vector.reciprocal(rsd[:, 0:n], rsd[:, 0:n]), reads=[rsdB], writes=[rsdB])
                    t0_ = tok0 // 128
                    yb = yT_t[t0_:t0_ + max(1, n // 128)]
                    for c in range(4):
                        t_t, t_b = tm[c % 2]
                        kb.op(DVE, lambda: nc.vector.tensor_tensor(t_t[:, 0:n], cv[:, c, 0:n], mean[:, 0:n], ALU.subtract),
                              reads=[cvB[c], meanB], writes=[t_b])
                        kb.op(DVE, lambda: nc.vector.tensor_tensor(t_t[:, 0:n], t_t[:, 0:n], rsd[:, 0:n], ALU.mult),
                              reads=[t_b, rsdB], writes=[t_b])
                        kb.op(ACT, lambda: nc.scalar.activation(
                            yT[:, 4 + c, tok0:tok0 + n], t_t[:, 0:n], AF.Silu,
                            bias=lnbT[:, c:c + 1], scale=lngT[:, c:c + 1]),
                            reads=[t_b, lngTB, lnbTB], writes=yb)
                return part_a, stats, part_b

            cblocks = [(2048, 16)] + [(blk * 256, 256) for blk in range(8)]
            prev_cb = None
            for bi_, (tk0, nn) in enumerate(cblocks):
                pa_f, st_f, pb_f = conv_block(tk0, nn, bi_ % 2)
                pa_f(0)
                if prev_cb is not None:
                    prev_cb[0]()
                    prev_cb[1]()
                pa_f(1)
                pa_f(2)
                pa_f(3)
                prev_cb = (st_f, pb_f)
            prev_cb[0]()
            prev_cb[1]()

            W_ar = Arena(DG_OFF, DG_OFF + 16384 + 32)
            wout, woutB = sb(W_ar, "wout", [128, 8, D], BF16)
            wout_v = wout_d.rearrange("(kc p) n -> p kc n", p=128)
            for i in range(2):
                kb.dma(POOL, d_wout, wout[:, i * 4:(i + 1) * 4, :], wout_v[:, i * 4:(i + 1) * 4, :],
                       reads=[], writes=dgB)
            woutB.w = (d_wout.sem, d_wout.n)

            def scale_wout(j):
                kc, q4 = j // 4, j % 4
                cs = slice(q4 * 256, (q4 + 1) * 256)
                kb.op(DVE, lambda: nc.vector.tensor_tensor(wout[:, kc, cs], wout[:, kc, cs], gt1b[:, cs], ALU.mult),
                      reads=[woutB, gt1bB], writes=[woutB])

            UT_OFF = None
            qB_ar = Arena(UT_ADDR, UT_ADDR + 4 * UW * 2)
            qB, qBB = sb(qB_ar, "qB", [128, 4, NQT], BF16)
            qB_t = [Buf() for _ in range(NQ)]
            for bb_ in qB_t:
                bb_.r = {id(e_.sem): (e_.sem, e_.n) for e_ in ENGS}

            def qprep(qt):
                cs = slice(qt * 128, (qt + 1) * 128)
                kb.op(POOL, lambda: nc.gpsimd.memset(qB[0:64, :, cs], 0.0), writes=[qB_t[qt]])
                kb.op(DVE, lambda: nc.vector.tensor_copy(qB[64:128, :, cs], qT[64:128, :, cs]),
                      reads=[qT_t[qt]], writes=[qB_t[qt]])
                kb.op(DVE, lambda: nc.vector.memset(qT[64:128, :, cs], 0.0), writes=[qT_t[qt]])
            attention_pass(scale_wout)

            if dbg:
                dump(yT[:, 1, 0:256], 128, 256, yT_t[0:2])
                dump(yT[:, 2, 2048 - 128:2048 + 128], 128, 256, yT_t[15:17])
                dump(yT[:, 5, 0:256], 128, 256, yT_t[0:2])
                dump(yT[:, 7, 2048 - 240:2048 + 16], 128, 256, yT_t[14:17])
            kb.barrier(ENGS, DSEMS)

        if stop_after >= 3:
            GMAX = max(FF_GROUPS)
            W0 = 3 * (GMAX * 128 * 8 * 2) + 96
            B2hi_ar = Arena(SB_HI - W0, SB_HI)
            B2_ar.hi = SB_HI - W0
            wupg, wupv, wdn = [None, None], [None, None], [None, None]
            wupg[0] = sb(B2hi_ar, "wupg0", [128, 8, GMAX * 128], BF16)
            wupv[0] = sb(B2hi_ar, "wupv0", [128, 8, GMAX * 128], BF16)
            wdn[0] = sb(B2hi_ar, "wdn0", [128, GMAX, D], BF16)
            wup_v = wup_d.rearrange("(kc p) n -> p kc n", p=128)
            gstart = [sum(FF_GROUPS[:i]) for i in range(len(FF_GROUPS))]
            wB = [[Buf(), Buf(), Buf()] for _ in range(2)]

            def load_group(gi):
                j0, n = gstart[gi], FF_GROUPS[gi]
                s = gi % 2
                kb.dma(POOL, d_wg[s], wupg[s][0][:, :, 0:n * 128], wup_v[:, :, j0 * 128:(j0 + n) * 128],
                       writes=[wB[s][0]])
                kb.dma(POOL, d_wv[s], wupv[s][0][:, :, 0:n * 128], wup_v[:, :, DFF + j0 * 128:DFF + (j0 + n) * 128],
                       writes=[wB[s][1]])
                kb.dma(POOL, d_wd[s], wdn[s][0][:, 0:n, :],
                       wdn_d[j0 * 128:(j0 + n) * 128, :].rearrange("(c p) n -> p c n", p=128),
                       writes=[wB[s][2]])

            def scale_group(gi):
                n = FF_GROUPS[gi]
                s = gi % 2
                kb.op(DVE, lambda: nc.vector.tensor_tensor(
                    wdn[s][0][:, 0:n, :], wdn[s][0][:, 0:n, :],
                    gt2b[:].unsqueeze(1).broadcast_to([128, n, D]), ALU.mult),
                    reads=[wB[s][2], gt2bB], writes=[wB[s][2]])

            B2_ar.lo = DG_OFF + 16384 + 32
            B2_ar.reset()
            xm, _ = sb(XM_ar, "xm", [128, NOWN, D], F32)
            xm_t = [Buf() for _ in range(NOWN)]
            H2W = 2 + NOWN * 128
            h2T, _ = sb(H2_ar, "h2T", [128, 8, H2W], BF16)
            h2_t = [Buf() for _ in range(NOWN + 2)]
            xr = [sb(B2_ar, f"xr{i}", [128, D], F32) for i in range(3)]
            xn2 = [sb(B2_ar, f"xn2{i}", [128, D], BF16) for i in range(4)]
            ssq2 = [sb(B2_ar, f"ssq2{i}", [128, 1], F32) for i in range(4)]
            rst2 = [sb(B2_ar, f"rst2{i}", [128, 1], F32) for i in range(4)]
            kb.op(DVE, lambda: nc.vector.memset(h2T[:, :, 0:1], 0.0), writes=[h2_t[0]])
            if stop_after >= 4:
                load_group(0)

            def norm2_tile(src_ap, srcB, np_, i, dst_cols, dstB):
                xn_t, xn_b = xn2[i % 4]
                ss_t, ss_b = ssq2[i % 4]
                rs_t, rs_b = rst2[i % 4]

                def stage_a():
                    kb.op(ACT, lambda: nc.scalar.activation(xn_t[0:np_, :], src_ap, AF.Square, accum_out=ss_t[0:np_, :]),
                          reads=[srcB], writes=[xn_b, ss_b])
                    kb.op(ACT, lambda: nc.scalar.activation(rs_t[0:np_, :], ss_t[0:np_, :], AF.Sqrt,
                                                            bias=epsT[0:np_, 0:1], scale=1.0 / D),
                          reads=[ss_b, epsTB], writes=[rs_b])

                def stage_b():
                    kb.op(DVE, lambda: nc.vector.reciprocal(rs_t[0:np_, :], rs_t[0:np_, :]), reads=[rs_b], writes=[rs_b])
                    kb.op(ACT, lambda: nc.scalar.activation(xn_t[0:np_, :], src_ap, AF.Copy, scale=rs_t[0:np_, 0:1]),
                          reads=[srcB, rs_b], writes=[xn_b])
                pb = 6 + (i % 2)

                def part2():
                    if np_ == 128:
                        tps = PS[pb].bitcast(BF16)
                        for kc in range(8):
                            kb.op(PE, lambda: nc.tensor.transpose(tps[:, kc * 128:(kc + 1) * 128],
                                                                  xn_t[:, kc * 128:(kc + 1) * 128], ident[:]),
                                  reads=[xn_b, identB], writes=[PSB[pb]], mark=(kc == 7))
                        hv = h2T[:, :, dst_cols[0]:dst_cols[1]]
                        kb.op(DVE, lambda: nc.vector.tensor_tensor(
                            hv, tps[:, 0:1024].rearrange("p (c t) -> p c t", t=128),
                            A2[:, :].unsqueeze(2).broadcast_to([128, 8, 128]), ALU.mult),
                            reads=[PSB[pb], A2B], writes=[dstB])
                        kb.op(DVE, lambda: nc.vector.tensor_tensor(
                            hv, hv, SH2[:, :].unsqueeze(2).broadcast_to([128, 8, 128]), ALU.add),
                            reads=[SH2B], writes=[dstB])
                    else:
                        for kc in range(8):
                            kb.op(PE, lambda: nc.tensor.matmul(PS[pb][:, kc:kc + 1], xn_t[0:1, kc * 128:(kc + 1) * 128],
                                                               ones[0:1, 0:1], start=True, stop=True),
                                  reads=[xn_b, onesB], writes=[PSB[pb]], mark=(kc == 7))
                        for kc in range(8):
                            kb.op(ACT, lambda: nc.scalar.activation(
                                h2T[:, kc, dst_cols[0]:dst_cols[1]], PS[pb][:, kc:kc + 1], AF.Identity,
                                bias=SH2[:, kc:kc + 1], scale=A2[:, kc:kc + 1]),
                                reads=[PSB[pb], A2B, SH2B], writes=[dstB])
                return stage_a, stage_b, part2

            B2a_ar = Arena(H2_END, DG_OFF)
            xh, xhB = sb(B2a_ar, "xh", [1, D], F32)
            kb.dma(SP, d_xh, xh[:], x_d[2048:2049, :], writes=[xhB])
            for half in range(2):
                pb = half
                for kc in range(8):
                    kb.op(PE, lambda: nc.tensor.matmul(PS[pb][0:1, :], yT[:, kc, 2048:2049],
                                                       wout[:, kc, half * 512:(half + 1) * 512],
                                                       start=(kc == 0), stop=(kc == 7)),
                          reads=[yT_t[16], woutB], writes=[PSB[pb]], mark=(kc == 7))
                kb.op(DVE, lambda: nc.vector.tensor_tensor(
                    xh[0:1, half * 512:(half + 1) * 512], PS[pb][0:1, :], xh[0:1, half * 512:(half + 1) * 512],
                    ALU.add), reads=[PSB[pb], xhB], writes=[xhB])
            halo_fns = list(norm2_tile(xh[0:1, :], xhB, 1, 3, (H2W - 1, H2W), h2_t[NOWN + 1]))
            halo_fns[0]()
            qB2, qC2 = [], []
            for t in range(NOWN):
                x_t, x_b = xr[t % 3]
                kb.dma(SP, d_xr[t % 3], x_t[:], x_d[t * 128:(t + 1) * 128, :], writes=[x_b])
                pb0 = (2 * t) % 6
                for half in range(2):
                    pb = pb0 + half
                    for kc in range(8):
                        kb.op(PE, lambda: nc.tensor.matmul(PS[pb][:, :], yT[:, kc, t * 128:(t + 1) * 128],
                                                           wout[:, kc, half * 512:(half + 1) * 512],
                                                           start=(kc == 0), stop=(kc == 7)),
                              reads=[yT_t[t], woutB], writes=[PSB[pb]], mark=(kc == 7))
                kb.op(DVE, lambda: nc.vector.tensor_tensor(
                    xm[:, t, :], PSBIG[:, pb0 * 512:pb0 * 512 + 1024], x_t[:, :], ALU.add),
                    reads=[PSB[pb0], PSB[pb0 + 1], x_b], writes=[xm_t[t]])
                p2 = norm2_tile(xm[:, t, :], xm_t[t], 128, t, (1 + t * 128, 1 + (t + 1) * 128), h2_t[t + 1])
                sa_, sb_, sc_ = p2
                sa_()
                if t == 0:
                    halo_fns[1]()
                if t == 1:
                    halo_fns[2]()
                if t == 11 and stop_after >= 4:
                    scale_group(0)
                if qB2:
                    qB2.pop(0)()
                qB2.append(sb_)
                qC2.append(sc_)
                if len(qC2) > 3:
                    qC2.pop(0)()
            while qB2:
                qB2.pop(0)()
            while qC2:
                qC2.pop(0)()

            if dbg:
                dump(xm[:, 3, 0:256], 128, 256, [xm_t[3]])
                dump(h2T[:, 2, 0:256], 128, 256, h2_t[0:3])
                dump(h2T[:, 6, H2W - 128:H2W], 128, 128, h2_t[16:18])

        if stop_after >= 4:
            B2_ar.lo = H2_END
            B2_ar.reset()
            Y_ar.reset()
            for i_ in (1,):
                wupg[i_] = sb(B2_ar, f"wupg{i_}", [128, 8, GMAX * 128], BF16)
                wupv[i_] = sb(B2_ar, f"wupv{i_}", [128, 8, GMAX * 128], BF16)
                wdn[i_] = sb(B2_ar, f"wdn{i_}", [128, GMAX, D], BF16)
            gfb, gfbB = sb(B2_ar, "gfb", [128, D], F32)
            ot = [sb(B2_ar, f"ot{i}", [128, D], F32) for i in range(3)]
            sq3, sq3B = sb(B2_ar, "sq3", [128, D], BF16)
            ssq3 = [sb(B2_ar, f"ssq3{i}", [128, 1], F32) for i in range(3)]
            rst3 = [sb(B2_ar, f"rst3{i}", [128, 1], F32) for i in range(3)]
            hid, _ = sb(Y_ar, "hid", [128, GMAX, NOWN * 128], BF16)
            hidB = [Buf() for _ in FF_BLOCKS]
            NB = 412
            gl = [sb(Y_ar, f"gl{i}", [128, NB], F32) for i in range(2)]
            vl = [sb(Y_ar, f"vl{i}", [128, NB], F32) for i in range(2)]
            sl = [sb(Y_ar, f"sl{i}", [128, NB], F32) for i in range(2)]
            gfbB.r = {id(e_.sem): (e_.sem, e_.n) for e_ in ENGS}
            for d_ in DSEMS:
                gfbB.r[id(d_.sem)] = (d_.sem, d_.n)
            kb.dma(SP, d_xh, gfb[:], gfin_d.partition_broadcast(128), writes=[gfbB])
            pend = []
            fin_q = []
            sq_q = []
            acc_i = [0]
            done_tiles = {}
            cnt = [0]

            def down(gi, t):
                n = FF_GROUPS[gi]
                s = gi % 2
                last = gi == len(FF_GROUPS) - 1
                hb = [hidB[bi] for bi, (o0, no) in enumerate(FF_BLOCKS)
                      if o0 < (t + 1) * 128 and o0 + no > t * 128]
                o_t, o_b = ot[t % 3]
                pb0 = 4 + (2 * t) % 4
                for half in range(2):
                    pb = pb0 + half
                    for i in range(n):
                        kb.op(PE, lambda: nc.tensor.matmul(PS[pb][:, :], hid[:, i, t * 128:(t + 1) * 128],
                                                           wdn[s][0][:, i, half * 512:(half + 1) * 512],
                                                           start=(i == 0), stop=(i == n - 1)),
                              reads=hb + [wB[s][2]], writes=[PSB[pb]], mark=(i == n - 1))
                psum2 = PSBIG[:, pb0 * 512:pb0 * 512 + 1024]
                k_ = acc_i[0]
                acc_i[0] += 1
                tmp_t, tmp_b = ot[k_ % 3]
                kb.op(ACT, lambda: nc.scalar.copy(tmp_t[:], psum2),
                      reads=[PSB[pb0], PSB[pb0 + 1]], writes=[tmp_b])
                kb.dma(POOL, d_acc[k_ % 8], xm[:, t, :], tmp_t[:], reads=[tmp_b, xm_t[t]], writes=[xm_t[t]],
                       accum_op=ALU.add)
                if last:
                    ss_t, ss_b = ssq3[t % 3]
                    rs_t, rs_b = rst3[t % 3]
                    while len(fin_q) > 1:
                        fin_q.pop(0)()
                    if sq_q:
                        sq_q.pop(0)()

                    def sq_stage():
                        kb.op(ACT, lambda: nc.scalar.activation(sq3[:], xm[:, t, :], AF.Square, accum_out=ss_t[:]),
                              reads=[xm_t[t]], writes=[sq3B, ss_b])
                        kb.op(ACT, lambda: nc.scalar.activation(rs_t[:], ss_t[:], AF.Sqrt, bias=epsT[:, 0:1],
                                                                scale=1.0 / D),
                              reads=[ss_b, epsTB], writes=[rs_b])

                    def fin():
                        kb.op(DVE, lambda: nc.vector.reciprocal(rs_t[:], rs_t[:]), reads=[rs_b], writes=[rs_b])
                        kb.op(DVE, lambda: nc.vector.scalar_tensor_tensor(
                            xm[:, t, :], xm[:, t, :], rs_t[:, 0:1], gfb[:], ALU.mult, ALU.mult),
                            reads=[xm_t[t], rs_b, gfbB], writes=[xm_t[t]])
                        kb.dma(SP, d_ot[t % 3], out_d[t * 128:(t + 1) * 128, :], xm[:, t, :], reads=[xm_t[t]])
                    sq_q.append(sq_stage)
                    fin_q.append(fin)

            for gi in range(len(FF_GROUPS)):
                j0, n = gstart[gi], FF_GROUPS[gi]
                s = gi % 2
                for bi, (o0, no) in enumerate(FF_BLOCKS):
                    ncol = no + 2
                    t_lo = max(o0 - 1, 0) // 128
                    t_hi = min((o0 + no) // 128, NOWN - 1)
                    hb = [h2_t[0]] + h2_t[1 + t_lo:2 + t_hi] + ([h2_t[NOWN + 1]] if o0 + no == NOWN * 128 else [])
                    for i in range(n):
                        j = j0 + i
                        k = cnt[0] % 2
                        cnt[0] += 1
                        bg, bv = 2 * k, 2 * k + 1
                        for kc in range(8):
                            kb.op(PE, lambda: nc.tensor.matmul(PS[bg][:, 0:ncol], wupg[s][0][:, kc, i * 128:(i + 1) * 128],
                                                               h2T[:, kc, o0:o0 + ncol], start=(kc == 0), stop=(kc == 7)),
                                  reads=hb + [wB[s][0]], writes=[PSB[bg]], mark=(kc == 7))
                        for kc in range(8):
                            kb.op(PE, lambda: nc.tensor.matmul(PS[bv][:, 0:ncol], wupv[s][0][:, kc, i * 128:(i + 1) * 128],
                                                               h2T[:, kc, o0:o0 + ncol], start=(kc == 0), stop=(kc == 7)),
                                  reads=hb + [wB[s][1]], writes=[PSB[bv]], mark=(kc == 7))
                        g_t, g_b = gl[k]
                        v_t_, v_b = vl[k]
                        s_t, s_b = sl[k]
                        for (acc, accB, pbk, ch) in ((g_t, g_b, bg, j), (v_t_, v_b, bv, NFC + j)):
                            kb.op(ACT, lambda: nc.scalar.activation(
                                acc[:, 0:no], PS[pbk][:, 1:1 + no], AF.Identity,
                                bias=fcb[:, ch:ch + 1], scale=fcw[:, ch, 1:2]),
                                reads=[PSB[pbk], fcwB, fcbB], writes=[accB])
                            kb.op(DVE, lambda: nc.vector.scalar_tensor_tensor(
                                acc[:, 0:no], PS[pbk][:, 0:no], fcw[:, ch, 0:1], acc[:, 0:no], ALU.mult, ALU.add),
                                reads=[PSB[pbk], accB, fcwB], writes=[accB])
                            kb.op(DVE, lambda: nc.vector.scalar_tensor_tensor(
                                acc[:, 0:no], PS[pbk][:, 2:2 + no], fcw[:, ch, 2:3], acc[:, 0:no], ALU.mult, ALU.add),
                                reads=[PSB[pbk], accB, fcwB], writes=[accB])
                        kb.op(ACT, lambda: nc.scalar.activation(s_t[:, 0:no], g_t[:, 0:no], AF.Silu),
                              reads=[g_b], writes=[s_b])
                        kb.op(POOL, lambda: nc.gpsimd.tensor_tensor(
                            hid[:, i, o0:o0 + no], s_t[:, 0:no], v_t_[:, 0:no], ALU.mult),
                            reads=[s_b, v_b], writes=[hidB[bi]])
                    for (pg, pt) in pend:
                        down(pg, pt)
                    pend = []
                    if bi == 0 and gi + 1 < len(FF_GROUPS):
                        load_group(gi + 1)
                    if bi == 3 and gi + 1 < len(FF_GROUPS):
                        scale_group(gi + 1)
                    t_done = (o0 + no) // 128
                    t_prev = done_tiles.get(gi, 0)
                    for t in range(t_prev, t_done):
                        pend.append((gi, t))
                    done_tiles[gi] = t_done
            for (pg, pt) in pend:
                down(pg, pt)
            pend = []
            while sq_q:
                sq_q.pop(0)()
            while fin_q:
                fin_q.pop(0)()

        kb.barrier(ENGS, DSEMS)
    return nc


def _bias_tables(rpb, half):
    H = rpb.shape[0]
    ext = np.concatenate([rpb.reshape(H, -1), np.full((H, 1), -30000.0, np.float32)], axis=1)
    MASK = 15 * 31
    specs = [(0, k) for k in range(4)] + [(1, k) for k in range(4)] + [(8, 8 + d) for d in range(-2, 3)]
    idx = np.zeros((128, NTAB, 128), np.int64)
    kk = np.arange(128)
    for ti, (qt, kt) in enumerate(specs):
        rl = 2 * qt + (kk // 64)
        cl = kk % 64
        krl = 2 * kt + (kk // 64)
        kcl = kk % 64
        if half == 0:
            rg, cg, krg, kcg = rl, cl, krl, kcl
        else:
            rg, cg, krg, kcg = 63 - rl, 63 - cl, 63 - krl, 63 - kcl
        rs = np.clip(rg - 4, 0, 56)
        cs = np.clip(cg - 8, 0, 48)
        KR, R = np.meshgrid(krg, rg, indexing="ij")
        KC, C = np.meshgrid(kcg, cg, indexing="ij")
        RS = np.broadcast_to(rs[None, :], KR.shape)
        CS = np.broadcast_to(cs[None, :], KR.shape)
        valid = (KR >= RS) & (KR < RS + 8) & (KC >= CS) & (KC < CS + 16)
        ro = np.clip(KR - R + 7, 0, 14)
        co = np.clip(KC - C + 15, 0, 30)
        idx[:, ti, :] = np.where(valid, ro * 31 + co, MASK)
    tab = ext[:, idx]
    return np.ascontiguousarray(np.transpose(tab, (1, 0, 2, 3))).astype(np.float32)


def _colT(v, n):
    return np.ascontiguousarray(v.reshape(n, 128).T)


def make_in_maps(x, c, ctx, c_ctx, w_mod, b_mod, g_norm1, w_in, rpb, conv_w, conv_b, ln_g, ln_b,
                 w_out, g_norm2, w_up, ffn_conv_w, ffn_conv_b, w_down, g_final):
    f = np.float32
    maps = []
    ident = np.eye(128, dtype=f)
    shared = {
        "w_mod": np.ascontiguousarray(w_mod[0], f), "bmodT": _colT(b_mod[0], 48),
        "bmodR": np.ascontiguousarray(b_mod[0].reshape(1, -1), f),
        "g1T": _colT(g_norm1[0], 8), "g2T": _colT(g_norm2[0], 8),
        "w_in": np.ascontiguousarray(w_in[0], f), "cbT": _colT(conv_b[0], 4),
        "lngT": _colT(ln_g[0], 4), "lnbT": _colT(ln_b[0], 4),
        "w_out": np.ascontiguousarray(w_out[0], f), "w_up": np.ascontiguousarray(w_up[0], f),
        "fcbT": _colT(ffn_conv_b[0], 44), "w_down": np.ascontiguousarray(w_down[0], f),
        "g_final": np.ascontiguousarray(g_final, f), "ident": ident,
    }
    for core in range(8):
        b, half = core // 2, core % 2
        xs = x[b] if half == 0 else x[b, ::-1]
        cw = conv_w[0] if half == 0 else conv_w[0, ::-1]
        fw = ffn_conv_w[0] if half == 0 else ffn_conv_w[0, ::-1]
        m = dict(shared)
        m["x"] = np.ascontiguousarray(xs[:TOK], f)
        m["cT"] = np.ascontiguousarray(np.stack([_colT(c[b], 8), _colT(c_ctx, 8)], axis=2), f)
        m["ctx"] = np.ascontiguousarray(ctx[b], f)
        m["btab"] = _bias_tables(rpb[0], half)
        m["cwT"] = np.ascontiguousarray(np.transpose(cw.reshape(31, 4, 128), (2, 1, 0)), f)
        m["fcwT"] = np.ascontiguousarray(np.transpose(fw.reshape(3, 44, 128), (2, 1, 0)), f)
        maps.append(m)
    return maps


_NC_CACHE = {}


def kernel(**inputs):
    inputs = {k: np.asarray(v) for k, v in inputs.items()}
    maps = make_in_maps(**inputs)
    if "nc" not in _NC_CACHE:
        _NC_CACHE["nc"] = build()
    res = run_bass_kernel_spmd(_NC_CACHE["nc"], maps, core_ids=list(range(8)))
    out = np.zeros((4, 4096, D), np.float32)
    for core in range(8):
        b, half = core // 2, core % 2
        o = np.asarray(res.results[core]["out"], np.float32)
        if half == 0:
            out[b, :2048] = o
        else:
            out[b, 2048:] = o[::-1]
    return out
```

```python
import numpy as np
from contextlib import ExitStack
import concourse.bass as bass
import concourse.mybir as mybir
from concourse.bass_utils import run_bass_kernel_spmd

F32 = mybir.dt.float32
BF16 = mybir.dt.bfloat16
AF = mybir.ActivationFunctionType
ALU = mybir.AluOpType

D = 1024
NT = 19
NOWN = 16
NQ = 17
TOK = NT * 128
NQT = NQ * 128
DFF = 2816
NFC = 22
EPS = 1e-6
FF_GROUPS = [4, 4, 4, 4, 3, 3]
FF_BLOCKS = [(0, 410), (410, 410), (820, 410), (1230, 410), (1640, 408)]
NTAB = 13
DBGW = 4096


class Buf:
    __slots__ = ("w", "r")

    def __init__(self):
        self.w = None
        self.r = {}


class Eng:
    def __init__(self, e, sem, pe=False):
        self.e = e
        self.sem = sem
        self.n = 0
        self.known = {}
        self.pe = pe


class DSem:
    def __init__(self, sem):
        self.sem = sem
        self.n = 0


class KB:
    def wait(self, eng, sem, val):
        if val <= 0 or eng.known.get(id(sem), 0) >= val:
            return
        eng.e.wait_ge(sem, val)
        eng.known[id(sem)] = val

    def deps(self, eng, reads, writes):
        need = {}

        def add(tok):
            if tok is None:
                return
            s, v = tok
            k = id(s)
            if k not in need or need[k][1] < v:
                need[k] = (s, v)

        for b in reads:
            add(b.w)
        for b in writes:
            add(b.w)
            for tok in b.r.values():
                add(tok)
        for s, v in need.values():
            if eng.pe and s is eng.sem:
                continue
            self.wait(eng, s, v)

    def op(self, eng, fn, reads=(), writes=(), mark=True):
        self.deps(eng, reads, writes)
        ins = fn()
        if mark:
            eng.n += 1
            ins.then_inc(eng.sem, 1)
            tok = (eng.sem, eng.n)
        else:
            tok = (eng.sem, eng.n + 1)
        for b in reads:
            b.r[id(eng.sem)] = tok
        for b in writes:
            b.w = tok
            b.r = {}
        return ins

    def dma(self, q, ds, out, in_, reads=(), writes=(), accum_op=None):
        self.deps(q, reads, writes)
        if accum_op is None:
            ins = q.e.dma_start(out=out, in_=in_)
        else:
            ins = q.e.dma_start(out=out, in_=in_, accum_op=accum_op)
        ds.n += 16
        ins.then_inc(ds.sem, 16)
        tok = (ds.sem, ds.n)
        for b in reads:
            b.r[id(ds.sem)] = tok
        for b in writes:
            b.w = tok
            b.r = {}
        return ins

    def barrier(self, engs, dsems):
        for a in engs:
            for b in engs:
                if a is not b:
                    self.wait(a, b.sem, b.n)
            for d in dsems:
                self.wait(a, d.sem, d.n)


def build(stop_after=99, dbg=False):
    nc = bass.Bass("TRN2", target_bir_lowering=False)

    def din(name, shape):
        return nc.dram_tensor(name, list(shape), F32, kind="ExternalInput").ap()

    x_d = din("x", [TOK, D])
    c_d = din("cT", [128, 8, 2])
    ctx_d = din("ctx", [256, D])
    wmod_d = din("w_mod", [D, 6 * D])
    bmodT_d = din("bmodT", [128, 48])
    bmodR_d = din("bmodR", [1, 6 * D])
    g1T_d = din("g1T", [128, 8])
    g2T_d = din("g2T", [128, 8])
    win_d = din("w_in", [D, 2560])
    btab_d = din("btab", [128, 8, NTAB, 128])
    cw_d = din("cwT", [128, 4, 31])
    cb_d = din("cbT", [128, 4])
    lng_d = din("lngT", [128, 4])
    lnb_d = din("lnbT", [128, 4])
    wout_d = din("w_out", [D, D])
    wup_d = din("w_up", [D, 2 * DFF])
    fcw_d = din("fcwT", [128, 44, 3])
    fcb_d = din("fcbT", [128, 44])
    wdn_d = din("w_down", [DFF, D])
    gfin_d = din("g_final", [D])
    ident_d = din("ident", [128, 128])
    out_d = nc.dram_tensor("out", [NOWN * 128, D], F32, kind="ExternalOutput").ap()
    dbg_d = nc.dram_tensor("dbg", [128, DBGW], F32, kind="ExternalOutput").ap() if dbg else None

    kb = KB()
    with ExitStack() as top:
        def sem(name):
            return top.enter_context(nc.semaphore(name))

        PE = Eng(nc.tensor, sem("s_pe"), pe=True)
        ACT = Eng(nc.scalar, sem("s_act"))
        DVE = Eng(nc.vector, sem("s_dve"))
        POOL = Eng(nc.gpsimd, sem("s_pool"))
        SP = Eng(nc.sync, sem("s_sp"))
        ENGS = [PE, ACT, DVE, POOL, SP]
        DSEMS = []

        def dsem(name):
            d = DSem(sem(name))
            DSEMS.append(d)
            return d

        d_c = dsem("d_c")
        d_id = dsem("d_id")
        d_g = dsem("d_g")
        d_gtr = dsem("d_gtr")
        d_bst = dsem("d_bst")
        d_wm = [dsem(f"d_wm{i}") for i in range(2)]
        d_win = [dsem(f"d_win{i}") for i in range(4)]
        d_wout = dsem("d_wout")
        d_xt = [dsem(f"d_xt{i}") for i in range(2)]
        d_xr = [dsem(f"d_xr{i}") for i in range(3)]
        d_xh = dsem("d_xh")
        d_wg = [dsem(f"d_wg{i}") for i in range(2)]
        d_wv = [dsem(f"d_wv{i}") for i in range(2)]
        d_wd = [dsem(f"d_wd{i}") for i in range(2)]
        d_ot = [dsem(f"d_ot{i}") for i in range(3)]
        d_acc = [dsem(f"d_acc{i}") for i in range(8)]

        class Arena:
            def __init__(self, lo, hi):
                self.lo, self.hi, self.cur = lo, hi, lo

            def reset(self):
                self.cur = self.lo

        uniq = [0]

        def sb(ar, name, shape, dt):
            size = int(np.prod(shape[1:])) * (2 if dt == BF16 else 4)
            off = (ar.cur + 31) // 32 * 32
            assert off + size <= ar.hi, (name, off, size, ar.hi)
            ar.cur = off + size
            uniq[0] += 1
            return nc.alloc_sbuf_tensor_at(f"sb{uniq[0]}_{name}", list(shape), dt, offset=off), Buf()

        SB_LO, SB_HI = 16512, 229344
        C_END = SB_LO + 10752
        Y_END = C_END + 34816
        K_END = Y_END + 78400
        XM_END = Y_END + 65536
        H2_END = XM_END + 32800
        top_ar = Arena(SB_LO, C_END)
        Y_ar = Arena(C_END, Y_END)
        K_ar = Arena(Y_END, K_END)
        B_ar = Arena(K_END, SB_HI)
        XM_ar = Arena(Y_END, XM_END)
        H2_ar = Arena(XM_END, H2_END)
        B2_ar = Arena(H2_END, SB_HI)

        PSBIG = top.enter_context(nc.psum_tensor("psbig", [128, 8 * 512], F32))
        PS = [PSBIG[:, i * 512:(i + 1) * 512] for i in range(8)]
        PSB = [Buf() for _ in range(8)]

        dbg_col = [0]

        def dump(ap, parts, width, reads):
            if not dbg:
                return
            c0 = dbg_col[0]
            assert c0 + width <= DBGW
            kb.dma(POOL, d_g, dbg_d[0:parts, c0:c0 + width], ap, reads=reads)
            dbg_col[0] += width
            return c0

        ident, identB = sb(top_ar, "ident", [128, 128], BF16)
        ones, onesB = sb(top_ar, "ones", [128, 128], BF16)
        A1, A1B = sb(top_ar, "A1", [128, 8, 2], F32)
        SH1, SH1B = sb(top_ar, "SH1", [128, 8, 2], F32)
        A2, A2B = sb(top_ar, "A2", [128, 8], F32)
        SH2, SH2B = sb(top_ar, "SH2", [128, 8], F32)
        gt1b, gt1bB = sb(top_ar, "gt1b", [128, D], F32)
        gt2b, gt2bB = sb(top_ar, "gt2b", [128, D], F32)
        fcw, fcwB = sb(top_ar, "fcw", [128, 44, 3], F32)
        fcb, fcbB = sb(top_ar, "fcb", [128, 44], F32)
        cwT, cwTB = sb(top_ar, "cwT", [128, 4, 31], F32)
        cbT, cbTB = sb(top_ar, "cbT", [128, 4], F32)
        lngT, lngTB = sb(top_ar, "lngT", [128, 4], F32)
        lnbT, lnbTB = sb(top_ar, "lnbT", [128, 4], F32)
        epsT, epsTB = sb(top_ar, "epsT", [128, 1], F32)

        kb.dma(POOL, d_id, ident[:], ident_d, writes=[identB])
        kb.op(DVE, lambda: nc.vector.memset(ones[:], 1.0), writes=[onesB])
        kb.op(DVE, lambda: nc.vector.memset(epsT[:], EPS), writes=[epsTB])
        kb.dma(SP, d_c, fcw[:], fcw_d, writes=[fcwB])
        kb.dma(SP, d_c, fcb[:], fcb_d, writes=[fcbB])
        kb.dma(SP, d_c, cwT[:], cw_d, writes=[cwTB])
        kb.dma(SP, d_c, cbT[:], cb_d, writes=[cbTB])
        kb.dma(SP, d_c, lngT[:], lng_d, writes=[lngTB])
        kb.dma(SP, d_c, lnbT[:], lnb_d, writes=[lnbTB])

        def rstd_from_ssq(eng_unused, ssq_ap, ssqB, out_ap, outB, n, parts=128):
            kb.op(ACT, lambda: nc.scalar.activation(out_ap, ssq_ap, AF.Sqrt, bias=epsT[0:parts, 0:1], scale=1.0 / n),
                  reads=[ssqB, epsTB], writes=[outB])
            kb.op(DVE, lambda: nc.vector.reciprocal(out_ap, out_ap), reads=[outB], writes=[outB])

        kT, kTB = sb(K_ar, "kT", [128, 4, TOK], BF16)
        qT, qTB = sb(K_ar, "qT", [128, 4, NQT], BF16)
        vaug, vaugB = sb(K_ar, "vaug", [128, NT, 8 * 65], BF16)
        UW = 15 + NQT + 1
        UT_ADDR = (K_ar.cur + 31) // 32 * 32
        uT, uTB = sb(K_ar, "uT", [128, 4, UW], BF16)
        kcT, kcTB = sb(K_ar, "kcT", [128, 4, 256], BF16)
        vcaug, vcaugB = sb(K_ar, "vcaug", [128, 2, 8 * 65], BF16)
        kT_t = [Buf() for _ in range(NT)]
        qT_t = [Buf() for _ in range(NQ)]
        v_t = [Buf() for _ in range(NT)]
        uT_t = [Buf() for _ in range(NQ + 1)]

        kb.op(DVE, lambda: nc.vector.memset(vaug[:], 1.0), writes=v_t)
        kb.op(DVE, lambda: nc.vector.memset(vcaug[:], 1.0), writes=[vcaugB])
        kb.op(DVE, lambda: nc.vector.memset(uT[:, :, 0:15], 0.0), writes=[uT_t[0]])

        if True:
            ETOP = 26624 + 3584 + 64
            Bhi_ar = Arena(SB_HI - ETOP, SB_HI)
            B_ar.hi = SB_HI - ETOP
            Etab, EtabB = sb(Bhi_ar, "Etab", [128, 8, NTAB, 128], BF16)
            BST_OFF = (Bhi_ar.cur + 31) // 32 * 32
            bst, bstB = sb(Bhi_ar, "bst", [128, 7, 128], F32)
            etab_items = []
            etab_specs = [(h_, tt0, ntt) for h_ in range(8) for (tt0, ntt) in ((0, 7), (7, 6))]

            def _etab_dma(k):
                h_, tt0, ntt = etab_specs[k]
                kb.dma(POOL, d_bst, bst[:, 0:ntt, :], btab_d[:, h_, tt0:tt0 + ntt, :], writes=[bstB])

            def _etab_exp(k):
                h_, tt0, ntt = etab_specs[k]
                kb.op(ACT, lambda: nc.scalar.activation(Etab[:, h_, tt0:tt0 + ntt, :], bst[:, 0:ntt, :], AF.Exp),
                      reads=[bstB], writes=[EtabB])

            for k_ in range(len(etab_specs) + 1):
                def _item(k_=k_):
                    if k_ >= 1:
                        _etab_exp(k_ - 1)
                    if k_ < len(etab_specs):
                        _etab_dma(k_)
                etab_items.append(_item)
            win, winB = sb(B_ar, "win", [128, 8, 2560], BF16)
            wm = [sb(Y_ar, f"wm{i}", [128, 8, 256], BF16) for i in range(2)]
            cc, ccB = sb(B_ar, "cc", [128, 8, 2], F32)
            ccb, ccbB = sb(B_ar, "ccb", [128, 8, 2], BF16)
            bmT, bmTB = sb(B_ar, "bmT", [128, 48], F32)
            g1T, g1TB = sb(B_ar, "g1T", [128, 8], F32)
            g2T, g2TB = sb(B_ar, "g2T", [128, 8], F32)
            modT, modTB = sb(B_ar, "modT", [128, 48, 2], F32)
            gtr, gtrB = sb(B_ar, "gtr", [1, 256], F32)
            gtrb, gtrbB = sb(B_ar, "gtrb", [1, 256], BF16)
            gtlo, gtloB = sb(B_ar, "gtlo", [1, 256], F32)
            gtlob, gtlobB = sb(B_ar, "gtlob", [1, 256], BF16)
            xt = [sb(Y_ar, f"xt{i}", [128, D], F32) for i in range(2)]
            xn = [sb(B_ar, f"xn{i}", [128, D], BF16) for i in range(2)]
            sqj, sqjB = sb(B_ar, "sqj", [128, D], BF16)
            ssq = [sb(B_ar, f"ssq{i}", [128, 1], F32) for i in range(2)]
            rst = [sb(B_ar, f"rst{i}", [128, 1], F32) for i in range(2)]
            hT = [sb(Y_ar, f"hT{i}", [128, 8, 512], BF16) for i in range(2)]
            sg = [sb(B_ar, f"sg{i}", [128, 512], F32) for i in range(2)]

            kb.dma(SP, d_c, cc[:], c_d, writes=[ccB])
            kb.dma(SP, d_c, bmT[:], bmodT_d, writes=[bmTB])
            kb.dma(SP, d_c, g1T[:], g1T_d, writes=[g1TB])
            kb.dma(SP, d_c, g2T[:], g2T_d, writes=[g2TB])
            for bb in (fcwB, fcbB, cwTB, cbTB, lngTB, lnbTB, ccB, bmTB, g1TB, g2TB):
                bb.w = (d_c.sem, d_c.n)
            wmod_v = wmod_d.rearrange("(kc p) n -> p kc n", p=128)
            win_v = win_d.rearrange("(kc p) n -> p kc n", p=128)

            def load_wm(blk):
                t, tb = wm[blk % 2]
                kb.dma(POOL, d_wm[blk % 2], t[:], wmod_v[:, :, blk * 256:(blk + 1) * 256], writes=[tb])

            load_wm(0)
            load_wm(1)
            kb.op(ACT, lambda: nc.scalar.activation(ccb[:], cc[:], AF.Silu), reads=[ccB], writes=[ccbB])

            mT_ps, mT_B = PS[0], PSB[0]
            row_ps = [(PS[1], PSB[1]), (PS[2], PSB[2])]
            bcast_q = []

            def mod_block(blk):
                while bcast_q:
                    bcast_q.pop(0)()
                t, tb = wm[blk % 2]
                for sub in range(2):
                    ch = blk * 2 + sub
                    for kc in range(8):
                        kb.op(PE, lambda: nc.tensor.matmul(
                            mT_ps[:, ch * 2:ch * 2 + 2], t[:, kc, sub * 128:(sub + 1) * 128], ccb[:, kc, :],
                            start=(kc == 0), stop=(kc == 7)),
                            reads=[tb, ccbB], writes=[mT_B], mark=(kc == 7))
                if blk in (8, 9, 10, 11, 20, 21, 22, 23):
                    rp, rpB = row_ps[blk % 2]
                    for kc in range(8):
                        kb.op(PE, lambda: nc.tensor.matmul(
                            rp[0:2, 0:256], ccb[:, kc, :], t[:, kc, :], start=(kc == 0), stop=(kc == 7)),
                            reads=[tb, ccbB], writes=[rpB], mark=(kc == 7))
                    kb.dma(SP, d_gtr, gtr[:], bmodR_d[0:1, blk * 256:(blk + 1) * 256], writes=[gtrB])
                    kb.op(DVE, lambda: nc.vector.tensor_tensor(gtr[0:1, :], rp[0:1, 0:256], gtr[0:1, :], ALU.add),
                          reads=[rpB, gtrB], writes=[gtrB])
                    kb.op(DVE, lambda: nc.vector.tensor_copy(gtrb[:], gtr[:]), reads=[gtrB], writes=[gtrbB])
                    kb.op(DVE, lambda: nc.vector.tensor_tensor(gtlo[:], gtr[:], gtrb[:], ALU.subtract),
                          reads=[gtrB, gtrbB], writes=[gtloB])
                    kb.op(DVE, lambda: nc.vector.tensor_copy(gtlob[:], gtlo[:]), reads=[gtloB], writes=[gtlobB])

                    def bcast_part(blk=blk):
                        bp, bpB = PS[3 + (blk % 2)], PSB[3 + (blk % 2)]
                        kb.op(PE, lambda: nc.tensor.matmul(bp[:, 0:256], ones[0:1, :], gtrb[0:1, :], start=True, stop=False),
                              reads=[onesB, gtrbB], writes=[bpB], mark=False)
                        kb.op(PE, lambda: nc.tensor.matmul(bp[:, 0:256], ones[0:1, :], gtlob[0:1, :], start=False, stop=True),
                              reads=[onesB, gtlobB], writes=[bpB])
                        dst, dstB = (gt1b, gt1bB) if blk < 12 else (gt2b, gt2bB)
                        o = (blk % 4) * 256
                        kb.op(ACT, lambda: nc.scalar.copy(dst[:, o:o + 256], bp[:, 0:256]),
                              reads=[bpB], writes=[dstB])
                    bcast_q.append(bcast_part)
                if blk + 2 < 24:
                    load_wm(blk + 2)
            for blk in range(8):
                mod_block(blk)
            win_k, win_q, win_ag, win_vb = Buf(), Buf(), Buf(), Buf()
            for (c_lo, c_hi, bb, dd) in ((512, 1024, win_k, d_win[0]), (0, 512, win_q, d_win[1]),
                                         (1536, 2560, win_ag, d_win[2]), (1024, 1536, win_vb, d_win[3])):
                kb.dma(POOL, dd, win[:, :, c_lo:c_hi], win_v[:, :, c_lo:c_hi], writes=[bb])
            kb.op(DVE, lambda: nc.vector.tensor_tensor(
                modT[:, 0:16, :], mT_ps[:, 0:32].rearrange("p (c t) -> p c t", t=2),
                bmT[:, 0:16].unsqueeze(2).broadcast_to([128, 16, 2]), ALU.add),
                reads=[mT_B, bmTB], writes=[modTB])
            kb.op(DVE, lambda: nc.vector.scalar_tensor_tensor(
                A1[:], modT[:, 8:16, :], 1.0, g1T[:].unsqueeze(2).broadcast_to([128, 8, 2]), ALU.add, ALU.mult),
                reads=[modTB, g1TB], writes=[A1B])
            kb.op(DVE, lambda: nc.vector.tensor_copy(SH1[:], modT[:, 0:8, :]), reads=[modTB], writes=[SH1B])

            def mod_finish():
                kb.op(DVE, lambda: nc.vector.tensor_tensor(
                    modT[:, 16:48, :], mT_ps[:, 32:96].rearrange("p (c t) -> p c t", t=2),
                    bmT[:, 16:48].unsqueeze(2).broadcast_to([128, 32, 2]), ALU.add),
                    reads=[mT_B, bmTB], writes=[modTB])
                kb.op(DVE, lambda: nc.vector.scalar_tensor_tensor(
                    A2[:], modT[:, 32:40, 0], 1.0, g2T[:], ALU.add, ALU.mult),
                    reads=[modTB, g2TB], writes=[A2B])
                kb.op(DVE, lambda: nc.vector.tensor_copy(SH2[:], modT[:, 24:32, 0]), reads=[modTB], writes=[SH2B])
                if dbg:
                    dump(modT[:].rearrange("p c t -> p (c t)"), 128, 96, [modTB])
                    dump(gt1b[:, 0:64], 128, 64, [gt1bB])
                    dump(gt2b[:, 0:64], 128, 64, [gt2bB])

            tp_banks = [6, 7]
            state = {"i": 0}

            def front(src_ap, hT_t, hT_b, col0, var):
                i = state["i"]
                state["i"] += 1
                x_t, x_b = xt[i % 2]
                xn_t, xn_b = xn[i % 2]
                ss_t, ss_b = ssq[i % 2]
                rs_t, rs_b = rst[i % 2]
                pb = tp_banks[i % 2]
                tps = PS[pb].bitcast(BF16)

                def p1():
                    kb.dma(SP, d_xt[i % 2], x_t[:], src_ap, writes=[x_b])
                    kb.op(ACT, lambda: nc.scalar.activation(sqj[:], x_t[:], AF.Square, accum_out=ss_t[:]),
                          reads=[x_b], writes=[sqjB, ss_b])
                    rstd_from_ssq(None, ss_t[:], ss_b, rs_t[:], rs_b, D)
                    kb.op(DVE, lambda: nc.vector.tensor_scalar(xn_t[:], x_t[:], rs_t[:, 0:1], None, ALU.mult),
                          reads=[x_b, rs_b], writes=[xn_b])

                def p2():
                    for kc in range(8):
                        kb.op(PE, lambda: nc.tensor.transpose(tps[:, kc * 128:(kc + 1) * 128],
                                                              xn_t[:, kc * 128:(kc + 1) * 128], ident[:]),
                              reads=[xn_b, identB], writes=[PSB[pb]], mark=(kc == 7))
                    hv = hT_t[:, :, col0:col0 + 128]
                    kb.op(DVE, lambda: nc.vector.tensor_tensor(
                        hv, tps[:, 0:1024].rearrange("p (c t) -> p c t", t=128),
                        A1[:, :, var:var + 1].broadcast_to([128, 8, 128]), ALU.mult),
                        reads=[PSB[pb], A1B], writes=[hT_b])
                    kb.op(DVE, lambda: nc.vector.tensor_tensor(
                        hv, hv, SH1[:, :, var:var + 1].broadcast_to([128, 8, 128]), ALU.add),
                        reads=[SH1B], writes=[hT_b])
                return p1, p2

            groups = [(0, 2), (2, 4), (6, 4), (10, 4), (14, 3), (17, 2)]
            bank_rr = [0]

            def nb():
                b = 1 + bank_rr[0] % 5
                bank_rr[0] += 1
                return b

            def group_front(gi):
                t0, ntl = groups[gi]
                h_t, h_b = hT[(gi + 1) % 2]
                parts = [front(x_d[(t0 + j) * 128:(t0 + j + 1) * 128, :], h_t, h_b, j * 128, 0)
                         for j in range(ntl)]
                items = [parts[0][0]]
                for j in range(ntl):
                    nxt_p1 = parts[j + 1][0] if j + 1 < ntl else (lambda: None)
                    items.append((lambda a, b: (lambda: (a(), b())))(parts[j][1], nxt_p1))
                return items

            for it in group_front(0):
                it()
            mod_next = [8]
            defer = []

            def tick():
                if defer:
                    defer.pop(0)()

            def mod_some():
                if mod_next[0] < 24:
                    mod_block(mod_next[0])
                    mod_next[0] += 1

            for gi, (t0, ntl) in enumerate(groups if stop_after >= 1 else []):
                h_t, h_b = hT[(gi + 1) % 2]
                defer = []
                if gi + 1 < len(groups):
                    gf = group_front(gi + 1)
                    for it in gf:
                        defer.append(it)
                        defer.append(mod_some)
                else:
                    hc_t, hc_b = hT[(len(groups) - 1) % 2]
                    cparts = [front(ctx_d[t * 128:(t + 1) * 128, :], hc_t, hc_b, t * 128, 1) for t in range(2)]
                    citems = [cparts[0][0], (lambda: (cparts[0][1](), cparts[1][0]())), cparts[1][1]]
                    for it in citems:
                        defer.append(it)
                        defer.append(mod_some)
                    defer += [mod_some] * 5
                if gi >= 1:
                    nd = len(defer)
                    for q_ in range(4):
                        if etab_items:
                            defer.insert(min(len(defer), 2 + q_ * (nd // 4 + 1) + q_), etab_items.pop(0))
                tick()
                N = ntl * 128
                nq = max(0, min(N, (NQ - t0) * 128))
                c0 = t0 * 128
                for ch in range(4):
                    pb = nb()
                    for kc in range(8):
                        kb.op(PE, lambda: nc.tensor.matmul(
                            PS[pb][:, 0:N], win[:, kc, 512 + ch * 128:512 + (ch + 1) * 128], h_t[:, kc, 0:N],
                            start=(kc == 0), stop=(kc == 7)),
                            reads=[win_k, h_b], writes=[PSB[pb]], mark=(kc == 7))
                    kb.op(DVE, lambda: nc.vector.tensor_copy(kT[:, ch, c0:c0 + N], PS[pb][:, 0:N]),
                          reads=[PSB[pb]], writes=kT_t[t0:t0 + ntl])
                    tick()
                for ch in range(4 if nq > 0 else 0):
                    pb = nb()
                    for kc in range(8):
                        kb.op(PE, lambda: nc.tensor.matmul(
                            PS[pb][:, 0:nq], win[:, kc, ch * 128:(ch + 1) * 128], h_t[:, kc, 0:nq],
                            start=(kc == 0), stop=(kc == 7)),
                            reads=[win_q, h_b], writes=[PSB[pb]], mark=(kc == 7))
                    kb.op(ACT, lambda: nc.scalar.mul(qT[:, ch, c0:c0 + nq], PS[pb][:, 0:nq], 0.125),
                          reads=[PSB[pb]], writes=qT_t[t0:t0 + nq // 128])
                    tick()
                for ch in range(4 if nq > 0 else 0):
                    pa_, pg_ = nb(), nb()
                    for kc in range(8):
                        kb.op(PE, lambda: nc.tensor.matmul(
                            PS[pg_][:, 0:nq], win[:, kc, 2048 + ch * 128:2048 + (ch + 1) * 128], h_t[:, kc, 0:nq],
                            start=(kc == 0), stop=(kc == 7)),
                            reads=[win_ag, h_b], writes=[PSB[pg_]], mark=(kc == 7))
                    for kc in range(8):
                        kb.op(PE, lambda: nc.tensor.matmul(
                            PS[pa_][:, 0:nq], win[:, kc, 1536 + ch * 128:1536 + (ch + 1) * 128], h_t[:, kc, 0:nq],
                            start=(kc == 0), stop=(kc == 7)),
                            reads=[win_ag, h_b], writes=[PSB[pa_]], mark=(kc == 7))
                    s_t, s_b = sg[ch % 2]
                    kb.op(ACT, lambda: nc.scalar.activation(s_t[:, 0:nq], PS[pg_][:, 0:nq], AF.Sigmoid),
                          reads=[PSB[pg_]], writes=[s_b])
                    kb.op(DVE, lambda: nc.vector.tensor_tensor(
                        uT[:, ch, 15 + c0:15 + c0 + nq], PS[pa_][:, 0:nq], s_t[:, 0:nq], ALU.mult),
                        reads=[PSB[pa_], s_b], writes=uT_t[1 + t0:1 + t0 + nq // 128])
                for j in range(ntl):
                    pb = nb()
                    for kc in range(8):
                        kb.op(PE, lambda: nc.tensor.matmul(
                            PS[pb][:, :], h_t[:, kc, j * 128:(j + 1) * 128], win[:, kc, 1024:1536],
                            start=(kc == 0), stop=(kc == 7)),
                            reads=[win_vb, h_b], writes=[PSB[pb]], mark=(kc == 7))
                    kb.op(ACT, lambda: nc.scalar.copy(
                        vaug[:, t0 + j, :].rearrange("p (h e) -> p h e", e=65)[:, :, 0:64],
                        PS[pb][:, :].rearrange("p (h e) -> p h e", e=64)),
                        reads=[PSB[pb]], writes=[v_t[t0 + j]])
                    tick()
                while defer:
                    tick()

            for ch in range(4):
                pb = 1 + ch
                for kc in range(8):
                    kb.op(PE, lambda: nc.tensor.matmul(
                        PS[pb][:, 0:256], win[:, kc, 512 + ch * 128:512 + (ch + 1) * 128], hc_t[:, kc, 0:256],
                        start=(kc == 0), stop=(kc == 7)),
                        reads=[win_k, hc_b], writes=[PSB[pb]], mark=(kc == 7))
                kb.op(DVE, lambda: nc.vector.tensor_copy(kcT[:, ch, :], PS[pb][:, 0:256]),
                      reads=[PSB[pb]], writes=[kcTB])
            for t in range(2):
                pb = 1 + t
                for kc in range(8):
                    kb.op(PE, lambda: nc.tensor.matmul(
                        PS[pb][:, :], hc_t[:, kc, t * 128:(t + 1) * 128], win[:, kc, 1024:1536],
                        start=(kc == 0), stop=(kc == 7)),
                        reads=[win_vb, hc_b], writes=[PSB[pb]], mark=(kc == 7))
                kb.op(DVE, lambda: nc.vector.tensor_copy(
                    vcaug[:, t, :].rearrange("p (h e) -> p h e", e=65)[:, :, 0:64],
                    PS[pb][:, :].rearrange("p (h e) -> p h e", e=64)),
                    reads=[PSB[pb]], writes=[vcaugB])

            if dbg:
                dump(kcT[:, 0, 0:64], 128, 64, [kcTB])
                dump(vcaug[:, 0, 0:130], 128, 130, [vcaugB])

            while mod_next[0] < 24:
                mod_block(mod_next[0])
                mod_next[0] += 1
            while bcast_q:
                bcast_q.pop(0)()
            mod_finish()
            if dbg and stop_after >= 1:
                dump(kT[:, 1, 0:256], 128, 256, kT_t[0:2])
                dump(qT[:, 2, 128:384], 128, 256, qT_t[1:3])
                dump(uT[:, 3, 0:256], 128, 256, uT_t[0:3])
                dump(vaug[:, 5, 0:260], 128, 260, [v_t[5]])
                dump(kT[:, 3, TOK - 128:TOK], 128, 128, [kT_t[NT - 1]])

            kb.barrier(ENGS, DSEMS)

        if stop_after >= 2:
            B_ar.reset()
            Y_ar.reset()
            yT, _ = sb(Y_ar, "yT", [128, 8, NQT], BF16)
            yT_t = [Buf() for _ in range(NQ)]
            while etab_items:
                etab_items.pop(0)()
            pT = [sb(B_ar, f"pT{i}", [128, 7, 128], BF16) for i in range(4)]
            on = [sb(B_ar, f"on{i}", [128, 512], BF16) for i in range(2)]
            rec, recB = sb(B_ar, "rec", [128, 8], F32)
            CN = 256
            cvS, cvbS, sqbS = [], [], []
            for i_ in range(2):
                cvS.append((sb(B_ar, f"cv{i_}", [128, 4, CN], F32)[0], [Buf() for _ in range(4)]))
                cvbS.append((sb(B_ar, f"cvb{i_}", [128, 4, CN], BF16)[0], [Buf() for _ in range(4)]))
                sqbS.append((sb(B_ar, f"sqb{i_}", [128, 4, CN], BF16)[0], [Buf() for _ in range(4)]))
            rsd, rsdB = sb(B_ar, "rsd", [128, CN], F32)
            T_ar = Arena(BST_OFF, BST_OFF + 3584)
            mean, meanB = sb(T_ar, "mean", [128, CN], F32)
            tm = [sb(T_ar, f"tm{i}", [128, CN], F32) for i in range(2)]
            for bb_ in (meanB, tm[0][1], tm[1][1]):
                bb_.r = {id(e_.sem): (e_.sem, e_.n) for e_ in ENGS}
                for d_ in DSEMS:
                    bb_.r[id(d_.sem)] = (d_.sem, d_.n)
            DG_OFF = (B_ar.cur + 31) // 32 * 32
            assert DG_OFF >= H2_END, (DG_OFF, H2_END)
            dg, _ = sb(B_ar, "dg", [128, 4, 31, 128], BF16)
            dgB = [Buf() for _ in range(4)]
            for c in range(4):
                kb.op(DVE, lambda: nc.vector.tensor_tensor(
                    dg[:, c, :, :], ident[:].unsqueeze(1).broadcast_to([128, 31, 128]),
                    cwT[:, c, :].unsqueeze(2).broadcast_to([128, 31, 128]), ALU.mult),
                    reads=[identB, cwTB], writes=[dgB[c]])

            ST_PAIRS = [(0, 1), (2, 3), (6, 7)]

            def tile_info(qt):
                kts = list(range(max(0, qt - 2), max(qt + 2, 3) + 1))
                ti0 = 0 if qt == 0 else (4 if qt == 1 else 8)
                return kts, len(kts), ti0

            def qk(qt, h, n):
                kts, nl, ti0 = tile_info(qt)
                hc, hp = h // 2, (h % 2) * 64
                bA, bB = ST_PAIRS[n % 3]
                p_t, p_b = pT[n % 4]
                q_ap = (qT if h % 2 == 0 else qB)[:, hc, qt * 128:(qt + 1) * 128]
                for j in range(nl + 2):
                    bank = bA if j < 4 else bB
                    col = (j % 4) * 128
                    if j < nl:
                        l_ap = kT[:, hc, kts[j] * 128:(kts[j] + 1) * 128]
                        lb = kT_t[kts[j]]
                    else:
                        l_ap = kcT[:, hc, (j - nl) * 128:(j - nl + 1) * 128]
                        lb = kcTB
                    kb.op(PE, lambda: nc.tensor.matmul(PS[bank][:, col:col + 128], l_ap, q_ap,
                                                       start=True, stop=True),
                          reads=[lb, qT_t[qt], qB_t[qt]], writes=[PSB[bank]], mark=(j == 3 or j == nl + 1))
                nt_ = nl + 2
                assert bB == bA + 1
                kb.op(ACT, lambda: nc.scalar.activation(
                    p_t[:, 0:nt_, :].rearrange("p a b -> p (a b)"),
                    PSBIG[:, bA * 512:bA * 512 + nt_ * 128], AF.Exp),
                    reads=[PSB[bA], PSB[bB]], writes=[p_b])
                kb.op(DVE, lambda: nc.vector.tensor_tensor(
                    p_t[:, 0:nl, :], p_t[:, 0:nl, :], Etab[:, h, ti0:ti0 + nl, :], ALU.mult),
                    reads=[p_b, EtabB], writes=[p_b])

            def pv(qt, h, n):
                kts, nl, ti0 = tile_info(qt)
                p_t, p_b = pT[n % 4]
                ob, oc = 4 + h // 4, (h % 4) * 65
                for j in range(nl + 2):
                    if j < nl:
                        r_ap, rb = vaug[:, kts[j], h * 65:(h + 1) * 65], v_t[kts[j]]
                    else:
                        r_ap, rb = vcaug[:, j - nl, h * 65:(h + 1) * 65], vcaugB
                    kb.op(PE, lambda: nc.tensor.matmul(PS[ob][:, oc:oc + 65], p_t[:, j, :], r_ap,
                                                       start=(j == 0), stop=(j == nl + 1)),
                          reads=[p_b, rb], writes=[PSB[ob]], mark=(j == nl + 1))

            def normalize(qt, halves=(0, 1)):
                o_t, o_b = on[qt % 2]
                for half in halves:
                    ob = 4 + half
                    ov = PS[ob][:, 0:260].rearrange("p (h e) -> p h e", e=65)
                    kb.op(DVE, lambda: nc.vector.reciprocal(
                        rec[:, half * 4:(half + 1) * 4].unsqueeze(2), ov[:, :, 64:65]),
                        reads=[PSB[ob]], writes=[recB])
                    kb.op(DVE, lambda: nc.vector.tensor_tensor(
                        o_t[:, half * 256:(half + 1) * 256].rearrange("p (h e) -> p h e", e=64), ov[:, :, 0:64],
                        rec[:, half * 4:(half + 1) * 4].unsqueeze(2).broadcast_to([128, 4, 64]), ALU.mult),
                        reads=[PSB[ob], recB], writes=[o_b])

            def finish(qt):
                o_t, o_b = on[qt % 2]
                tps = PS[5].bitcast(BF16)
                for c in range(4):
                    kb.op(PE, lambda: nc.tensor.transpose(tps[:, c * 128:(c + 1) * 128],
                                                          o_t[:, c * 128:(c + 1) * 128], ident[:]),
                          reads=[o_b, identB], writes=[PSB[5]], mark=(c == 3))
                kb.op(DVE, lambda: nc.vector.tensor_copy(
                    yT[:, 0:4, qt * 128:(qt + 1) * 128], tps[:, 0:512].rearrange("p (c t) -> p c t", t=128)),
                    reads=[PSB[5]], writes=[yT_t[qt]])

            def attention_pass(mid_hook):
                units = [(qt, h) for qt in range(NQ) for h in range(8)]
                qprep(0)
                qprep(1)
                qk(units[0][0], units[0][1], 0)
                qk(units[1][0], units[1][1], 1)
                def pv_unit(m):
                    qt_, h_ = units[m]
                    pv(qt_, h_, m)
                    if h_ == 3:
                        normalize(qt_, (0,))
                    if h_ == 7:
                        normalize(qt_, (1,))
                    if h_ == 2 and qt_ > 0:
                        finish(qt_ - 1)

                for n, (qt, h) in enumerate(units):
                    if n + 2 < len(units):
                        if h == 0 and qt + 2 < NQ:
                            qprep(qt + 2)
                        qk(units[n + 2][0], units[n + 2][1], n + 2)
                    if n >= 1:
                        pv_unit(n - 1)
                    if 40 <= n < 72:
                        mid_hook(n - 40)
                pv_unit(len(units) - 1)
                finish(NQ - 1)

            CB = [0, 1, 4, 5]

            def conv_block(tok0, n, si):
                tl0 = max(tok0 - 15, 0) // 128
                tl1 = min((tok0 + n + 14) // 128, NQ - 1)
                ub = [uT_t[0]] + uT_t[1 + tl0:2 + tl1]
                cv, cvB = cvS[si]
                cvb, cvbB = cvbS[si]
                sqb, sqbB = sqbS[si]

                def part_a(c):
                    for k in range(31):
                        kb.op(PE, lambda: nc.tensor.matmul(PS[CB[c]][:, 0:n], dg[:, c, k, :],
                                                           uT[:, c, tok0 + k:tok0 + k + n],
                                                           start=(k == 0), stop=(k == 30)),
                              reads=[dgB[c]] + ub, writes=[PSB[CB[c]]], mark=(k == 30))
                    kb.op(ACT, lambda: nc.scalar.activation(cv[:, c, 0:n], PS[CB[c]][:, 0:n], AF.Identity,
                                                            bias=cbT[:, c:c + 1]),
                          reads=[PSB[CB[c]], cbTB], writes=[cvB[c]])
                    kb.op(ACT, lambda: nc.scalar.activation(sqb[:, c, 0:n], PS[CB[c]][:, 0:n], AF.Square,
                                                            bias=cbT[:, c:c + 1]),
                          reads=[PSB[CB[c]], cbTB], writes=[sqbB[c]])
                    kb.op(DVE, lambda: nc.vector.tensor_copy(cvb[:, c, 0:n], cv[:, c, 0:n]),
                          reads=[cvB[c]], writes=[cvbB[c]])

                def stats():
                    for c in range(4):
                        kb.op(PE, lambda: nc.tensor.matmul(PS[2][:, 0:n], ones[:], cvb[:, c, 0:n],
                                                           start=(c == 0), stop=(c == 3)),
                              reads=[onesB, cvbB[c]], writes=[PSB[2]], mark=(c == 3))
                    for c in range(4):
                        kb.op(PE, lambda: nc.tensor.matmul(PS[3][:, 0:n], ones[:], sqb[:, c, 0:n],
                                                           start=(c == 0), stop=(c == 3)),
                              reads=[onesB, sqbB[c]], writes=[PSB[3]], mark=(c == 3))

                def part_b():
                    kb.op(DVE, lambda: nc.vector.tensor_scalar(mean[:, 0:n], PS[2][:, 0:n], 1.0 / 512, None, ALU.mult),
                          reads=[PSB[2]], writes=[meanB])
                    kb.op(DVE, lambda: nc.vector.tensor_tensor(rsd[:, 0:n], mean[:, 0:n], mean[:, 0:n], ALU.mult),
                          reads=[meanB], writes=[rsdB])
                    kb.op(DVE, lambda: nc.vector.scalar_tensor_tensor(
                        rsd[:, 0:n], PS[3][:, 0:n], 1.0 / 512, rsd[:, 0:n], ALU.mult, ALU.subtract),
                        reads=[PSB[3], rsdB], writes=[rsdB])
                    kb.op(ACT, lambda: nc.scalar.activation(rsd[:, 0:n], rsd[:, 0:n], AF.Sqrt, bias=epsT[:, 0:1]),
                          reads=[rsdB, epsTB], writes=[rsdB])
                    kb.op(DVE, lambda: nc.vector.reciprocal(rsd[:, 0:n], rsd[:, 0:n]), reads=[rsdB], writes=[rsdB])
                    t0_ = tok0 // 128
                    yb = yT_t[t0_:t0_ + max(1, n // 128)]
                    for c in range(4):
                        t_t, t_b = tm[c % 2]
                        kb.op(DVE, lambda: nc.vector.tensor_tensor(t_t[:, 0:n], cv[:, c, 0:n], mean[:, 0:n], ALU.subtract),
                              reads=[cvB[c], meanB], writes=[t_b])
                        kb.op(DVE, lambda: nc.vector.tensor_tensor(t_t[:, 0:n], t_t[:, 0:n], rsd[:, 0:n], ALU.mult),
                              reads=[t_b, rsdB], writes=[t_b])
                        kb.op(ACT, lambda: nc.scalar.activation(
                            yT[:, 4 + c, tok0:tok0 + n], t_t[:, 0:n], AF.Silu,
                            bias=lnbT[:, c:c + 1], scale=lngT[:, c:c + 1]),
                            reads=[t_b, lngTB, lnbTB], writes=yb)
                return part_a, stats, part_b

            cblocks = [(2048, 16)] + [(blk * 256, 256) for blk in range(8)]
            prev_cb = None
            for bi_, (tk0, nn) in enumerate(cblocks):
                pa_f, st_f, pb_f = conv_block(tk0, nn, bi_ % 2)
                pa_f(0)
                if prev_cb is not None:
                    prev_cb[0]()
                    prev_cb[1]()
                pa_f(1)
                pa_f(2)
                pa_f(3)
                prev_cb = (st_f, pb_f)
            prev_cb[0]()
            prev_cb[1]()

            W_ar = Arena(DG_OFF, DG_OFF + 16384 + 32)
            wout, woutB = sb(W_ar, "wout", [128, 8, D], BF16)
            wout_v = wout_d.rearrange("(kc p) n -> p kc n", p=128)
            for i in range(2):
                kb.dma(POOL, d_wout, wout[:, i * 4:(i + 1) * 4, :], wout_v[:, i * 4:(i + 1) * 4, :],
                       reads=[], writes=dgB)
            woutB.w = (d_wout.sem, d_wout.n)

            def scale_wout(j):
                kc, q4 = j // 4, j % 4
                cs = slice(q4 * 256, (q4 + 1) * 256)
                kb.op(DVE, lambda: nc.vector.tensor_tensor(wout[:, kc, cs], wout[:, kc, cs], gt1b[:, cs], ALU.mult),
                      reads=[woutB, gt1bB], writes=[woutB])

            UT_OFF = None
            qB_ar = Arena(UT_ADDR, UT_ADDR + 4 * UW * 2)
            qB, qBB = sb(qB_ar, "qB", [128, 4, NQT], BF16)
            qB_t = [Buf() for _ in range(NQ)]
            for bb_ in qB_t:
                bb_.r = {id(e_.sem): (e_.sem, e_.n) for e_ in ENGS}

            def qprep(qt):
                cs = slice(qt * 128, (qt + 1) * 128)
                kb.op(POOL, lambda: nc.gpsimd.memset(qB[0:64, :, cs], 0.0), writes=[qB_t[qt]])
                kb.op(DVE, lambda: nc.vector.tensor_copy(qB[64:128, :, cs], qT[64:128, :, cs]),
                      reads=[qT_t[qt]], writes=[qB_t[qt]])
                kb.op(DVE, lambda: nc.vector.memset(qT[64:128, :, cs], 0.0), writes=[qT_t[qt]])
            attention_pass(scale_wout)

            if dbg:
                dump(yT[:, 1, 0:256], 128, 256, yT_t[0:2])
                dump(yT[:, 2, 2048 - 128:2048 + 128], 128, 256, yT_t[15:17])
                dump(yT[:, 5, 0:256], 128, 256, yT_t[0:2])
                dump(yT[:, 7, 2048 - 240:2048 + 16], 128, 256, yT_t[14:17])
            kb.barrier(ENGS, DSEMS)

        if stop_after >= 3:
            GMAX = max(FF_GROUPS)
            W0 = 3 * (GMAX * 128 * 8 * 2) + 96
            B2hi_ar = Arena(SB_HI - W0, SB_HI)
            B2_ar.hi = SB_HI - W0
            wupg, wupv, wdn = [None, None], [None, None], [None, None]
            wupg[0] = sb(B2hi_ar, "wupg0", [128, 8, GMAX * 128], BF16)
            wupv[0] = sb(B2hi_ar, "wupv0", [128, 8, GMAX * 128], BF16)
            wdn[0] = sb(B2hi_ar, "wdn0", [128, GMAX, D], BF16)
            wup_v = wup_d.rearrange("(kc p) n -> p kc n", p=128)
            gstart = [sum(FF_GROUPS[:i]) for i in range(len(FF_GROUPS))]
            wB = [[Buf(), Buf(), Buf()] for _ in range(2)]

            def load_group(gi):
                j0, n = gstart[gi], FF_GROUPS[gi]
                s = gi % 2
                kb.dma(POOL, d_wg[s], wupg[s][0][:, :, 0:n * 128], wup_v[:, :, j0 * 128:(j0 + n) * 128],
                       writes=[wB[s][0]])
                kb.dma(POOL, d_wv[s], wupv[s][0][:, :, 0:n * 128], wup_v[:, :, DFF + j0 * 128:DFF + (j0 + n) * 128],
                       writes=[wB[s][1]])
                kb.dma(POOL, d_wd[s], wdn[s][0][:, 0:n, :],
                       wdn_d[j0 * 128:(j0 + n) * 128, :].rearrange("(c p) n -> p c n", p=128),
                       writes=[wB[s][2]])

            def scale_group(gi):
                n = FF_GROUPS[gi]
                s = gi % 2
                kb.op(DVE, lambda: nc.vector.tensor_tensor(
                    wdn[s][0][:, 0:n, :], wdn[s][0][:, 0:n, :],
                    gt2b[:].unsqueeze(1).broadcast_to([128, n, D]), ALU.mult),
                    reads=[wB[s][2], gt2bB], writes=[wB[s][2]])

            B2_ar.lo = DG_OFF + 16384 + 32
            B2_ar.reset()
            xm, _ = sb(XM_ar, "xm", [128, NOWN, D], F32)
            xm_t = [Buf() for _ in range(NOWN)]
            H2W = 2 + NOWN * 128
            h2T, _ = sb(H2_ar, "h2T", [128, 8, H2W], BF16)
            h2_t = [Buf() for _ in range(NOWN + 2)]
            xr = [sb(B2_ar, f"xr{i}", [128, D], F32) for i in range(3)]
            xn2 = [sb(B2_ar, f"xn2{i}", [128, D], BF16) for i in range(4)]
            ssq2 = [sb(B2_ar, f"ssq2{i}", [128, 1], F32) for i in range(4)]
            rst2 = [sb(B2_ar, f"rst2{i}", [128, 1], F32) for i in range(4)]
            kb.op(DVE, lambda: nc.vector.memset(h2T[:, :, 0:1], 0.0), writes=[h2_t[0]])
            if stop_after >= 4:
                load_group(0)

            def norm2_tile(src_ap, srcB, np_, i, dst_cols, dstB):
                xn_t, xn_b = xn2[i % 4]
                ss_t, ss_b = ssq2[i % 4]
                rs_t, rs_b = rst2[i % 4]

                def stage_a():
                    kb.op(ACT, lambda: nc.scalar.activation(xn_t[0:np_, :], src_ap, AF.Square, accum_out=ss_t[0:np_, :]),
                          reads=[srcB], writes=[xn_b, ss_b])
                    kb.op(ACT, lambda: nc.scalar.activation(rs_t[0:np_, :], ss_t[0:np_, :], AF.Sqrt,
                                                            bias=epsT[0:np_, 0:1], scale=1.0 / D),
                          reads=[ss_b, epsTB], writes=[rs_b])

                def stage_b():
                    kb.op(DVE, lambda: nc.vector.reciprocal(rs_t[0:np_, :], rs_t[0:np_, :]), reads=[rs_b], writes=[rs_b])
                    kb.op(ACT, lambda: nc.scalar.activation(xn_t[0:np_, :], src_ap, AF.Copy, scale=rs_t[0:np_, 0:1]),
                          reads=[srcB, rs_b], writes=[xn_b])
                pb = 6 + (i % 2)

                def part2():
                    if np_ == 128:
                        tps = PS[pb].bitcast(BF16)
                        for kc in range(8):
                            kb.op(PE, lambda: nc.tensor.transpose(tps[:, kc * 128:(kc + 1) * 128],
                                                                  xn_t[:, kc * 128:(kc + 1) * 128], ident[:]),
                                  reads=[xn_b, identB], writes=[PSB[pb]], mark=(kc == 7))
                        hv = h2T[:, :, dst_cols[0]:dst_cols[1]]
                        kb.op(DVE, lambda: nc.vector.tensor_tensor(
                            hv, tps[:, 0:1024].rearrange("p (c t) -> p c t", t=128),
                            A2[:, :].unsqueeze(2).broadcast_to([128, 8, 128]), ALU.mult),
                            reads=[PSB[pb], A2B], writes=[dstB])
                        kb.op(DVE, lambda: nc.vector.tensor_tensor(
                            hv, hv, SH2[:, :].unsqueeze(2).broadcast_to([128, 8, 128]), ALU.add),
                            reads=[SH2B], writes=[dstB])
                    else:
                        for kc in range(8):
                            kb.op(PE, lambda: nc.tensor.matmul(PS[pb][:, kc:kc + 1], xn_t[0:1, kc * 128:(kc + 1) * 128],
                                                               ones[0:1, 0:1], start=True, stop=True),
                                  reads=[xn_b, onesB], writes=[PSB[pb]], mark=(kc == 7))
                        for kc in range(8):
                            kb.op(ACT, lambda: nc.scalar.activation(
                                h2T[:, kc, dst_cols[0]:dst_cols[1]], PS[pb][:, kc:kc + 1], AF.Identity,
                                bias=SH2[:, kc:kc + 1], scale=A2[:, kc:kc + 1]),
                                reads=[PSB[pb], A2B, SH2B], writes=[dstB])
                return stage_a, stage_b, part2

            B2a_ar = Arena(H2_END, DG_OFF)
            xh, xhB = sb(B2a_ar, "xh", [1, D], F32)
            kb.dma(SP, d_xh, xh[:], x_d[2048:2049, :], writes=[xhB])
            for half in range(2):
                pb = half
                for kc in range(8):
                    kb.op(PE, lambda: nc.tensor.matmul(PS[pb][0:1, :], yT[:, kc, 2048:2049],
                                                       wout[:, kc, half * 512:(half + 1) * 512],
                                                       start=(kc == 0), stop=(kc == 7)),
                          reads=[yT_t[16], woutB], writes=[PSB[pb]], mark=(kc == 7))
                kb.op(DVE, lambda: nc.vector.tensor_tensor(
                    xh[0:1, half * 512:(half + 1) * 512], PS[pb][0:1, :], xh[0:1, half * 512:(half + 1) * 512],
                    ALU.add), reads=[PSB[pb], xhB], writes=[xhB])
            halo_fns = list(norm2_tile(xh[0:1, :], xhB, 1, 3, (H2W - 1, H2W), h2_t[NOWN + 1]))
            halo_fns[0]()
            qB2, qC2 = [], []
            for t in range(NOWN):
                x_t, x_b = xr[t % 3]
                kb.dma(SP, d_xr[t % 3], x_t[:], x_d[t * 128:(t + 1) * 128, :], writes=[x_b])
                pb0 = (2 * t) % 6
                for half in range(2):
                    pb = pb0 + half
                    for kc in range(8):
                        kb.op(PE, lambda: nc.tensor.matmul(PS[pb][:, :], yT[:, kc, t * 128:(t + 1) * 128],
                                                           wout[:, kc, half * 512:(half + 1) * 512],
                                                           start=(kc == 0), stop=(kc == 7)),
                              reads=[yT_t[t], woutB], writes=[PSB[pb]], mark=(kc == 7))
                kb.op(DVE, lambda: nc.vector.tensor_tensor(
                    xm[:, t, :], PSBIG[:, pb0 * 512:pb0 * 512 + 1024], x_t[:, :], ALU.add),
                    reads=[PSB[pb0], PSB[pb0 + 1], x_b], writes=[xm_t[t]])
                p2 = norm2_tile(xm[:, t, :], xm_t[t], 128, t, (1 + t * 128, 1 + (t + 1) * 128), h2_t[t + 1])
                sa_, sb_, sc_ = p2
                sa_()
                if t == 0:
                    halo_fns[1]()
                if t == 1:
                    halo_fns[2]()
                if t == 11 and stop_after >= 4:
                    scale_group(0)
                if qB2:
                    qB2.pop(0)()
                qB2.append(sb_)
                qC2.append(sc_)
                if len(qC2) > 3:
                    qC2.pop(0)()
            while qB2:
                qB2.pop(0)()
            while qC2:
                qC2.pop(0)()

            if dbg:
                dump(xm[:, 3, 0:256], 128, 256, [xm_t[3]])
                dump(h2T[:, 2, 0:256], 128, 256, h2_t[0:3])
                dump(h2T[:, 6, H2W - 128:H2W], 128, 128, h2_t[16:18])

        if stop_after >= 4:
            B2_ar.lo = H2_END
            B2_ar.reset()
            Y_ar.reset()
            for i_ in (1,):
                wupg[i_] = sb(B2_ar, f"wupg{i_}", [128, 8, GMAX * 128], BF16)
                wupv[i_] = sb(B2_ar, f"wupv{i_}", [128, 8, GMAX * 128], BF16)
                wdn[i_] = sb(B2_ar, f"wdn{i_}", [128, GMAX, D], BF16)
            gfb, gfbB = sb(B2_ar, "gfb", [128, D], F32)
            ot = [sb(B2_ar, f"ot{i}", [128, D], F32) for i in range(3)]
            sq3, sq3B = sb(B2_ar, "sq3", [128, D], BF16)
            ssq3 = [sb(B2_ar, f"ssq3{i}", [128, 1], F32) for i in range(3)]
            rst3 = [sb(B2_ar, f"rst3{i}", [128, 1], F32) for i in range(3)]
            hid, _ = sb(Y_ar, "hid", [128, GMAX, NOWN * 128], BF16)
            hidB = [Buf() for _ in FF_BLOCKS]
            NB = 412
            gl = [sb(Y_ar, f"gl{i}", [128, NB], F32) for i in range(2)]
            vl = [sb(Y_ar, f"vl{i}", [128, NB], F32) for i in range(2)]
            sl = [sb(Y_ar, f"sl{i}", [128, NB], F32) for i in range(2)]
            gfbB.r = {id(e_.sem): (e_.sem, e_.n) for e_ in ENGS}
            for d_ in DSEMS:
                gfbB.r[id(d_.sem)] = (d_.sem, d_.n)
            kb.dma(SP, d_xh, gfb[:], gfin_d.partition_broadcast(128), writes=[gfbB])
            pend = []
            fin_q = []
            acc_i = [0]
            done_tiles = {}
            cnt = [0]

            def down(gi, t):
                n = FF_GROUPS[gi]
                s = gi % 2
                last = gi == len(FF_GROUPS) - 1
                hb = [hidB[bi] for bi, (o0, no) in enumerate(FF_BLOCKS)
                      if o0 < (t + 1) * 128 and o0 + no > t * 128]
                o_t, o_b = ot[t % 3]
                pb0 = 4 + (2 * t) % 4
                for half in range(2):
                    pb = pb0 + half
                    for i in range(n):
                        kb.op(PE, lambda: nc.tensor.matmul(PS[pb][:, :], hid[:, i, t * 128:(t + 1) * 128],
                                                           wdn[s][0][:, i, half * 512:(half + 1) * 512],
                                                           start=(i == 0), stop=(i == n - 1)),
                              reads=hb + [wB[s][2]], writes=[PSB[pb]], mark=(i == n - 1))
                psum2 = PSBIG[:, pb0 * 512:pb0 * 512 + 1024]
                if not last:
                    k_ = acc_i[0]
                    acc_i[0] += 1
                    tmp_t, tmp_b = ot[k_ % 3]
                    kb.op(ACT, lambda: nc.scalar.copy(tmp_t[:], psum2),
                          reads=[PSB[pb0], PSB[pb0 + 1]], writes=[tmp_b])
                    kb.dma(POOL, d_acc[k_ % 8], xm[:, t, :], tmp_t[:], reads=[tmp_b, xm_t[t]], writes=[xm_t[t]],
                           accum_op=ALU.add)
                else:
                    kb.op(DVE, lambda: nc.vector.tensor_tensor(o_t[:], psum2, xm[:, t, :], ALU.add),
                          reads=[PSB[pb0], PSB[pb0 + 1], xm_t[t]], writes=[o_b])
                if last:
                    ss_t, ss_b = ssq3[t % 3]
                    rs_t, rs_b = rst3[t % 3]
                    while fin_q:
                        fin_q.pop(0)()
                    kb.op(ACT, lambda: nc.scalar.activation(sq3[:], o_t[:], AF.Square, accum_out=ss_t[:]),
                          reads=[o_b], writes=[sq3B, ss_b])
                    kb.op(ACT, lambda: nc.scalar.activation(rs_t[:], ss_t[:], AF.Sqrt, bias=epsT[:, 0:1], scale=1.0 / D),
                          reads=[ss_b, epsTB], writes=[rs_b])

                    def fin():
                        kb.op(DVE, lambda: nc.vector.reciprocal(rs_t[:], rs_t[:]), reads=[rs_b], writes=[rs_b])
                        kb.op(DVE, lambda: nc.vector.scalar_tensor_tensor(
                            o_t[:], o_t[:], rs_t[:, 0:1], gfb[:], ALU.mult, ALU.mult),
                            reads=[o_b, rs_b, gfbB], writes=[o_b])
                        kb.dma(SP, d_ot[t % 3], out_d[t * 128:(t + 1) * 128, :], o_t[:], reads=[o_b])
                    fin_q.append(fin)

            for gi in range(len(FF_GROUPS)):
                j0, n = gstart[gi], FF_GROUPS[gi]
                s = gi % 2
                for bi, (o0, no) in enumerate(FF_BLOCKS):
                    ncol = no + 2
                    t_lo = max(o0 - 1, 0) // 128
                    t_hi = min((o0 + no) // 128, NOWN - 1)
                    hb = [h2_t[0]] + h2_t[1 + t_lo:2 + t_hi] + ([h2_t[NOWN + 1]] if o0 + no == NOWN * 128 else [])
                    for i in range(n):
                        j = j0 + i
                        k = cnt[0] % 2
                        cnt[0] += 1
                        bg, bv = 2 * k, 2 * k + 1
                        for kc in range(8):
                            kb.op(PE, lambda: nc.tensor.matmul(PS[bg][:, 0:ncol], wupg[s][0][:, kc, i * 128:(i + 1) * 128],
                                                               h2T[:, kc, o0:o0 + ncol], start=(kc == 0), stop=(kc == 7)),
                                  reads=hb + [wB[s][0]], writes=[PSB[bg]], mark=(kc == 7))
                        for kc in range(8):
                            kb.op(PE, lambda: nc.tensor.matmul(PS[bv][:, 0:ncol], wupv[s][0][:, kc, i * 128:(i + 1) * 128],
                                                               h2T[:, kc, o0:o0 + ncol], start=(kc == 0), stop=(kc == 7)),
                                  reads=hb + [wB[s][1]], writes=[PSB[bv]], mark=(kc == 7))
                        g_t, g_b = gl[k]
                        v_t_, v_b = vl[k]
                        s_t, s_b = sl[k]
                        for (acc, accB, pbk, ch) in ((g_t, g_b, bg, j), (v_t_, v_b, bv, NFC + j)):
                            kb.op(ACT, lambda: nc.scalar.activation(
                                acc[:, 0:no], PS[pbk][:, 1:1 + no], AF.Identity,
                                bias=fcb[:, ch:ch + 1], scale=fcw[:, ch, 1:2]),
                                reads=[PSB[pbk], fcwB, fcbB], writes=[accB])
                            kb.op(DVE, lambda: nc.vector.scalar_tensor_tensor(
                                acc[:, 0:no], PS[pbk][:, 0:no], fcw[:, ch, 0:1], acc[:, 0:no], ALU.mult, ALU.add),
                                reads=[PSB[pbk], accB, fcwB], writes=[accB])
                            kb.op(DVE, lambda: nc.vector.scalar_tensor_tensor(
                                acc[:, 0:no], PS[pbk][:, 2:2 + no], fcw[:, ch, 2:3], acc[:, 0:no], ALU.mult, ALU.add),
                                reads=[PSB[pbk], accB, fcwB], writes=[accB])
                        kb.op(ACT, lambda: nc.scalar.activation(s_t[:, 0:no], g_t[:, 0:no], AF.Silu),
                              reads=[g_b], writes=[s_b])
                        kb.op(POOL, lambda: nc.gpsimd.tensor_tensor(
                            hid[:, i, o0:o0 + no], s_t[:, 0:no], v_t_[:, 0:no], ALU.mult),
                            reads=[s_b, v_b], writes=[hidB[bi]])
                    for (pg, pt) in pend:
                        down(pg, pt)
                    pend = []
                    if bi == 0 and gi + 1 < len(FF_GROUPS):
                        load_group(gi + 1)
                    if bi == 3 and gi + 1 < len(FF_GROUPS):
                        scale_group(gi + 1)
                    t_done = (o0 + no) // 128
                    t_prev = done_tiles.get(gi, 0)
                    for t in range(t_prev, t_done):
                        pend.append((gi, t))
                    done_tiles[gi] = t_done
            for (pg, pt) in pend:
                down(pg, pt)
            pend = []
            while fin_q:
                fin_q.pop(0)()

        kb.barrier(ENGS, DSEMS)
    return nc


def _bias_tables(rpb, half):
    H = rpb.shape[0]
    ext = np.concatenate([rpb.reshape(H, -1), np.full((H, 1), -30000.0, np.float32)], axis=1)
    MASK = 15 * 31
    specs = [(0, k) for k in range(4)] + [(1, k) for k in range(4)] + [(8, 8 + d) for d in range(-2, 3)]
    idx = np.zeros((128, NTAB, 128), np.int64)
    kk = np.arange(128)
    for ti, (qt, kt) in enumerate(specs):
        rl = 2 * qt + (kk // 64)
        cl = kk % 64
        krl = 2 * kt + (kk // 64)
        kcl = kk % 64
        if half == 0:
            rg, cg, krg, kcg = rl, cl, krl, kcl
        else:
            rg, cg, krg, kcg = 63 - rl, 63 - cl, 63 - krl, 63 - kcl
        rs = np.clip(rg - 4, 0, 56)
        cs = np.clip(cg - 8, 0, 48)
        KR, R = np.meshgrid(krg, rg, indexing="ij")
        KC, C = np.meshgrid(kcg, cg, indexing="ij")
        RS = np.broadcast_to(rs[None, :], KR.shape)
        CS = np.broadcast_to(cs[None, :], KR.shape)
        valid = (KR >= RS) & (KR < RS + 8) & (KC >= CS) & (KC < CS + 16)
        ro = np.clip(KR - R + 7, 0, 14)
        co = np.clip(KC - C + 15, 0, 30)
        idx[:, ti, :] = np.where(valid, ro * 31 + co, MASK)
    tab = ext[:, idx]
    return np.ascontiguousarray(np.transpose(tab, (1, 0, 2, 3))).astype(np.float32)


def _colT(v, n):
    return np.ascontiguousarray(v.reshape(n, 128).T)


def make_in_maps(x, c, ctx, c_ctx, w_mod, b_mod, g_norm1, w_in, rpb, conv_w, conv_b, ln_g, ln_b,
                 w_out, g_norm2, w_up, ffn_conv_w, ffn_conv_b, w_down, g_final):
    f = np.float32
    maps = []
    ident = np.eye(128, dtype=f)
    shared = {
        "w_mod": np.ascontiguousarray(w_mod[0], f), "bmodT": _colT(b_mod[0], 48),
        "bmodR": np.ascontiguousarray(b_mod[0].reshape(1, -1), f),
        "g1T": _colT(g_norm1[0], 8), "g2T": _colT(g_norm2[0], 8),
        "w_in": np.ascontiguousarray(w_in[0], f), "cbT": _colT(conv_b[0], 4),
        "lngT": _colT(ln_g[0], 4), "lnbT": _colT(ln_b[0], 4),
        "w_out": np.ascontiguousarray(w_out[0], f), "w_up": np.ascontiguousarray(w_up[0], f),
        "fcbT": _colT(ffn_conv_b[0], 44), "w_down": np.ascontiguousarray(w_down[0], f),
        "g_final": np.ascontiguousarray(g_final, f), "ident": ident,
    }
    for core in range(8):
        b, half = core // 2, core % 2
        xs = x[b] if half == 0 else x[b, ::-1]
        cw = conv_w[0] if half == 0 else conv_w[0, ::-1]
        fw = ffn_conv_w[0] if half == 0 else ffn_conv_w[0, ::-1]
        m = dict(shared)
        m["x"] = np.ascontiguousarray(xs[:TOK], f)
        m["cT"] = np.ascontiguousarray(np.stack([_colT(c[b], 8), _colT(c_ctx, 8)], axis=2), f)
        m["ctx"] = np.ascontiguousarray(ctx[b], f)
        m["btab"] = _bias_tables(rpb[0], half)
        m["cwT"] = np.ascontiguousarray(np.transpose(cw.reshape(31, 4, 128), (2, 1, 0)), f)
        m["fcwT"] = np.ascontiguousarray(np.transpose(fw.reshape(3, 44, 128), (2, 1, 0)), f)
        maps.append(m)
    return maps


_NC_CACHE = {}


def kernel(**inputs):
    inputs = {k: np.asarray(v) for k, v in inputs.items()}
    maps = make_in_maps(**inputs)
    if "nc" not in _NC_CACHE:
        _NC_CACHE["nc"] = build()
    res = run_bass_kernel_spmd(_NC_CACHE["nc"], maps, core_ids=list(range(8)))
    out = np.zeros((4, 4096, D), np.float32)
    for core in range(8):
        b, half = core // 2, core % 2
        o = np.asarray(res.results[core]["out"], np.float32)
        if half == 0:
            out[b, :2048] = o
        else:
            out[b, 2048:] = o[::-1]
    return out
```

```python
import numpy as np
from contextlib import ExitStack
import concourse.bass as bass
import concourse.mybir as mybir
from concourse.bass_utils import run_bass_kernel_spmd

F32 = mybir.dt.float32
BF16 = mybir.dt.bfloat16
AF = mybir.ActivationFunctionType
ALU = mybir.AluOpType

D = 1024
NT = 19
NOWN = 16
NQ = 17
TOK = NT * 128
NQT = NQ * 128
DFF = 2816
NFC = 22
EPS = 1e-6
FF_GROUPS = [4, 4, 4, 4, 3, 3]
FF_BLOCKS = [(0, 410), (410, 410), (820, 410), (1230, 410), (1640, 408)]
NTAB = 13
DBGW = 4096


class Buf:
    __slots__ = ("w", "r")

    def __init__(self):
        self.w = None
        self.r = {}


class Eng:
    def __init__(self, e, sem, pe=False):
        self.e = e
        self.sem = sem
        self.n = 0
        self.known = {}
        self.pe = pe


class DSem:
    def __init__(self, sem):
        self.sem = sem
        self.n = 0


class KB:
    def wait(self, eng, sem, val):
        if val <= 0 or eng.known.get(id(sem), 0) >= val:
            return
        eng.e.wait_ge(sem, val)
        eng.known[id(sem)] = val

    def deps(self, eng, reads, writes):
        need = {}

        def add(tok):
            if tok is None:
                return
            s, v = tok
            k = id(s)
            if k not in need or need[k][1] < v:
                need[k] = (s, v)

        for b in reads:
            add(b.w)
        for b in writes:
            add(b.w)
            for tok in b.r.values():
                add(tok)
        for s, v in need.values():
            if eng.pe and s is eng.sem:
                continue
            self.wait(eng, s, v)

    def op(self, eng, fn, reads=(), writes=(), mark=True):
        self.deps(eng, reads, writes)
        ins = fn()
        if mark:
            eng.n += 1
            ins.then_inc(eng.sem, 1)
            tok = (eng.sem, eng.n)
        else:
            tok = (eng.sem, eng.n + 1)
        for b in reads:
            b.r[id(eng.sem)] = tok
        for b in writes:
            b.w = tok
            b.r = {}
        return ins

    def dma(self, q, ds, out, in_, reads=(), writes=(), accum_op=None):
        self.deps(q, reads, writes)
        if accum_op is None:
            ins = q.e.dma_start(out=out, in_=in_)
        else:
            ins = q.e.dma_start(out=out, in_=in_, accum_op=accum_op)
        ds.n += 16
        ins.then_inc(ds.sem, 16)
        tok = (ds.sem, ds.n)
        for b in reads:
            b.r[id(ds.sem)] = tok
        for b in writes:
            b.w = tok
            b.r = {}
        return ins

    def barrier(self, engs, dsems):
        for a in engs:
            for b in engs:
                self.wait(a, b.sem, b.n)
            for d in dsems:
                self.wait(a, d.sem, d.n)


def build(stop_after=99, dbg=False):
    nc = bass.Bass("TRN2", target_bir_lowering=False)

    def din(name, shape):
        return nc.dram_tensor(name, list(shape), F32, kind="ExternalInput").ap()

    x_d = din("x", [TOK, D])
    c_d = din("cT", [128, 8, 2])
    ctx_d = din("ctx", [256, D])
    wmod_d = din("w_mod", [D, 6 * D])
    bmodT_d = din("bmodT", [128, 48])
    bmodR_d = din("bmodR", [1, 6 * D])
    g1T_d = din("g1T", [128, 8])
    g2T_d = din("g2T", [128, 8])
    win_d = din("w_in", [D, 2560])
    btab_d = din("btab", [128, 8, NTAB, 128])
    cw_d = din("cwT", [128, 4, 31])
    cb_d = din("cbT", [128, 4])
    lng_d = din("lngT", [128, 4])
    lnb_d = din("lnbT", [128, 4])
    wout_d = din("w_out", [D, D])
    wup_d = din("w_up", [D, 2 * DFF])
    fcw_d = din("fcwT", [128, 44, 3])
    fcb_d = din("fcbT", [128, 44])
    wdn_d = din("w_down", [DFF, D])
    gfin_d = din("g_final", [D])
    ident_d = din("ident", [128, 128])
    out_d = nc.dram_tensor("out", [NOWN * 128, D], F32, kind="ExternalOutput").ap()
    dbg_d = nc.dram_tensor("dbg", [128, DBGW], F32, kind="ExternalOutput").ap() if dbg else None

    kb = KB()
    with ExitStack() as top:
        def sem(name):
            return top.enter_context(nc.semaphore(name))

        PE = Eng(nc.tensor, sem("s_pe"), pe=True)
        ACT = Eng(nc.scalar, sem("s_act"))
        DVE = Eng(nc.vector, sem("s_dve"))
        POOL = Eng(nc.gpsimd, sem("s_pool"))
        SP = Eng(nc.sync, sem("s_sp"))
        ENGS = [PE, ACT, DVE, POOL, SP]
        DSEMS = []

        def dsem(name):
            d = DSem(sem(name))
            DSEMS.append(d)
            return d

        d_c = dsem("d_c")
        d_id = dsem("d_id")
        d_g = dsem("d_g")
        d_gtr = dsem("d_gtr")
        d_bst = dsem("d_bst")
        d_wm = [dsem(f"d_wm{i}") for i in range(2)]
        d_win = [dsem(f"d_win{i}") for i in range(4)]
        d_wout = dsem("d_wout")
        d_xt = [dsem(f"d_xt{i}") for i in range(2)]
        d_xr = [dsem(f"d_xr{i}") for i in range(3)]
        d_xh = dsem("d_xh")
        d_wg = [dsem(f"d_wg{i}") for i in range(2)]
        d_wv = [dsem(f"d_wv{i}") for i in range(2)]
        d_wd = [dsem(f"d_wd{i}") for i in range(2)]
        d_ot = [dsem(f"d_ot{i}") for i in range(3)]
        d_acc = [dsem(f"d_acc{i}") for i in range(8)]

        class Arena:
            def __init__(self, lo, hi):
                self.lo, self.hi, self.cur = lo, hi, lo

            def reset(self):
                self.cur = self.lo

        uniq = [0]

        def sb(ar, name, shape, dt):
            size = int(np.prod(shape[1:])) * (2 if dt == BF16 else 4)
            off = (ar.cur + 31) // 32 * 32
            assert off + size <= ar.hi, (name, off, size, ar.hi)
            ar.cur = off + size
            uniq[0] += 1
            return nc.alloc_sbuf_tensor_at(f"sb{uniq[0]}_{name}", list(shape), dt, offset=off), Buf()

        SB_LO, SB_HI = 16512, 229344
        C_END = SB_LO + 10752
        Y_END = C_END + 34816
        K_END = Y_END + 78400
        XM_END = Y_END + 65536
        H2_END = XM_END + 32800
        top_ar = Arena(SB_LO, C_END)
        Y_ar = Arena(C_END, Y_END)
        K_ar = Arena(Y_END, K_END)
        B_ar = Arena(K_END, SB_HI)
        XM_ar = Arena(Y_END, XM_END)
        H2_ar = Arena(XM_END, H2_END)
        B2_ar = Arena(H2_END, SB_HI)

        PSBIG = top.enter_context(nc.psum_tensor("psbig", [128, 8 * 512], F32))
        PS = [PSBIG[:, i * 512:(i + 1) * 512] for i in range(8)]
        PSB = [Buf() for _ in range(8)]

        dbg_col = [0]

        def dump(ap, parts, width, reads):
            if not dbg:
                return
            c0 = dbg_col[0]
            assert c0 + width <= DBGW
            kb.dma(POOL, d_g, dbg_d[0:parts, c0:c0 + width], ap, reads=reads)
            dbg_col[0] += width
            return c0

        ident, identB = sb(top_ar, "ident", [128, 128], BF16)
        ones, onesB = sb(top_ar, "ones", [128, 128], BF16)
        A1, A1B = sb(top_ar, "A1", [128, 8, 2], F32)
        SH1, SH1B = sb(top_ar, "SH1", [128, 8, 2], F32)
        A2, A2B = sb(top_ar, "A2", [128, 8], F32)
        SH2, SH2B = sb(top_ar, "SH2", [128, 8], F32)
        gt1b, gt1bB = sb(top_ar, "gt1b", [128, D], F32)
        gt2b, gt2bB = sb(top_ar, "gt2b", [128, D], F32)
        fcw, fcwB = sb(top_ar, "fcw", [128, 44, 3], F32)
        fcb, fcbB = sb(top_ar, "fcb", [128, 44], F32)
        cwT, cwTB = sb(top_ar, "cwT", [128, 4, 31], F32)
        cbT, cbTB = sb(top_ar, "cbT", [128, 4], F32)
        lngT, lngTB = sb(top_ar, "lngT", [128, 4], F32)
        lnbT, lnbTB = sb(top_ar, "lnbT", [128, 4], F32)
        epsT, epsTB = sb(top_ar, "epsT", [128, 1], F32)

        kb.dma(POOL, d_id, ident[:], ident_d, writes=[identB])
        kb.op(DVE, lambda: nc.vector.memset(ones[:], 1.0), writes=[onesB])
        kb.op(DVE, lambda: nc.vector.memset(epsT[:], EPS), writes=[epsTB])
        kb.dma(SP, d_c, fcw[:], fcw_d, writes=[fcwB])
        kb.dma(SP, d_c, fcb[:], fcb_d, writes=[fcbB])
        kb.dma(SP, d_c, cwT[:], cw_d, writes=[cwTB])
        kb.dma(SP, d_c, cbT[:], cb_d, writes=[cbTB])
        kb.dma(SP, d_c, lngT[:], lng_d, writes=[lngTB])
        kb.dma(SP, d_c, lnbT[:], lnb_d, writes=[lnbTB])

        def rstd_from_ssq(eng_unused, ssq_ap, ssqB, out_ap, outB, n, parts=128):
            kb.op(ACT, lambda: nc.scalar.activation(out_ap, ssq_ap, AF.Sqrt, bias=epsT[0:parts, 0:1], scale=1.0 / n),
                  reads=[ssqB, epsTB], writes=[outB])
            kb.op(DVE, lambda: nc.vector.reciprocal(out_ap, out_ap), reads=[outB], writes=[outB])

        kT, kTB = sb(K_ar, "kT", [128, 4, TOK], BF16)
        qT, qTB = sb(K_ar, "qT", [128, 4, NQT], BF16)
        vaug, vaugB = sb(K_ar, "vaug", [128, NT, 8 * 65], BF16)
        UW = 15 + NQT + 1
        UT_ADDR = (K_ar.cur + 31) // 32 * 32
        uT, uTB = sb(K_ar, "uT", [128, 4, UW], BF16)
        kcT, kcTB = sb(K_ar, "kcT", [128, 4, 256], BF16)
        vcaug, vcaugB = sb(K_ar, "vcaug", [128, 2, 8 * 65], BF16)
        kT_t = [Buf() for _ in range(NT)]
        qT_t = [Buf() for _ in range(NQ)]
        v_t = [Buf() for _ in range(NT)]
        uT_t = [Buf() for _ in range(NQ + 1)]

        kb.op(DVE, lambda: nc.vector.memset(vaug[:], 1.0), writes=v_t)
        kb.op(DVE, lambda: nc.vector.memset(vcaug[:], 1.0), writes=[vcaugB])
        kb.op(DVE, lambda: nc.vector.memset(uT[:, :, 0:15], 0.0), writes=[uT_t[0]])

        if True:
            ETOP = 26624 + 3584 + 64
            Bhi_ar = Arena(SB_HI - ETOP, SB_HI)
            B_ar.hi = SB_HI - ETOP
            Etab, EtabB = sb(Bhi_ar, "Etab", [128, 8, NTAB, 128], BF16)
            BST_OFF = (Bhi_ar.cur + 31) // 32 * 32
            bst, bstB = sb(Bhi_ar, "bst", [128, 7, 128], F32)
            etab_items = []
            etab_specs = [(h_, tt0, ntt) for h_ in range(8) for (tt0, ntt) in ((0, 7), (7, 6))]

            def _etab_dma(k):
                h_, tt0, ntt = etab_specs[k]
                kb.dma(POOL, d_bst, bst[:, 0:ntt, :], btab_d[:, h_, tt0:tt0 + ntt, :], writes=[bstB])

            def _etab_exp(k):
                h_, tt0, ntt = etab_specs[k]
                kb.op(ACT, lambda: nc.scalar.activation(Etab[:, h_, tt0:tt0 + ntt, :], bst[:, 0:ntt, :], AF.Exp),
                      reads=[bstB], writes=[EtabB])

            for k_ in range(len(etab_specs) + 1):
                def _item(k_=k_):
                    if k_ >= 1:
                        _etab_exp(k_ - 1)
                    if k_ < len(etab_specs):
                        _etab_dma(k_)
                etab_items.append(_item)
            win, winB = sb(B_ar, "win", [128, 8, 2560], BF16)
            wm = [sb(Y_ar, f"wm{i}", [128, 8, 256], BF16) for i in range(2)]
            cc, ccB = sb(B_ar, "cc", [128, 8, 2], F32)
            ccb, ccbB = sb(B_ar, "ccb", [128, 8, 2], BF16)
            bmT, bmTB = sb(B_ar, "bmT", [128, 48], F32)
            g1T, g1TB = sb(B_ar, "g1T", [128, 8], F32)
            g2T, g2TB = sb(B_ar, "g2T", [128, 8], F32)
            modT, modTB = sb(B_ar, "modT", [128, 48, 2], F32)
            gtr, gtrB = sb(B_ar, "gtr", [1, 256], F32)
            gtrb, gtrbB = sb(B_ar, "gtrb", [1, 256], BF16)
            gtlo, gtloB = sb(B_ar, "gtlo", [1, 256], F32)
            gtlob, gtlobB = sb(B_ar, "gtlob", [1, 256], BF16)
            xt = [sb(Y_ar, f"xt{i}", [128, D], F32) for i in range(2)]
            xn = [sb(B_ar, f"xn{i}", [128, D], BF16) for i in range(2)]
            sqj, sqjB = sb(B_ar, "sqj", [128, D], BF16)
            ssq = [sb(B_ar, f"ssq{i}", [128, 1], F32) for i in range(2)]
            rst = [sb(B_ar, f"rst{i}", [128, 1], F32) for i in range(2)]
            hT = [sb(Y_ar, f"hT{i}", [128, 8, 512], BF16) for i in range(2)]
            sg = [sb(B_ar, f"sg{i}", [128, 512], F32) for i in range(2)]

            kb.dma(SP, d_c, cc[:], c_d, writes=[ccB])
            kb.dma(SP, d_c, bmT[:], bmodT_d, writes=[bmTB])
            kb.dma(SP, d_c, g1T[:], g1T_d, writes=[g1TB])
            kb.dma(SP, d_c, g2T[:], g2T_d, writes=[g2TB])
            for bb in (fcwB, fcbB, cwTB, cbTB, lngTB, lnbTB, ccB, bmTB, g1TB, g2TB):
                bb.w = (d_c.sem, d_c.n)
            wmod_v = wmod_d.rearrange("(kc p) n -> p kc n", p=128)
            win_v = win_d.rearrange("(kc p) n -> p kc n", p=128)

            def load_wm(blk):
                t, tb = wm[blk % 2]
                kb.dma(POOL, d_wm[blk % 2], t[:], wmod_v[:, :, blk * 256:(blk + 1) * 256], writes=[tb])

            load_wm(0)
            load_wm(1)
            kb.op(ACT, lambda: nc.scalar.activation(ccb[:], cc[:], AF.Silu), reads=[ccB], writes=[ccbB])

            mT_ps, mT_B = PS[0], PSB[0]
            row_ps = [(PS[1], PSB[1]), (PS[2], PSB[2])]
            bcast_q = []

            def mod_block(blk):
                while bcast_q:
                    bcast_q.pop(0)()
                t, tb = wm[blk % 2]
                for sub in range(2):
                    ch = blk * 2 + sub
                    for kc in range(8):
                        kb.op(PE, lambda: nc.tensor.matmul(
                            mT_ps[:, ch * 2:ch * 2 + 2], t[:, kc, sub * 128:(sub + 1) * 128], ccb[:, kc, :],
                            start=(kc == 0), stop=(kc == 7)),
                            reads=[tb, ccbB], writes=[mT_B], mark=(kc == 7))
                if blk in (8, 9, 10, 11, 20, 21, 22, 23):
                    rp, rpB = row_ps[blk % 2]
                    for kc in range(8):
                        kb.op(PE, lambda: nc.tensor.matmul(
                            rp[0:2, 0:256], ccb[:, kc, :], t[:, kc, :], start=(kc == 0), stop=(kc == 7)),
                            reads=[tb, ccbB], writes=[rpB], mark=(kc == 7))
                    kb.dma(SP, d_gtr, gtr[:], bmodR_d[0:1, blk * 256:(blk + 1) * 256], writes=[gtrB])
                    kb.op(DVE, lambda: nc.vector.tensor_tensor(gtr[0:1, :], rp[0:1, 0:256], gtr[0:1, :], ALU.add),
                          reads=[rpB, gtrB], writes=[gtrB])
                    kb.op(DVE, lambda: nc.vector.tensor_copy(gtrb[:], gtr[:]), reads=[gtrB], writes=[gtrbB])
                    kb.op(DVE, lambda: nc.vector.tensor_tensor(gtlo[:], gtr[:], gtrb[:], ALU.subtract),
                          reads=[gtrB, gtrbB], writes=[gtloB])
                    kb.op(DVE, lambda: nc.vector.tensor_copy(gtlob[:], gtlo[:]), reads=[gtloB], writes=[gtlobB])

                    def bcast_part(blk=blk):
                        bp, bpB = PS[3 + (blk % 2)], PSB[3 + (blk % 2)]
                        kb.op(PE, lambda: nc.tensor.matmul(bp[:, 0:256], ones[0:1, :], gtrb[0:1, :], start=True, stop=False),
                              reads=[onesB, gtrbB], writes=[bpB], mark=False)
                        kb.op(PE, lambda: nc.tensor.matmul(bp[:, 0:256], ones[0:1, :], gtlob[0:1, :], start=False, stop=True),
                              reads=[onesB, gtlobB], writes=[bpB])
                        dst, dstB = (gt1b, gt1bB) if blk < 12 else (gt2b, gt2bB)
                        o = (blk % 4) * 256
                        kb.op(ACT, lambda: nc.scalar.copy(dst[:, o:o + 256], bp[:, 0:256]),
                              reads=[bpB], writes=[dstB])
                    bcast_q.append(bcast_part)
                if blk + 2 < 24:
                    load_wm(blk + 2)
            for blk in range(8):
                mod_block(blk)
            win_k, win_q, win_ag, win_vb = Buf(), Buf(), Buf(), Buf()
            for (c_lo, c_hi, bb, dd) in ((512, 1024, win_k, d_win[0]), (0, 512, win_q, d_win[1]),
                                         (1536, 2560, win_ag, d_win[2]), (1024, 1536, win_vb, d_win[3])):
                kb.dma(POOL, dd, win[:, :, c_lo:c_hi], win_v[:, :, c_lo:c_hi], writes=[bb])
            kb.op(DVE, lambda: nc.vector.tensor_tensor(
                modT[:, 0:16, :], mT_ps[:, 0:32].rearrange("p (c t) -> p c t", t=2),
                bmT[:, 0:16].unsqueeze(2).broadcast_to([128, 16, 2]), ALU.add),
                reads=[mT_B, bmTB], writes=[modTB])
            kb.op(DVE, lambda: nc.vector.scalar_tensor_tensor(
                A1[:], modT[:, 8:16, :], 1.0, g1T[:].unsqueeze(2).broadcast_to([128, 8, 2]), ALU.add, ALU.mult),
                reads=[modTB, g1TB], writes=[A1B])
            kb.op(DVE, lambda: nc.vector.tensor_copy(SH1[:], modT[:, 0:8, :]), reads=[modTB], writes=[SH1B])

            def mod_finish():
                kb.op(DVE, lambda: nc.vector.tensor_tensor(
                    modT[:, 16:48, :], mT_ps[:, 32:96].rearrange("p (c t) -> p c t", t=2),
                    bmT[:, 16:48].unsqueeze(2).broadcast_to([128, 32, 2]), ALU.add),
                    reads=[mT_B, bmTB], writes=[modTB])
                kb.op(DVE, lambda: nc.vector.scalar_tensor_tensor(
                    A2[:], modT[:, 32:40, 0], 1.0, g2T[:], ALU.add, ALU.mult),
                    reads=[modTB, g2TB], writes=[A2B])
                kb.op(DVE, lambda: nc.vector.tensor_copy(SH2[:], modT[:, 24:32, 0]), reads=[modTB], writes=[SH2B])
                if dbg:
                    dump(modT[:].rearrange("p c t -> p (c t)"), 128, 96, [modTB])
                    dump(gt1b[:, 0:64], 128, 64, [gt1bB])
                    dump(gt2b[:, 0:64], 128, 64, [gt2bB])

            tp_banks = [6, 7]
            state = {"i": 0}

            def front(src_ap, hT_t, hT_b, col0, var):
                i = state["i"]
                state["i"] += 1
                x_t, x_b = xt[i % 2]
                xn_t, xn_b = xn[i % 2]
                ss_t, ss_b = ssq[i % 2]
                rs_t, rs_b = rst[i % 2]
                pb = tp_banks[i % 2]
                tps = PS[pb].bitcast(BF16)

                def p1():
                    kb.dma(SP, d_xt[i % 2], x_t[:], src_ap, writes=[x_b])
                    kb.op(ACT, lambda: nc.scalar.activation(sqj[:], x_t[:], AF.Square, accum_out=ss_t[:]),
                          reads=[x_b], writes=[sqjB, ss_b])
                    rstd_from_ssq(None, ss_t[:], ss_b, rs_t[:], rs_b, D)
                    kb.op(DVE, lambda: nc.vector.tensor_scalar(xn_t[:], x_t[:], rs_t[:, 0:1], None, ALU.mult),
                          reads=[x_b, rs_b], writes=[xn_b])

                def p2():
                    for kc in range(8):
                        kb.op(PE, lambda: nc.tensor.transpose(tps[:, kc * 128:(kc + 1) * 128],
                                                              xn_t[:, kc * 128:(kc + 1) * 128], ident[:]),
                              reads=[xn_b, identB], writes=[PSB[pb]], mark=(kc == 7))
                    hv = hT_t[:, :, col0:col0 + 128]
                    kb.op(DVE, lambda: nc.vector.tensor_tensor(
                        hv, tps[:, 0:1024].rearrange("p (c t) -> p c t", t=128),
                        A1[:, :, var:var + 1].broadcast_to([128, 8, 128]), ALU.mult),
                        reads=[PSB[pb], A1B], writes=[hT_b])
                    kb.op(DVE, lambda: nc.vector.tensor_tensor(
                        hv, hv, SH1[:, :, var:var + 1].broadcast_to([128, 8, 128]), ALU.add),
                        reads=[SH1B], writes=[hT_b])
                return p1, p2

            groups = [(0, 2), (2, 4), (6, 4), (10, 4), (14, 3), (17, 2)]
            bank_rr = [0]

            def nb():
                b = 1 + bank_rr[0] % 5
                bank_rr[0] += 1
                return b

            def group_front(gi):
                t0, ntl = groups[gi]
                h_t, h_b = hT[(gi + 1) % 2]
                parts = [front(x_d[(t0 + j) * 128:(t0 + j + 1) * 128, :], h_t, h_b, j * 128, 0)
                         for j in range(ntl)]
                items = [parts[0][0]]
                for j in range(ntl):
                    nxt_p1 = parts[j + 1][0] if j + 1 < ntl else (lambda: None)
                    items.append((lambda a, b: (lambda: (a(), b())))(parts[j][1], nxt_p1))
                return items

            for it in group_front(0):
                it()
            mod_next = [8]
            defer = []

            def tick():
                if defer:
                    defer.pop(0)()

            def mod_some():
                if mod_next[0] < 24:
                    mod_block(mod_next[0])
                    mod_next[0] += 1

            for gi, (t0, ntl) in enumerate(groups if stop_after >= 1 else []):
                h_t, h_b = hT[(gi + 1) % 2]
                defer = []
                if gi + 1 < len(groups):
                    gf = group_front(gi + 1)
                    for it in gf:
                        defer.append(it)
                        defer.append(mod_some)
                else:
                    hc_t, hc_b = hT[(len(groups) - 1) % 2]
                    cparts = [front(ctx_d[t * 128:(t + 1) * 128, :], hc_t, hc_b, t * 128, 1) for t in range(2)]
                    citems = [cparts[0][0], (lambda: (cparts[0][1](), cparts[1][0]())), cparts[1][1]]
                    for it in citems:
                        defer.append(it)
                        defer.append(mod_some)
                    defer += [mod_some] * 5
                if gi >= 1:
                    nd = len(defer)
                    for q_ in range(4):
                        if etab_items:
                            defer.insert(min(len(defer), 2 + q_ * (nd // 4 + 1) + q_), etab_items.pop(0))
                tick()
                N = ntl * 128
                nq = max(0, min(N, (NQ - t0) * 128))
                c0 = t0 * 128
                for ch in range(4):
                    pb = nb()
                    for kc in range(8):
                        kb.op(PE, lambda: nc.tensor.matmul(
                            PS[pb][:, 0:N], win[:, kc, 512 + ch * 128:512 + (ch + 1) * 128], h_t[:, kc, 0:N],
                            start=(kc == 0), stop=(kc == 7)),
                            reads=[win_k, h_b], writes=[PSB[pb]], mark=(kc == 7))
                    kb.op(DVE, lambda: nc.vector.tensor_copy(kT[:, ch, c0:c0 + N], PS[pb][:, 0:N]),
                          reads=[PSB[pb]], writes=kT_t[t0:t0 + ntl])
                    tick()
                for ch in range(4 if nq > 0 else 0):
                    pb = nb()
                    for kc in range(8):
                        kb.op(PE, lambda: nc.tensor.matmul(
                            PS[pb][:, 0:nq], win[:, kc, ch * 128:(ch + 1) * 128], h_t[:, kc, 0:nq],
                            start=(kc == 0), stop=(kc == 7)),
                            reads=[win_q, h_b], writes=[PSB[pb]], mark=(kc == 7))
                    kb.op(ACT, lambda: nc.scalar.mul(qT[:, ch, c0:c0 + nq], PS[pb][:, 0:nq], 0.125),
                          reads=[PSB[pb]], writes=qT_t[t0:t0 + nq // 128])
                    tick()
                for ch in range(4 if nq > 0 else 0):
                    pa_, pg_ = nb(), nb()
                    for kc in range(8):
                        kb.op(PE, lambda: nc.tensor.matmul(
                            PS[pg_][:, 0:nq], win[:, kc, 2048 + ch * 128:2048 + (ch + 1) * 128], h_t[:, kc, 0:nq],
                            start=(kc == 0), stop=(kc == 7)),
                            reads=[win_ag, h_b], writes=[PSB[pg_]], mark=(kc == 7))
                    for kc in range(8):
                        kb.op(PE, lambda: nc.tensor.matmul(
                            PS[pa_][:, 0:nq], win[:, kc, 1536 + ch * 128:1536 + (ch + 1) * 128], h_t[:, kc, 0:nq],
                            start=(kc == 0), stop=(kc == 7)),
                            reads=[win_ag, h_b], writes=[PSB[pa_]], mark=(kc == 7))
                    s_t, s_b = sg[ch % 2]
                    kb.op(ACT, lambda: nc.scalar.activation(s_t[:, 0:nq], PS[pg_][:, 0:nq], AF.Sigmoid),
                          reads=[PSB[pg_]], writes=[s_b])
                    kb.op(DVE, lambda: nc.vector.tensor_tensor(
                        uT[:, ch, 15 + c0:15 + c0 + nq], PS[pa_][:, 0:nq], s_t[:, 0:nq], ALU.mult),
                        reads=[PSB[pa_], s_b], writes=uT_t[1 + t0:1 + t0 + nq // 128])
                for j in range(ntl):
                    pb = nb()
                    for kc in range(8):
                        kb.op(PE, lambda: nc.tensor.matmul(
                            PS[pb][:, :], h_t[:, kc, j * 128:(j + 1) * 128], win[:, kc, 1024:1536],
                            start=(kc == 0), stop=(kc == 7)),
                            reads=[win_vb, h_b], writes=[PSB[pb]], mark=(kc == 7))
                    kb.op(ACT, lambda: nc.scalar.copy(
                        vaug[:, t0 + j, :].rearrange("p (h e) -> p h e", e=65)[:, :, 0:64],
                        PS[pb][:, :].rearrange("p (h e) -> p h e", e=64)),
                        reads=[PSB[pb]], writes=[v_t[t0 + j]])
                    tick()
                while defer:
                    tick()

            for ch in range(4):
                pb = 1 + ch
                for kc in range(8):
                    kb.op(PE, lambda: nc.tensor.matmul(
                        PS[pb][:, 0:256], win[:, kc, 512 + ch * 128:512 + (ch + 1) * 128], hc_t[:, kc, 0:256],
                        start=(kc == 0), stop=(kc == 7)),
                        reads=[win_k, hc_b], writes=[PSB[pb]], mark=(kc == 7))
                kb.op(DVE, lambda: nc.vector.tensor_copy(kcT[:, ch, :], PS[pb][:, 0:256]),
                      reads=[PSB[pb]], writes=[kcTB])
            for t in range(2):
                pb = 1 + t
                for kc in range(8):
                    kb.op(PE, lambda: nc.tensor.matmul(
                        PS[pb][:, :], hc_t[:, kc, t * 128:(t + 1) * 128], win[:, kc, 1024:1536],
                        start=(kc == 0), stop=(kc == 7)),
                        reads=[win_vb, hc_b], writes=[PSB[pb]], mark=(kc == 7))
                kb.op(DVE, lambda: nc.vector.tensor_copy(
                    vcaug[:, t, :].rearrange("p (h e) -> p h e", e=65)[:, :, 0:64],
                    PS[pb][:, :].rearrange("p (h e) -> p h e", e=64)),
                    reads=[PSB[pb]], writes=[vcaugB])

            if dbg:
                dump(kcT[:, 0, 0:64], 128, 64, [kcTB])
                dump(vcaug[:, 0, 0:130], 128, 130, [vcaugB])

            while mod_next[0] < 24:
                mod_block(mod_next[0])
                mod_next[0] += 1
            while bcast_q:
                bcast_q.pop(0)()
            mod_finish()
            if dbg and stop_after >= 1:
                dump(kT[:, 1, 0:256], 128, 256, kT_t[0:2])
                dump(qT[:, 2, 128:384], 128, 256, qT_t[1:3])
                dump(uT[:, 3, 0:256], 128, 256, uT_t[0:3])
                dump(vaug[:, 5, 0:260], 128, 260, [v_t[5]])
                dump(kT[:, 3, TOK - 128:TOK], 128, 128, [kT_t[NT - 1]])

            kb.barrier(ENGS, DSEMS)

        if stop_after >= 2:
            B_ar.reset()
            Y_ar.reset()
            yT, _ = sb(Y_ar, "yT", [128, 8, NQT], BF16)
            yT_t = [Buf() for _ in range(NQ)]
            while etab_items:
                etab_items.pop(0)()
            pT = [sb(B_ar, f"pT{i}", [128, 7, 128], BF16) for i in range(4)]
            on = [sb(B_ar, f"on{i}", [128, 512], BF16) for i in range(2)]
            rec, recB = sb(B_ar, "rec", [128, 8], F32)
            CN = 256
            cvS, cvbS, sqbS = [], [], []
            for i_ in range(2):
                cvS.append((sb(B_ar, f"cv{i_}", [128, 4, CN], F32)[0], [Buf() for _ in range(4)]))
                cvbS.append((sb(B_ar, f"cvb{i_}", [128, 4, CN], BF16)[0], [Buf() for _ in range(4)]))
                sqbS.append((sb(B_ar, f"sqb{i_}", [128, 4, CN], BF16)[0], [Buf() for _ in range(4)]))
            rsd, rsdB = sb(B_ar, "rsd", [128, CN], F32)
            T_ar = Arena(BST_OFF, BST_OFF + 3584)
            mean, meanB = sb(T_ar, "mean", [128, CN], F32)
            tm = [sb(T_ar, f"tm{i}", [128, CN], F32) for i in range(2)]
            for bb_ in (meanB, tm[0][1], tm[1][1]):
                bb_.r = {id(e_.sem): (e_.sem, e_.n) for e_ in ENGS}
                for d_ in DSEMS:
                    bb_.r[id(d_.sem)] = (d_.sem, d_.n)
            DG_OFF = (B_ar.cur + 31) // 32 * 32
            assert DG_OFF >= H2_END, (DG_OFF, H2_END)
            dg, _ = sb(B_ar, "dg", [128, 4, 31, 128], BF16)
            dgB = [Buf() for _ in range(4)]
            for c in range(4):
                kb.op(DVE, lambda: nc.vector.tensor_tensor(
                    dg[:, c, :, :], ident[:].unsqueeze(1).broadcast_to([128, 31, 128]),
                    cwT[:, c, :].unsqueeze(2).broadcast_to([128, 31, 128]), ALU.mult),
                    reads=[identB, cwTB], writes=[dgB[c]])

            ST_PAIRS = [(0, 1), (2, 3), (6, 7)]

            def tile_info(qt):
                kts = list(range(max(0, qt - 2), max(qt + 2, 3) + 1))
                ti0 = 0 if qt == 0 else (4 if qt == 1 else 8)
                return kts, len(kts), ti0

            def qk(qt, h, n):
                kts, nl, ti0 = tile_info(qt)
                hc, hp = h // 2, (h % 2) * 64
                bA, bB = ST_PAIRS[n % 3]
                p_t, p_b = pT[n % 4]
                q_ap = (qT if h % 2 == 0 else qB)[:, hc, qt * 128:(qt + 1) * 128]
                for j in range(nl + 2):
                    bank = bA if j < 4 else bB
                    col = (j % 4) * 128
                    if j < nl:
                        l_ap = kT[:, hc, kts[j] * 128:(kts[j] + 1) * 128]
                        lb = kT_t[kts[j]]
                    else:
                        l_ap = kcT[:, hc, (j - nl) * 128:(j - nl + 1) * 128]
                        lb = kcTB
                    kb.op(PE, lambda: nc.tensor.matmul(PS[bank][:, col:col + 128], l_ap, q_ap,
                                                       start=True, stop=True),
                          reads=[lb, qT_t[qt], qB_t[qt]], writes=[PSB[bank]], mark=(j == 3 or j == nl + 1))
                nt_ = nl + 2
                assert bB == bA + 1
                kb.op(ACT, lambda: nc.scalar.activation(
                    p_t[:, 0:nt_, :].rearrange("p a b -> p (a b)"),
                    PSBIG[:, bA * 512:bA * 512 + nt_ * 128], AF.Exp),
                    reads=[PSB[bA], PSB[bB]], writes=[p_b])
                kb.op(DVE, lambda: nc.vector.tensor_tensor(
                    p_t[:, 0:nl, :], p_t[:, 0:nl, :], Etab[:, h, ti0:ti0 + nl, :], ALU.mult),
                    reads=[p_b, EtabB], writes=[p_b])

            def pv(qt, h, n):
                kts, nl, ti0 = tile_info(qt)
                p_t, p_b = pT[n % 4]
                ob, oc = 4 + h // 4, (h % 4) * 65
                for j in range(nl + 2):
                    if j < nl:
                        r_ap, rb = vaug[:, kts[j], h * 65:(h + 1) * 65], v_t[kts[j]]
                    else:
                        r_ap, rb = vcaug[:, j - nl, h * 65:(h + 1) * 65], vcaugB
                    kb.op(PE, lambda: nc.tensor.matmul(PS[ob][:, oc:oc + 65], p_t[:, j, :], r_ap,
                                                       start=(j == 0), stop=(j == nl + 1)),
                          reads=[p_b, rb], writes=[PSB[ob]], mark=(j == nl + 1))

            def normalize(qt, halves=(0, 1)):
                o_t, o_b = on[qt % 2]
                for half in halves:
                    ob = 4 + half
                    ov = PS[ob][:, 0:260].rearrange("p (h e) -> p h e", e=65)
                    kb.op(DVE, lambda: nc.vector.reciprocal(
                        rec[:, half * 4:(half + 1) * 4].unsqueeze(2), ov[:, :, 64:65]),
                        reads=[PSB[ob]], writes=[recB])
                    kb.op(DVE, lambda: nc.vector.tensor_tensor(
                        o_t[:, half * 256:(half + 1) * 256].rearrange("p (h e) -> p h e", e=64), ov[:, :, 0:64],
                        rec[:, half * 4:(half + 1) * 4].unsqueeze(2).broadcast_to([128, 4, 64]), ALU.mult),
                        reads=[PSB[ob], recB], writes=[o_b])

            def finish(qt):
                o_t, o_b = on[qt % 2]
                tps = PS[5].bitcast(BF16)
                for c in range(4):
                    kb.op(PE, lambda: nc.tensor.transpose(tps[:, c * 128:(c + 1) * 128],
                                                          o_t[:, c * 128:(c + 1) * 128], ident[:]),
                          reads=[o_b, identB], writes=[PSB[5]], mark=(c == 3))
                kb.op(DVE, lambda: nc.vector.tensor_copy(
                    yT[:, 0:4, qt * 128:(qt + 1) * 128], tps[:, 0:512].rearrange("p (c t) -> p c t", t=128)),
                    reads=[PSB[5]], writes=[yT_t[qt]])

            def attention_pass(mid_hook):
                units = [(qt, h) for qt in range(NQ) for h in range(8)]
                qprep(0)
                qprep(1)
                qk(units[0][0], units[0][1], 0)
                qk(units[1][0], units[1][1], 1)
                def pv_unit(m):
                    qt_, h_ = units[m]
                    pv(qt_, h_, m)
                    if h_ == 3:
                        normalize(qt_, (0,))
                    if h_ == 7:
                        normalize(qt_, (1,))
                    if h_ == 2 and qt_ > 0:
                        finish(qt_ - 1)

                for n, (qt, h) in enumerate(units):
                    if n + 2 < len(units):
                        if h == 0 and qt + 2 < NQ:
                            qprep(qt + 2)
                        qk(units[n + 2][0], units[n + 2][1], n + 2)
                    if n >= 1:
                        pv_unit(n - 1)
                    if 40 <= n < 72:
                        mid_hook(n - 40)
                pv_unit(len(units) - 1)
                finish(NQ - 1)

            CB = [0, 1, 4, 5]

            def conv_block(tok0, n, si):
                tl0 = max(tok0 - 15, 0) // 128
                tl1 = min((tok0 + n + 14) // 128, NQ - 1)
                ub = [uT_t[0]] + uT_t[1 + tl0:2 + tl1]
                cv, cvB = cvS[si]
                cvb, cvbB = cvbS[si]
                sqb, sqbB = sqbS[si]

                def part_a(c):
                    for k in range(31):
                        kb.op(PE, lambda: nc.tensor.matmul(PS[CB[c]][:, 0:n], dg[:, c, k, :],
                                                           uT[:, c, tok0 + k:tok0 + k + n],
                                                           start=(k == 0), stop=(k == 30)),
                              reads=[dgB[c]] + ub, writes=[PSB[CB[c]]], mark=(k == 30))
                    kb.op(ACT, lambda: nc.scalar.activation(cv[:, c, 0:n], PS[CB[c]][:, 0:n], AF.Identity,
                                                            bias=cbT[:, c:c + 1]),
                          reads=[PSB[CB[c]], cbTB], writes=[cvB[c]])
                    kb.op(ACT, lambda: nc.scalar.activation(sqb[:, c, 0:n], PS[CB[c]][:, 0:n], AF.Square,
                                                            bias=cbT[:, c:c + 1]),
                          reads=[PSB[CB[c]], cbTB], writes=[sqbB[c]])
                    kb.op(DVE, lambda: nc.vector.tensor_copy(cvb[:, c, 0:n], cv[:, c, 0:n]),
                          reads=[cvB[c]], writes=[cvbB[c]])

                def stats():
                    for c in range(4):
                        kb.op(PE, lambda: nc.tensor.matmul(PS[2][:, 0:n], ones[:], cvb[:, c, 0:n],
                                                           start=(c == 0), stop=(c == 3)),
                              reads=[onesB, cvbB[c]], writes=[PSB[2]], mark=(c == 3))
                    for c in range(4):
                        kb.op(PE, lambda: nc.tensor.matmul(PS[3][:, 0:n], ones[:], sqb[:, c, 0:n],
                                                           start=(c == 0), stop=(c == 3)),
                              reads=[onesB, sqbB[c]], writes=[PSB[3]], mark=(c == 3))

                def part_b():
                    kb.op(DVE, lambda: nc.vector.tensor_scalar(mean[:, 0:n], PS[2][:, 0:n], 1.0 / 512, None, ALU.mult),
                          reads=[PSB[2]], writes=[meanB])
                    kb.op(DVE, lambda: nc.vector.tensor_tensor(rsd[:, 0:n], mean[:, 0:n], mean[:, 0:n], ALU.mult),
                          reads=[meanB], writes=[rsdB])
                    kb.op(DVE, lambda: nc.vector.scalar_tensor_tensor(
                        rsd[:, 0:n], PS[3][:, 0:n], 1.0 / 512, rsd[:, 0:n], ALU.mult, ALU.subtract),
                        reads=[PSB[3], rsdB], writes=[rsdB])
                    kb.op(ACT, lambda: nc.scalar.activation(rsd[:, 0:n], rsd[:, 0:n], AF.Sqrt, bias=epsT[:, 0:1]),
                          reads=[rsdB, epsTB], writes=[rsdB])
                    kb.op(DVE, lambda: nc.vector.reciprocal(rsd[:, 0:n], rsd[:, 0:n]), reads=[rsdB], writes=[rsdB])
                    t0_ = tok0 // 128
                    yb = yT_t[t0_:t0_ + max(1, n // 128)]
                    for c in range(4):
                        t_t, t_b = tm[c % 2]
                        kb.op(DVE, lambda: nc.vector.tensor_tensor(t_t[:, 0:n], cv[:, c, 0:n], mean[:, 0:n], ALU.subtract),
                              reads=[cvB[c], meanB], writes=[t_b])
                        kb.op(DVE, lambda: nc.vector.tensor_tensor(t_t[:, 0:n], t_t[:, 0:n], rsd[:, 0:n], ALU.mult),
                              reads=[t_b, rsdB], writes=[t_b])
                        kb.op(ACT, lambda: nc.scalar.activation(
                            yT[:, 4 + c, tok0:tok0 + n], t_t[:, 0:n], AF.Silu,
                            bias=lnbT[:, c:c + 1], scale=lngT[:, c:c + 1]),
                            reads=[t_b, lngTB, lnbTB], writes=yb)
                return part_a, stats, part_b

            cblocks = [(2048, 16)] + [(blk * 256, 256) for blk in range(8)]
            prev_cb = None
            for bi_, (tk0, nn) in enumerate(cblocks):
                pa_f, st_f, pb_f = conv_block(tk0, nn, bi_ % 2)
                pa_f(0)
                if prev_cb is not None:
                    prev_cb[0]()
                    prev_cb[1]()
                pa_f(1)
                pa_f(2)
                pa_f(3)
                prev_cb = (st_f, pb_f)
            prev_cb[0]()
            prev_cb[1]()

            W_ar = Arena(DG_OFF, DG_OFF + 16384 + 32)
            wout, woutB = sb(W_ar, "wout", [128, 8, D], BF16)
            wout_v = wout_d.rearrange("(kc p) n -> p kc n", p=128)
            for i in range(2):
                kb.dma(POOL, d_wout, wout[:, i * 4:(i + 1) * 4, :], wout_v[:, i * 4:(i + 1) * 4, :],
                       reads=[], writes=dgB)
            woutB.w = (d_wout.sem, d_wout.n)

            def scale_wout(j):
                kc, q4 = j // 4, j % 4
                cs = slice(q4 * 256, (q4 + 1) * 256)
                kb.op(DVE, lambda: nc.vector.tensor_tensor(wout[:, kc, cs], wout[:, kc, cs], gt1b[:, cs], ALU.mult),
                      reads=[woutB, gt1bB], writes=[woutB])

            UT_OFF = None
            qB_ar = Arena(UT_ADDR, UT_ADDR + 4 * UW * 2)
            qB, qBB = sb(qB_ar, "qB", [128, 4, NQT], BF16)
            qB_t = [Buf() for _ in range(NQ)]
            for bb_ in qB_t:
                bb_.r = {id(e_.sem): (e_.sem, e_.n) for e_ in ENGS}

            def qprep(qt):
                cs = slice(qt * 128, (qt + 1) * 128)
                kb.op(POOL, lambda: nc.gpsimd.memset(qB[0:64, :, cs], 0.0), writes=[qB_t[qt]])
                kb.op(DVE, lambda: nc.vector.tensor_copy(qB[64:128, :, cs], qT[64:128, :, cs]),
                      reads=[qT_t[qt]], writes=[qB_t[qt]])
                kb.op(DVE, lambda: nc.vector.memset(qT[64:128, :, cs], 0.0), writes=[qT_t[qt]])
            attention_pass(scale_wout)

            if dbg:
                dump(yT[:, 1, 0:256], 128, 256, yT_t[0:2])
                dump(yT[:, 2, 2048 - 128:2048 + 128], 128, 256, yT_t[15:17])
                dump(yT[:, 5, 0:256], 128, 256, yT_t[0:2])
                dump(yT[:, 7, 2048 - 240:2048 + 16], 128, 256, yT_t[14:17])
            kb.barrier(ENGS, DSEMS)

        if stop_after >= 3:
            GMAX = max(FF_GROUPS)
            W0 = 3 * (GMAX * 128 * 8 * 2) + 96
            B2hi_ar = Arena(SB_HI - W0, SB_HI)
            B2_ar.hi = SB_HI - W0
            wupg, wupv, wdn = [None, None], [None, None], [None, None]
            wupg[0] = sb(B2hi_ar, "wupg0", [128, 8, GMAX * 128], BF16)
            wupv[0] = sb(B2hi_ar, "wupv0", [128, 8, GMAX * 128], BF16)
            wdn[0] = sb(B2hi_ar, "wdn0", [128, GMAX, D], BF16)
            wup_v = wup_d.rearrange("(kc p) n -> p kc n", p=128)
            gstart = [sum(FF_GROUPS[:i]) for i in range(len(FF_GROUPS))]
            wB = [[Buf(), Buf(), Buf()] for _ in range(2)]

            def load_group(gi):
                j0, n = gstart[gi], FF_GROUPS[gi]
                s = gi % 2
                kb.dma(POOL, d_wg[s], wupg[s][0][:, :, 0:n * 128], wup_v[:, :, j0 * 128:(j0 + n) * 128],
                       writes=[wB[s][0]])
                kb.dma(POOL, d_wv[s], wupv[s][0][:, :, 0:n * 128], wup_v[:, :, DFF + j0 * 128:DFF + (j0 + n) * 128],
                       writes=[wB[s][1]])
                kb.dma(POOL, d_wd[s], wdn[s][0][:, 0:n, :],
                       wdn_d[j0 * 128:(j0 + n) * 128, :].rearrange("(c p) n -> p c n", p=128),
                       writes=[wB[s][2]])

            def scale_group(gi):
                n = FF_GROUPS[gi]
                s = gi % 2
                kb.op(DVE, lambda: nc.vector.tensor_tensor(
                    wdn[s][0][:, 0:n, :], wdn[s][0][:, 0:n, :],
                    gt2b[:].unsqueeze(1).broadcast_to([128, n, D]), ALU.mult),
                    reads=[wB[s][2], gt2bB], writes=[wB[s][2]])

            B2_ar.lo = DG_OFF + 16384 + 32
            B2_ar.reset()
            xm, _ = sb(XM_ar, "xm", [128, NOWN, D], F32)
            xm_t = [Buf() for _ in range(NOWN)]
            H2W = 2 + NOWN * 128
            h2T, _ = sb(H2_ar, "h2T", [128, 8, H2W], BF16)
            h2_t = [Buf() for _ in range(NOWN + 2)]
            xr = [sb(B2_ar, f"xr{i}", [128, D], F32) for i in range(3)]
            xn2 = [sb(B2_ar, f"xn2{i}", [128, D], BF16) for i in range(4)]
            ssq2 = [sb(B2_ar, f"ssq2{i}", [128, 1], F32) for i in range(4)]
            rst2 = [sb(B2_ar, f"rst2{i}", [128, 1], F32) for i in range(4)]
            kb.op(DVE, lambda: nc.vector.memset(h2T[:, :, 0:1], 0.0), writes=[h2_t[0]])
            if stop_after >= 4:
                load_group(0)

            def norm2_tile(src_ap, srcB, np_, i, dst_cols, dstB):
                xn_t, xn_b = xn2[i % 4]
                ss_t, ss_b = ssq2[i % 4]
                rs_t, rs_b = rst2[i % 4]

                def stage_a():
                    kb.op(ACT, lambda: nc.scalar.activation(xn_t[0:np_, :], src_ap, AF.Square, accum_out=ss_t[0:np_, :]),
                          reads=[srcB], writes=[xn_b, ss_b])
                    kb.op(ACT, lambda: nc.scalar.activation(rs_t[0:np_, :], ss_t[0:np_, :], AF.Sqrt,
                                                            bias=epsT[0:np_, 0:1], scale=1.0 / D),
                          reads=[ss_b, epsTB], writes=[rs_b])

                def stage_b():
                    kb.op(DVE, lambda: nc.vector.reciprocal(rs_t[0:np_, :], rs_t[0:np_, :]), reads=[rs_b], writes=[rs_b])
                    kb.op(ACT, lambda: nc.scalar.activation(xn_t[0:np_, :], src_ap, AF.Copy, scale=rs_t[0:np_, 0:1]),
                          reads=[srcB, rs_b], writes=[xn_b])
                pb = 6 + (i % 2)

                def part2():
                    if np_ == 128:
                        tps = PS[pb].bitcast(BF16)
                        for kc in range(8):
                            kb.op(PE, lambda: nc.tensor.transpose(tps[:, kc * 128:(kc + 1) * 128],
                                                                  xn_t[:, kc * 128:(kc + 1) * 128], ident[:]),
                                  reads=[xn_b, identB], writes=[PSB[pb]], mark=(kc == 7))
                        hv = h2T[:, :, dst_cols[0]:dst_cols[1]]
                        kb.op(DVE, lambda: nc.vector.tensor_tensor(
                            hv, tps[:, 0:1024].rearrange("p (c t) -> p c t", t=128),
                            A2[:, :].unsqueeze(2).broadcast_to([128, 8, 128]), ALU.mult),
                            reads=[PSB[pb], A2B], writes=[dstB])
                        kb.op(DVE, lambda: nc.vector.tensor_tensor(
                            hv, hv, SH2[:, :].unsqueeze(2).broadcast_to([128, 8, 128]), ALU.add),
                            reads=[SH2B], writes=[dstB])
                    else:
                        for kc in range(8):
                            kb.op(PE, lambda: nc.tensor.matmul(PS[pb][:, kc:kc + 1], xn_t[0:1, kc * 128:(kc + 1) * 128],
                                                               ones[0:1, 0:1], start=True, stop=True),
                                  reads=[xn_b, onesB], writes=[PSB[pb]], mark=(kc == 7))
                        for kc in range(8):
                            kb.op(ACT, lambda: nc.scalar.activation(
                                h2T[:, kc, dst_cols[0]:dst_cols[1]], PS[pb][:, kc:kc + 1], AF.Identity,
                                bias=SH2[:, kc:kc + 1], scale=A2[:, kc:kc + 1]),
                                reads=[PSB[pb], A2B, SH2B], writes=[dstB])
                return stage_a, stage_b, part2

            B2a_ar = Arena(H2_END, DG_OFF)
            xh, xhB = sb(B2a_ar, "xh", [1, D], F32)
            kb.dma(SP, d_xh, xh[:], x_d[2048:2049, :], writes=[xhB])
            for half in range(2):
                pb = half
                for kc in range(8):
                    kb.op(PE, lambda: nc.tensor.matmul(PS[pb][0:1, :], yT[:, kc, 2048:2049],
                                                       wout[:, kc, half * 512:(half + 1) * 512],
                                                       start=(kc == 0), stop=(kc == 7)),
                          reads=[yT_t[16], woutB], writes=[PSB[pb]], mark=(kc == 7))
                kb.op(DVE, lambda: nc.vector.tensor_tensor(
                    xh[0:1, half * 512:(half + 1) * 512], PS[pb][0:1, :], xh[0:1, half * 512:(half + 1) * 512],
                    ALU.add), reads=[PSB[pb], xhB], writes=[xhB])
            halo_fns = list(norm2_tile(xh[0:1, :], xhB, 1, 3, (H2W - 1, H2W), h2_t[NOWN + 1]))
            halo_fns[0]()
            qB2, qC2 = [], []
            for t in range(NOWN):
                x_t, x_b = xr[t % 3]
                kb.dma(SP, d_xr[t % 3], x_t[:], x_d[t * 128:(t + 1) * 128, :], writes=[x_b])
                pb0 = (2 * t) % 6
                for half in range(2):
                    pb = pb0 + half
                    for kc in range(8):
                        kb.op(PE, lambda: nc.tensor.matmul(PS[pb][:, :], yT[:, kc, t * 128:(t + 1) * 128],
                                                           wout[:, kc, half * 512:(half + 1) * 512],
                                                           start=(kc == 0), stop=(kc == 7)),
                              reads=[yT_t[t], woutB], writes=[PSB[pb]], mark=(kc == 7))
                kb.op(DVE, lambda: nc.vector.tensor_tensor(
                    xm[:, t, :], PSBIG[:, pb0 * 512:pb0 * 512 + 1024], x_t[:, :], ALU.add),
                    reads=[PSB[pb0], PSB[pb0 + 1], x_b], writes=[xm_t[t]])
                p2 = norm2_tile(xm[:, t, :], xm_t[t], 128, t, (1 + t * 128, 1 + (t + 1) * 128), h2_t[t + 1])
                sa_, sb_, sc_ = p2
                sa_()
                if t == 0:
                    halo_fns[1]()
                if t == 1:
                    halo_fns[2]()
                if t == 11 and stop_after >= 4:
                    scale_group(0)
                if qB2:
                    qB2.pop(0)()
                qB2.append(sb_)
                qC2.append(sc_)
                if len(qC2) > 3:
                    qC2.pop(0)()
            while qB2:
                qB2.pop(0)()
            while qC2:
                qC2.pop(0)()

            if dbg:
                dump(xm[:, 3, 0:256], 128, 256, [xm_t[3]])
                dump(h2T[:, 2, 0:256], 128, 256, h2_t[0:3])
                dump(h2T[:, 6, H2W - 128:H2W], 128, 128, h2_t[16:18])

        if stop_after >= 4:
            B2_ar.lo = H2_END
            B2_ar.reset()
            Y_ar.reset()
            for i_ in (1,):
                wupg[i_] = sb(B2_ar, f"wupg{i_}", [128, 8, GMAX * 128], BF16)
                wupv[i_] = sb(B2_ar, f"wupv{i_}", [128, 8, GMAX * 128], BF16)
                wdn[i_] = sb(B2_ar, f"wdn{i_}", [128, GMAX, D], BF16)
            gfb, gfbB = sb(B2_ar, "gfb", [128, D], F32)
            ot = [sb(B2_ar, f"ot{i}", [128, D], F32) for i in range(3)]
            sq3, sq3B = sb(B2_ar, "sq3", [128, D], BF16)
            ssq3 = [sb(B2_ar, f"ssq3{i}", [128, 1], F32) for i in range(3)]
            rst3 = [sb(B2_ar, f"rst3{i}", [128, 1], F32) for i in range(3)]
            hid, _ = sb(Y_ar, "hid", [128, GMAX, NOWN * 128], BF16)
            hidB = [Buf() for _ in FF_BLOCKS]
            NB = 412
            gl = [sb(Y_ar, f"gl{i}", [128, NB], F32) for i in range(2)]
            vl = [sb(Y_ar, f"vl{i}", [128, NB], F32) for i in range(2)]
            sl = [sb(Y_ar, f"sl{i}", [128, NB], F32) for i in range(2)]
            gfbB.r = {id(e_.sem): (e_.sem, e_.n) for e_ in ENGS}
            for d_ in DSEMS:
                gfbB.r[id(d_.sem)] = (d_.sem, d_.n)
            kb.dma(SP, d_xh, gfb[:], gfin_d.partition_broadcast(128), writes=[gfbB])
            pend = []
            fin_q = []
            acc_i = [0]
            done_tiles = {}
            cnt = [0]

            def down(gi, t):
                n = FF_GROUPS[gi]
                s = gi % 2
                last = gi == len(FF_GROUPS) - 1
                hb = [hidB[bi] for bi, (o0, no) in enumerate(FF_BLOCKS)
                      if o0 < (t + 1) * 128 and o0 + no > t * 128]
                o_t, o_b = ot[t % 3]
                pb0 = 4 + (2 * t) % 4
                for half in range(2):
                    pb = pb0 + half
                    for i in range(n):
                        kb.op(PE, lambda: nc.tensor.matmul(PS[pb][:, :], hid[:, i, t * 128:(t + 1) * 128],
                                                           wdn[s][0][:, i, half * 512:(half + 1) * 512],
                                                           start=(i == 0), stop=(i == n - 1)),
                              reads=hb + [wB[s][2]], writes=[PSB[pb]], mark=(i == n - 1))
                psum2 = PSBIG[:, pb0 * 512:pb0 * 512 + 1024]
                if not last:
                    k_ = acc_i[0]
                    acc_i[0] += 1
                    tmp_t, tmp_b = ot[k_ % 3]
                    kb.op(ACT, lambda: nc.scalar.copy(tmp_t[:], psum2),
                          reads=[PSB[pb0], PSB[pb0 + 1]], writes=[tmp_b])
                    kb.dma(POOL, d_acc[k_ % 8], xm[:, t, :], tmp_t[:], reads=[tmp_b, xm_t[t]], writes=[xm_t[t]],
                           accum_op=ALU.add)
                else:
                    kb.op(DVE, lambda: nc.vector.tensor_tensor(o_t[:], psum2, xm[:, t, :], ALU.add),
                          reads=[PSB[pb0], PSB[pb0 + 1], xm_t[t]], writes=[o_b])
                if last:
                    ss_t, ss_b = ssq3[t % 3]
                    rs_t, rs_b = rst3[t % 3]
                    while fin_q:
                        fin_q.pop(0)()
                    kb.op(ACT, lambda: nc.scalar.activation(sq3[:], o_t[:], AF.Square, accum_out=ss_t[:]),
                          reads=[o_b], writes=[sq3B, ss_b])
                    kb.op(ACT, lambda: nc.scalar.activation(rs_t[:], ss_t[:], AF.Sqrt, bias=epsT[:, 0:1], scale=1.0 / D),
                          reads=[ss_b, epsTB], writes=[rs_b])

                    def fin():
                        kb.op(DVE, lambda: nc.vector.reciprocal(rs_t[:], rs_t[:]), reads=[rs_b], writes=[rs_b])
                        kb.op(DVE, lambda: nc.vector.scalar_tensor_tensor(
                            o_t[:], o_t[:], rs_t[:, 0:1], gfb[:], ALU.mult, ALU.mult),
                            reads=[o_b, rs_b, gfbB], writes=[o_b])
                        kb.dma(SP, d_ot[t % 3], out_d[t * 128:(t + 1) * 128, :], o_t[:], reads=[o_b])
                    fin_q.append(fin)

            for gi in range(len(FF_GROUPS)):
                j0, n = gstart[gi], FF_GROUPS[gi]
                s = gi % 2
                for bi, (o0, no) in enumerate(FF_BLOCKS):
                    ncol = no + 2
                    t_lo = max(o0 - 1, 0) // 128
                    t_hi = min((o0 + no) // 128, NOWN - 1)
                    hb = [h2_t[0]] + h2_t[1 + t_lo:2 + t_hi] + ([h2_t[NOWN + 1]] if o0 + no == NOWN * 128 else [])
                    for i in range(n):
                        j = j0 + i
                        k = cnt[0] % 2
                        cnt[0] += 1
                        bg, bv = 2 * k, 2 * k + 1
                        for kc in range(8):
                            kb.op(PE, lambda: nc.tensor.matmul(PS[bg][:, 0:ncol], wupg[s][0][:, kc, i * 128:(i + 1) * 128],
                                                               h2T[:, kc, o0:o0 + ncol], start=(kc == 0), stop=(kc == 7)),
                                  reads=hb + [wB[s][0]], writes=[PSB[bg]], mark=(kc == 7))
                        for kc in range(8):
                            kb.op(PE, lambda: nc.tensor.matmul(PS[bv][:, 0:ncol], wupv[s][0][:, kc, i * 128:(i + 1) * 128],
                                                               h2T[:, kc, o0:o0 + ncol], start=(kc == 0), stop=(kc == 7)),
                                  reads=hb + [wB[s][1]], writes=[PSB[bv]], mark=(kc == 7))
                        g_t, g_b = gl[k]
                        v_t_, v_b = vl[k]
                        s_t, s_b = sl[k]
                        for (acc, accB, pbk, ch) in ((g_t, g_b, bg, j), (v_t_, v_b, bv, NFC + j)):
                            kb.op(ACT, lambda: nc.scalar.activation(
                                acc[:, 0:no], PS[pbk][:, 1:1 + no], AF.Identity,
                                bias=fcb[:, ch:ch + 1], scale=fcw[:, ch, 1:2]),
                                reads=[PSB[pbk], fcwB, fcbB], writes=[accB])
                            kb.op(DVE, lambda: nc.vector.scalar_tensor_tensor(
                                acc[:, 0:no], PS[pbk][:, 0:no], fcw[:, ch, 0:1], acc[:, 0:no], ALU.mult, ALU.add),
                                reads=[PSB[pbk], accB, fcwB], writes=[accB])
                            kb.op(DVE, lambda: nc.vector.scalar_tensor_tensor(
                                acc[:, 0:no], PS[pbk][:, 2:2 + no], fcw[:, ch, 2:3], acc[:, 0:no], ALU.mult, ALU.add),
                                reads=[PSB[pbk], accB, fcwB], writes=[accB])
                        kb.op(ACT, lambda: nc.scalar.activation(s_t[:, 0:no], g_t[:, 0:no], AF.Silu),
                              reads=[g_b], writes=[s_b])
                        kb.op(POOL, lambda: nc.gpsimd.tensor_tensor(
                            hid[:, i, o0:o0 + no], s_t[:, 0:no], v_t_[:, 0:no], ALU.mult),
                            reads=[s_b, v_b], writes=[hidB[bi]])
                    for (pg, pt) in pend:
                        down(pg, pt)
                    pend = []
                    if bi == 0 and gi + 1 < len(FF_GROUPS):
                        load_group(gi + 1)
                    if bi == 3 and gi + 1 < len(FF_GROUPS):
                        scale_group(gi + 1)
                    t_done = (o0 + no) // 128
                    t_prev = done_tiles.get(gi, 0)
                    for t in range(t_prev, t_done):
                        pend.append((gi, t))
                    done_tiles[gi] = t_done
            for (pg, pt) in pend:
                down(pg, pt)
            pend = []
            while fin_q:
                fin_q.pop(0)()

        kb.barrier(ENGS, DSEMS)
    return nc


def _bias_tables(rpb, half):
    H = rpb.shape[0]
    ext = np.concatenate([rpb.reshape(H, -1), np.full((H, 1), -30000.0, np.float32)], axis=1)
    MASK = 15 * 31
    specs = [(0, k) for k in range(4)] + [(1, k) for k in range(4)] + [(8, 8 + d) for d in range(-2, 3)]
    idx = np.zeros((128, NTAB, 128), np.int64)
    kk = np.arange(128)
    for ti, (qt, kt) in enumerate(specs):
        rl = 2 * qt + (kk // 64)
        cl = kk % 64
        krl = 2 * kt + (kk // 64)
        kcl = kk % 64
        if half == 0:
            rg, cg, krg, kcg = rl, cl, krl, kcl
        else:
            rg, cg, krg, kcg = 63 - rl, 63 - cl, 63 - krl, 63 - kcl
        rs = np.clip(rg - 4, 0, 56)
        cs = np.clip(cg - 8, 0, 48)
        KR, R = np.meshgrid(krg, rg, indexing="ij")
        KC, C = np.meshgrid(kcg, cg, indexing="ij")
        RS = np.broadcast_to(rs[None, :], KR.shape)
        CS = np.broadcast_to(cs[None, :], KR.shape)
        valid = (KR >= RS) & (KR < RS + 8) & (KC >= CS) & (KC < CS + 16)
        ro = np.clip(KR - R + 7, 0, 14)
        co = np.clip(KC - C + 15, 0, 30)
        idx[:, ti, :] = np.where(valid, ro * 31 + co, MASK)
    tab = ext[:, idx]
    return np.ascontiguousarray(np.transpose(tab, (1, 0, 2, 3))).astype(np.float32)


def _colT(v, n):
    return np.ascontiguousarray(v.reshape(n, 128).T)


def make_in_maps(x, c, ctx, c_ctx, w_mod, b_mod, g_norm1, w_in, rpb, conv_w, conv_b, ln_g, ln_b,
                 w_out, g_norm2, w_up, ffn_conv_w, ffn_conv_b, w_down, g_final):
    f = np.float32
    maps = []
    ident = np.eye(128, dtype=f)
    shared = {
        "w_mod": np.ascontiguousarray(w_mod[0], f), "bmodT": _colT(b_mod[0], 48),
        "bmodR": np.ascontiguousarray(b_mod[0].reshape(1, -1), f),
        "g1T": _colT(g_norm1[0], 8), "g2T": _colT(g_norm2[0], 8),
        "w_in": np.ascontiguousarray(w_in[0], f), "cbT": _colT(conv_b[0], 4),
        "lngT": _colT(ln_g[0], 4), "lnbT": _colT(ln_b[0], 4),
        "w_out": np.ascontiguousarray(w_out[0], f), "w_up": np.ascontiguousarray(w_up[0], f),
        "fcbT": _colT(ffn_conv_b[0], 44), "w_down": np.ascontiguousarray(w_down[0], f),
        "g_final": np.ascontiguousarray(g_final, f), "ident": ident,
    }
    for core in range(8):
        b, half = core // 2, core % 2
        xs = x[b] if half == 0 else x[b, ::-1]
        cw = conv_w[0] if half == 0 else conv_w[0, ::-1]
        fw = ffn_conv_w[0] if half == 0 else ffn_conv_w[0, ::-1]
        m = dict(shared)
        m["x"] = np.ascontiguousarray(xs[:TOK], f)
        m["cT"] = np.ascontiguousarray(np.stack([_colT(c[b], 8), _colT(c_ctx, 8)], axis=2), f)
        m["ctx"] = np.ascontiguousarray(ctx[b], f)
        m["btab"] = _bias_tables(rpb[0], half)
        m["cwT"] = np.ascontiguousarray(np.transpose(cw.reshape(31, 4, 128), (2, 1, 0)), f)
        m["fcwT"] = np.ascontiguousarray(np.transpose(fw.reshape(3, 44, 128), (2, 1, 0)), f)
        maps.append(m)
    return maps


_NC_CACHE = {}


def kernel(**inputs):
    inputs = {k: np.asarray(v) for k, v in inputs.items()}
    maps = make_in_maps(**inputs)
    if "nc" not in _NC_CACHE:
        _NC_CACHE["nc"] = build()
    res = run_bass_kernel_spmd(_NC_CACHE["nc"], maps, core_ids=list(range(8)))
    out = np.zeros((4, 4096, D), np.float32)
    for core in range(8):
        b, half = core // 2, core % 2
        o = np.asarray(res.results[core]["out"], np.float32)
        if half == 0:
            out[b, :2048] = o
        else:
            out[b, 2048:] = o[::-1]
    return out
```
